# Optimizing a Trainium2 kernel written in Bass

```python
import jax, jax.numpy as jnp
from jax import lax
import numpy as np

D_MODEL = 1024
BATCH = 4
SEQ = 8192
DEPTH = 2

SC_WIDTH = D_MODEL // 2
SC_KERNEL = 3
MLA_HEADS = 8
MLA_NOPE = 64
MLA_ROPE = 32
MLA_V = 64
MLA_Q_RANK = 384
MLA_KV_RANK = 256
ROPE_BASE = 10000.0
Q_BLOCK = 128
IN_COLS = 3 * SC_WIDTH + MLA_Q_RANK + MLA_KV_RANK + MLA_ROPE
MIX_WIDTH = SC_WIDTH + MLA_HEADS * MLA_V
CONF_KERNEL = 31
CONF_WIDTH = D_MODEL
D_FF = 4 * D_MODEL
N_EVEN = (DEPTH + 1) // 2
N_ODD = DEPTH // 2
EPS = 1e-6

kernel_name = "hybrid_shortconv_mla_conformer_encoder"


def rms_norm(x, g):
    xf = x.astype(jnp.float32)
    y = xf * lax.rsqrt(jnp.mean(xf * xf, axis=-1, keepdims=True) + EPS)
    return (y * g.astype(jnp.float32)).astype(x.dtype)


def layer_norm(x, g, b):
    xf = x.astype(jnp.float32)
    mu = jnp.mean(xf, axis=-1, keepdims=True)
    var = jnp.mean(jnp.square(xf - mu), axis=-1, keepdims=True)
    y = (xf - mu) * lax.rsqrt(var + EPS)
    return (y * g.astype(jnp.float32) + b.astype(jnp.float32)).astype(x.dtype)


def depthwise_conv(x, w):
    k = w.shape[0]
    return lax.conv_general_dilated(
        x, w[:, None, :].astype(x.dtype), window_strides=(1,),
        padding=[(k // 2, k // 2)], dimension_numbers=('NWC', 'WIO', 'NWC'),
        feature_group_count=x.shape[-1])


def apply_rope(x, cos, sin):
    half = x.shape[-1] // 2
    xf = x.astype(jnp.float32)
    x1, x2 = xf[..., :half], xf[..., half:]
    out = jnp.concatenate([x1 * cos - x2 * sin, x2 * cos + x1 * sin], axis=-1)
    return out.astype(x.dtype)


def block_attention(q, k, v):
    b, s, h, dq = q.shape
    nblk = s // Q_BLOCK
    qb = q.reshape(b, nblk, Q_BLOCK, h, dq).transpose(1, 0, 2, 3, 4)
    kf = k.astype(jnp.float32)
    scale = dq ** -0.5

    def one_block(q_blk):
        sc = jnp.einsum('bqhd,bkhd->bhqk', q_blk.astype(jnp.float32), kf) * scale
        p = jax.nn.softmax(sc, axis=-1)
        return jnp.einsum('bhqk,bkhd->bqhd', p.astype(v.dtype), v)

    o = lax.map(one_block, qb)
    return o.transpose(1, 0, 2, 3, 4).reshape(b, s, h * v.shape[-1])


def parallel_conv_mla(h, positions, w_in, sc_kernel, q_norm, w_uq, kv_norm, w_ukv, w_out):
    b, s, _ = h.shape
    proj = h @ w_in
    cuts = [SC_WIDTH, 2 * SC_WIDTH, 3 * SC_WIDTH,
            3 * SC_WIDTH + MLA_Q_RANK, 3 * SC_WIDTH + MLA_Q_RANK + MLA_KV_RANK]
    gate_b, gate_c, xs, q_lat, kv_lat, k_rope = jnp.split(proj, cuts, axis=-1)

    y_a = gate_b * depthwise_conv(gate_c * xs, sc_kernel)

    q = (rms_norm(q_lat, q_norm) @ w_uq).reshape(b, s, MLA_HEADS, MLA_NOPE + MLA_ROPE)
    q_nope, q_pe = q[..., :MLA_NOPE], q[..., MLA_NOPE:]
    kv = (rms_norm(kv_lat, kv_norm) @ w_ukv).reshape(b, s, MLA_HEADS, MLA_NOPE + MLA_V)
    k_nope, v = kv[..., :MLA_NOPE], kv[..., MLA_NOPE:]
    half = MLA_ROPE // 2
    inv_freq = 1.0 / (ROPE_BASE ** (jnp.arange(half, dtype=jnp.float32) / half))
    ang = positions.astype(jnp.float32)[..., None] * inv_freq
    cos, sin = jnp.cos(ang), jnp.sin(ang)
    q_pe = apply_rope(q_pe, cos[:, :, None, :], sin[:, :, None, :])
    k_pe = apply_rope(k_rope, cos, sin)[:, :, None, :]
    q_full = jnp.concatenate([q_nope, q_pe], axis=-1)
    k_full = jnp.concatenate(
        [k_nope, jnp.broadcast_to(k_pe, (b, s, MLA_HEADS, MLA_ROPE))], axis=-1)
    y_b = block_attention(q_full, k_full, v)

    return jnp.concatenate([y_a, y_b], axis=-1) @ w_out


def conformer_conv(h, w_pw1, b_pw1, w_dw, b_dw, ln_g, ln_b, w_pw2, b_pw2):
    u = h @ w_pw1 + b_pw1
    a, g = jnp.split(u, 2, axis=-1)
    u = a * jax.nn.sigmoid(g)
    u = depthwise_conv(u, w_dw) + b_dw
    u = jax.nn.silu(layer_norm(u, ln_g, ln_b))
    return u @ w_pw2 + b_pw2


def squared_relu_mlp(h, w1, w2):
    return jnp.square(jax.nn.relu(h @ w1)) @ w2


def setup_inputs(seed: int = 0) -> dict:
    key = jax.random.key(seed)
    ks = jax.random.split(key, 24)

    def nrm(k, shape, fan_in):
        return jax.random.normal(k, shape, jnp.float32) * (fan_in ** -0.5)

    def gain(k, shape):
        return 1.0 + 0.05 * jax.random.normal(k, shape, jnp.float32)

    def bias(k, shape):
        return 0.02 * jax.random.normal(k, shape, jnp.float32)

    x = jax.random.normal(ks[0], (BATCH, SEQ, D_MODEL), jnp.float32)
    offsets = jax.random.randint(ks[1], (BATCH, 1), 0, SEQ, dtype=jnp.int32)
    positions = jnp.arange(SEQ, dtype=jnp.int32)[None, :] + offsets
    return {
        "x": x,
        "positions": positions,
        "sandwich_gains": gain(ks[2], (DEPTH, 4, D_MODEL)),
        "even_w_in": nrm(ks[3], (N_EVEN, D_MODEL, IN_COLS), D_MODEL),
        "even_sc_kernel": nrm(ks[4], (N_EVEN, SC_KERNEL, SC_WIDTH), SC_KERNEL),
        "even_q_norm": gain(ks[5], (N_EVEN, MLA_Q_RANK)),
        "even_w_uq": nrm(ks[6], (N_EVEN, MLA_Q_RANK, MLA_HEADS * (MLA_NOPE + MLA_ROPE)), MLA_Q_RANK),
        "even_kv_norm": gain(ks[7], (N_EVEN, MLA_KV_RANK)),
        "even_w_ukv": nrm(ks[8], (N_EVEN, MLA_KV_RANK, MLA_HEADS * (MLA_NOPE + MLA_V)), MLA_KV_RANK),
        "even_w_out": nrm(ks[9], (N_EVEN, MIX_WIDTH, D_MODEL), MIX_WIDTH),
        "odd_w_pw1": nrm(ks[10], (N_ODD, D_MODEL, 2 * CONF_WIDTH), D_MODEL),
        "odd_b_pw1": bias(ks[11], (N_ODD, 2 * CONF_WIDTH)),
        "odd_w_dw": nrm(ks[12], (N_ODD, CONF_KERNEL, CONF_WIDTH), CONF_KERNEL),
        "odd_b_dw": bias(ks[13], (N_ODD, CONF_WIDTH)),
        "odd_ln_g": gain(ks[14], (N_ODD, CONF_WIDTH)),
        "odd_ln_b": bias(ks[15], (N_ODD, CONF_WIDTH)),
        "odd_w_pw2": nrm(ks[16], (N_ODD, CONF_WIDTH, D_MODEL), CONF_WIDTH),
        "odd_b_pw2": bias(ks[17], (N_ODD, D_MODEL)),
        "mlp_w1": nrm(ks[18], (DEPTH, D_MODEL, D_FF), D_MODEL),
        "mlp_w2": nrm(ks[19], (DEPTH, D_FF, D_MODEL), D_FF),
    }


def reference(x, positions, sandwich_gains, even_w_in, even_sc_kernel, even_q_norm,
              even_w_uq, even_kv_norm, even_w_ukv, even_w_out, odd_w_pw1, odd_b_pw1,
              odd_w_dw, odd_b_dw, odd_ln_g, odd_ln_b, odd_w_pw2, odd_b_pw2,
              mlp_w1, mlp_w2):
    for i in range(DEPTH):
        g = sandwich_gains[i]
        j = i // 2
        h = rms_norm(x, g[0])
        if i % 2 == 0:
            m = parallel_conv_mla(h, positions, even_w_in[j], even_sc_kernel[j],
                                  even_q_norm[j], even_w_uq[j], even_kv_norm[j],
                                  even_w_ukv[j], even_w_out[j])
        else:
            m = conformer_conv(h, odd_w_pw1[j], odd_b_pw1[j], odd_w_dw[j], odd_b_dw[j],
                               odd_ln_g[j], odd_ln_b[j], odd_w_pw2[j], odd_b_pw2[j])
        x = x + rms_norm(m, g[1])
        h = rms_norm(x, g[2])
        x = x + rms_norm(squared_relu_mlp(h, mlp_w1[i], mlp_w2[i]), g[3])
    return x
```

```python
import numpy as np
from contextlib import ExitStack
import concourse.bass as bass
import concourse.mybir as mybir
from concourse.bass_utils import run_bass_kernel_spmd

F32 = mybir.dt.float32
BF16 = mybir.dt.bfloat16
I32 = mybir.dt.int32
AF = mybir.ActivationFunctionType
ALU = mybir.AluOpType
AX = mybir.AxisListType

D = 1024
NH = 8
HALO = 16
TW = 464
EPS = 1e-6
SEQ_FULL = 8192
SEM_LIM = 30000


class Buf:
    __slots__ = ("name", "last_w", "readers")

    def __init__(self, name):
        self.name = name
        self.last_w = None
        self.readers = []


class Op:
    __slots__ = ("eng", "fn", "deps", "dma", "sig", "idx", "dma_deps", "bar")

    def __init__(self, eng, fn, dma):
        self.eng = eng
        self.fn = fn
        self.dma = dma
        self.deps = set()
        self.dma_deps = {}
        self.sig = None
        self.bar = None


ENGS = ("pe", "act", "dve", "pool", "sp")


class Sched:
    def __init__(self, nc, stack):
        self.nc = nc
        self.stack = stack
        self.ops = []
        self.allbufs = []
        self.sig_count = {e: 0 for e in ENGS}
        self.esems = {e: [] for e in ENGS}
        self.dsems = {}
        self.known = {e: {} for e in ENGS}
        self.known_d = {e: {} for e in ENGS}
        self.pending_bar = None
        self.out_sems = []

    def buf(self, name):
        b = Buf(name)
        self.allbufs.append(b)
        return b

    def dsem(self, key):
        if key not in self.dsems:
            s = self.stack.enter_context(self.nc.semaphore("d_" + key))
            self.dsems[key] = [s, 0]
        return self.dsems[key]

    def esem(self, eng, epoch):
        lst = self.esems[eng]
        while len(lst) <= epoch:
            lst.append(self.stack.enter_context(self.nc.semaphore("e_%s_%d" % (eng, len(lst)))))
        return lst[epoch]

    def op(self, eng, fn, reads=(), writes=(), dma_key=None, extra_dma=None):
        o = Op(eng, fn, None)
        oid = len(self.ops)
        if dma_key is not None:
            d = self.dsem(dma_key)
            d[1] += 16
            o.dma = (dma_key, d[1])
        deps = []
        for b in reads:
            if b.last_w is not None:
                deps.append((b.last_w, "raw"))
        for b in writes:
            if b.last_w is not None:
                deps.append((b.last_w, "waw"))
            for r in b.readers:
                deps.append((r, "war"))
        for (pid, kind) in deps:
            if pid == oid:
                continue
            p = self.ops[pid]
            if p.dma is not None:
                k = p.dma[0]
                v = self.dsems[k][1] if o.dma is None or o.dma[0] != k else p.dma[1]
                o.dma_deps[k] = max(o.dma_deps.get(k, 0), v)
            else:
                if p.eng == eng and o.dma is None:
                    if eng == "pe":
                        continue
                o.deps.add(pid)
        if extra_dma:
            for k, v in extra_dma.items():
                o.dma_deps[k] = max(o.dma_deps.get(k, 0), v)
        for b in reads:
            b.readers.append(oid)
        for b in writes:
            b.last_w = oid
            b.readers = []
        self.ops.append(o)
        return oid

    def flush(self, final=False):
        ops = self.ops
        nc = self.nc
        needed = set()
        for o in ops:
            for pid in o.deps:
                needed.add(pid)
        last_of = {}
        for i, o in enumerate(ops):
            if o.dma is None:
                last_of[o.eng] = i
        for e, i in last_of.items():
            needed.add(i)
        for i, o in enumerate(ops):
            if i in needed and o.dma is None:
                self.sig_count[o.eng] += 1
                o.sig = self.sig_count[o.eng]
        bar = self.pending_bar
        per_eng = {e: [] for e in ENGS}
        for i, o in enumerate(ops):
            per_eng[o.eng].append(o)
        sched = self

        def emit(engname, engine):
            known = sched.known[engname]
            known_d = sched.known_d[engname]

            def wait_sig(e2, sig):
                if known.get(e2, 0) >= sig:
                    return
                known[e2] = sig
                ep, val = (sig - 1) // SEM_LIM, (sig - 1) % SEM_LIM + 1
                engine.wait_ge(sched.esem(e2, ep), val)

            def wait_dma(k, v):
                if known_d.get(k, 0) >= v:
                    return
                known_d[k] = v
                engine.wait_ge(sched.dsems[k][0], v)

            if bar is not None:
                for e2, sig in bar[0].items():
                    if e2 != engname and sig > 0:
                        wait_sig(e2, sig)
                for k, v in bar[1].items():
                    if not k.startswith("cv_"):
                        wait_dma(k, v)
            for o in per_eng[engname]:
                for pid in sorted(o.deps):
                    p = ops[pid]
                    wait_sig(p.eng, p.sig)
                for k, v in o.dma_deps.items():
                    wait_dma(k, v)
                ins = o.fn(engine)
                if o.dma is not None:
                    ins.then_inc(sched.dsems[o.dma[0]][0], 16)
                elif o.sig is not None:
                    ep = (o.sig - 1) // SEM_LIM
                    ins.then_inc(sched.esem(o.eng, ep), 1)
            if final and engname == "sp":
                for k, d in sched.dsems.items():
                    wait_dma(k, d[1])

        for e in ENGS:
            self.esem(e, max(0, (self.sig_count[e] - 1)) // SEM_LIM)
        with nc.Block() as block:
            @block.tensor
            def _(eng):
                emit("pe", eng)

            @block.scalar
            def _(eng):
                emit("act", eng)

            @block.vector
            def _(eng):
                emit("dve", eng)

            @block.gpsimd
            def _(eng):
                emit("pool", eng)

            @block.sync
            def _(eng):
                emit("sp", eng)
        self.pending_bar = ({e: self.sig_count[e] for e in ENGS}, {k: d[1] for k, d in self.dsems.items()})
        self.ops = []
        for b in self.allbufs:
            b.last_w = None
            b.readers = []


def weight_layout():
    off = {}
    cur = 0

    def add(name, n):
        nonlocal cur
        off[name] = (cur, n)
        cur += n
    add("w_inA", 7 * 8 * 128)
    add("w_uq", 8 * 2 * 3 * 96)
    add("w_ukv", 2 * 1024)
    add("w_inG", 12 * 8 * 128)
    add("w_out", 8 * 8 * 128)
    add("m0w1", 32 * 8 * 128)
    add("m0w2", 8 * 32 * 128)
    add("pw1", 16 * 8 * 128)
    add("pw2", 8 * 8 * 128)
    add("m1w1", 32 * 8 * 128)
    add("m1w2", 8 * 32 * 128)
    add("dwd", 8 * 31 * 128)
    return off, cur


def const_layout():
    off = {}
    cur = 0

    def add(name, n):
        nonlocal cur
        off[name] = (cur, n)
        cur += n
    add("gains", 64)
    add("qg", 3)
    add("kvg", 2)
    add("sck", 12)
    add("bpw1", 16)
    add("wdw", 8 * 31)
    add("bdw", 8)
    add("lng", 8)
    add("lnb", 8)
    add("bpw2", 8)
    add("invf", 1)
    add("sgn", 1)
    return off, cur


def pack_lhsT(w, nk, nm, mw=128):
    return np.ascontiguousarray(w.reshape(nk, 128, nm, mw).transpose(1, 2, 0, 3)).reshape(128, -1)


def pack_w2(w):
    return np.concatenate([pack_lhsT(w[h * 2048:(h + 1) * 2048], 16, 8) for h in range(2)], axis=1)


def colpack(v):
    return np.ascontiguousarray(v.reshape(-1, 128).T)


def split_tiles(n, w):
    out = []
    t0 = 0
    while t0 < n:
        ww = min(w, n - t0)
        out.append((t0, ww))
        t0 += ww
    return out


def build(S):
    OWN = S // 2
    OWNH = OWN + HALO
    own_tiles = split_tiles(OWNH, TW)
    oth_tiles = [(OWNH + a, w) for (a, w) in split_tiles(S - OWNH, 512)]
    NKT = S // 128
    woff, WTOT = weight_layout()
    coff, CTOT = const_layout()
    scale = float((64 + 32) ** -0.5)

    nc = bass.Bass("TRN2", target_bir_lowering=False)
    xT = nc.dram_tensor("xT", [128, 8, S], F32, kind="ExternalInput").ap()
    posd = nc.dram_tensor("pos", [1, S], I32, kind="ExternalInput").ap()
    wpack = nc.dram_tensor("wpack", [128, WTOT], F32, kind="ExternalInput").ap()
    cpackd = nc.dram_tensor("cpack", [128, CTOT], F32, kind="ExternalInput").ap()
    wbf = nc.dram_tensor("wbf", [128, WTOT], BF16, kind="Internal").ap()
    outT = nc.dram_tensor("outT", [128, 8, OWN], F32, kind="ExternalOutput").ap()

    with ExitStack() as top:
        sc = Sched(nc, top)

        def sb(stack, name, shape, dt):
            return stack.enter_context(nc.sbuf_tensor(name, shape, dt))

        psum = top.enter_context(nc.psum_tensor("ps", [128, 8 * 512], F32))
        PSB = [sc.buf("psb%d" % i) for i in range(8)]
        ps_rr = [0]

        def bank(i):
            return psum[:, i * 512:(i + 1) * 512]

        def next_bank(lo=0, hi=8):
            i = lo + ps_rr[0] % (hi - lo)
            ps_rr[0] += 1
            return i

        cp = sb(top, "cp", [128, CTOT], F32)
        B_cp = sc.buf("cp")
        ones = sb(top, "ones", [128, 128], BF16)
        B_ones = sc.buf("ones")
        ybT = sb(top, "ybT", [128, 4, OWNH], BF16)
        B_yb = sc.buf("yb")

        def ccol(name, i=0, p0=0, p1=128):
            o = coff[name][0] + i
            return cp[p0:p1, o:o + 1]

        sc.op("sp", lambda e: e.dma_start(out=cp[:, :], in_=cpackd[:, :]), writes=[B_cp], dma_key="cp")
        sc.op("dve", lambda e: e.memset(ones[:, :], 1.0), writes=[B_ones])
        eps_col = sb(top, "eps_col", [128, 1], F32)
        sc.op("dve", lambda e: e.memset(eps_col[:, :], EPS), writes=[B_ones])
        B_wbf = {}
        order = ["w_inA", "w_uq", "w_ukv", "w_inG", "w_out", "m0w1", "m0w2", "pw1", "dwd", "pw2", "m1w1", "m1w2"]
        for name in order:
            o, n = woff[name]
            B_wbf[name] = sc.buf("wbf_" + name)
            step = 8192
            for a in range(0, n, step):
                b = min(n, a + step)
                sc.op("pool", (lambda e, a=a, b=b, o=o: e.dma_start(out=wbf[:, o + a:o + b], in_=wpack[:, o + a:o + b])),
                      writes=[B_wbf[name]], dma_key="cv_" + name)

        evac_rr = [0]

        def evac_eng():
            evac_rr[0] += 1
            return "act" if evac_rr[0] % 2 else "dve"

        def copy_op(eng, out, in_, reads, writes):
            if eng == "act":
                sc.op("act", lambda e: e.activation(out=out, in_=in_, func=AF.Copy), reads=reads, writes=writes)
            elif eng == "dve":
                sc.op("dve", lambda e: e.tensor_copy(out=out, in_=in_), reads=reads, writes=writes)
            else:
                sc.op("pool", lambda e: e.tensor_copy(out=out, in_=in_), reads=reads, writes=writes)

        def mm_group(bi, W, pairs, reads, M=128, c0=0):
            n = len(pairs)
            for i, (l, r) in enumerate(pairs):
                sc.op("pe", (lambda e, l=l, r=r, i=i: e.matmul(bank(bi)[0:M, c0:c0 + W], l, r, start=(i == 0), stop=(i == n - 1))),
                      reads=reads, writes=[PSB[bi]])

        def stat_rstd(sq_chunks, reads, W, dim, rstd_ap, B_rstd, mean_out=None):
            bi = next_bank()
            mm_group(bi, W, [(ones[:, :], s) for s in sq_chunks], reads + [B_ones])
            sc.op("act", lambda e: e.activation(out=rstd_ap, in_=bank(bi)[:, 0:W], func=AF.Ln, bias=eps_col[:, 0:1], scale=1.0 / dim),
                  reads=[PSB[bi], B_ones], writes=[B_rstd])
            sc.op("act", lambda e: e.activation(out=rstd_ap, in_=rstd_ap, func=AF.Exp, scale=-0.5), reads=[B_rstd], writes=[B_rstd])

        def rms_apply(nch, src, B_src, gname, gidx0, rstd_ap, B_rstd, dst, B_dst, W, engs=("dve",)):
            for c in range(nch):
                eng = engs[c % len(engs)]
                sc.op(eng, (lambda e, c=c: e.scalar_tensor_tensor(out=dst(c), in0=src(c), scalar=ccol(gname, gidx0 + c),
                                                                   in1=rstd_ap, op0=ALU.mult, op1=ALU.mult)),
                      reads=[B_src, B_rstd, B_cp], writes=[B_dst])

        with ExitStack() as ab:
            qnT = sb(ab, "qnT", [128, 3, OWNH], BF16)
            B_qn = sc.buf("qn")
            cosT = sb(ab, "cosT", [128, OWNH], BF16)
            sinT = sb(ab, "sinT", [128, OWNH], BF16)
            B_tab = sc.buf("tab")
            ckvT = sb(ab, "ckvT", [128, 2, S], BF16)
            B_ckv = sc.buf("ckv")
            Kt = sb(ab, "Kt", [128, S], BF16)
            B_Kpe = sc.buf("Kpe")
            B_Kn = sc.buf("Kn")
            with ExitStack() as pa:
                xt = [sb(pa, "xtA", [128, 4, 512], F32), sb(pa, "xtB", [128, 4, 512], F32)]
                B_xt = [sc.buf("xtA"), sc.buf("xtB")]
                sqa = sb(pa, "sqa", [128, 8, 512], BF16)
                B_sq = sc.buf("sq")
                hbuf = [sb(pa, "hbuf%d" % i, [128, 8, 512], BF16) for i in range(2)]
                B_hs = [sc.buf("h%d" % i) for i in range(2)]
                rstd = sb(pa, "rstd", [128, 512], F32)
                B_rstd = sc.buf("rstd")
                winA = sb(pa, "winA", [128, 7, 8, 128], BF16)
                B_winA = sc.buf("winA")
                lat = sb(pa, "lat", [128, 3, 512], F32)
                B_lat = sc.buf("lat")
                sq2 = sb(pa, "sq2", [128, 3, 512], BF16)
                B_sq2 = sc.buf("sq2")
                rstd2 = sb(pa, "rstd2", [128, 512], F32)
                B_rstd2 = sc.buf("rstd2")
                posi = sb(pa, "posi", [128, 512], I32)
                B_pos = sc.buf("pos")
                ang = sb(pa, "ang", [128, 512], F32)
                B_ang = sc.buf("ang")
                cts = [sb(pa, "ct%d" % i, [128, 512], F32) for i in range(2)]
                sts = [sb(pa, "st%d" % i, [128, 512], F32) for i in range(2)]
                B_css = [sc.buf("cs%d" % i) for i in range(2)]
                t1 = sb(pa, "t1", [128, 512], F32)
                t2 = sb(pa, "t2", [128, 512], F32)
                B_t12 = sc.buf("t12")
                k1, k2, B_k12 = lat[:, 0, :], lat[:, 1, :], B_lat
                o, n = woff["w_inA"]
                sc.op("sp", lambda e, o=o, n=n: e.dma_start(out=winA[:, :, :, :].rearrange("p m k j -> p (m k j)"), in_=wbf[:, o:o + n]),
                      reads=[B_wbf["w_inA"]], writes=[B_winA], dma_key="winA")

                R = slice(64, 96)
                PI = float(np.pi)
                MAGIC = 12582912.0
                C1 = 6.28125
                C2 = float(2 * np.pi - 6.28125)
                all_tiles = own_tiles + oth_tiles
                n_own = len(own_tiles)

                def gen_N(ti):
                    T0, W = all_tiles[ti]
                    hb, B_hb = hbuf[ti % 2], B_hs[ti % 2]
                    for hf in range(2):
                        sc.op("sp", (lambda e, hf=hf: e.dma_start(out=xt[hf][:, :, 0:W], in_=xT[:, 4 * hf:4 * hf + 4, T0:T0 + W])),
                              writes=[B_xt[hf]], dma_key="xt%d" % hf)
                    for hf in range(2):
                        sc.op("act", (lambda e, hf=hf: e.activation(out=sqa[:, 4 * hf:4 * hf + 4, 0:W], in_=xt[hf][:, :, 0:W], func=AF.Square)),
                              reads=[B_xt[hf]], writes=[B_sq])
                    yield
                    stat_rstd([sqa[:, c, 0:W] for c in range(8)], [B_sq], W, float(D), rstd[:, 0:W], B_rstd)
                    yield
                    for c in range(8):
                        sc.op("dve", (lambda e, c=c: e.scalar_tensor_tensor(out=hb[:, c, 0:W], in0=xt[c // 4][:, c % 4, 0:W],
                                                                             scalar=ccol("gains", c), in1=rstd[:, 0:W],
                                                                             op0=ALU.mult, op1=ALU.mult)),
                              reads=[B_xt[c // 4], B_rstd, B_cp], writes=[B_hb])
                    yield

                def gen_R(ti):
                    T0, W = all_tiles[ti]
                    ct, st, B_cs = cts[ti % 2], sts[ti % 2], B_css[ti % 2]
                    sc.op("sp", (lambda e: e.dma_start(out=posi[64:96, 0:W], in_=posd[0:1, T0:T0 + W].broadcast_to([32, W]))),
                          writes=[B_pos], dma_key="pos")
                    sc.op("dve", (lambda e: e.tensor_copy(out=ang[R, 0:W], in_=posi[R, 0:W])), reads=[B_pos], writes=[B_ang])
                    sc.op("dve", (lambda e: e.tensor_scalar(out=ang[R, 0:W], in0=ang[R, 0:W], scalar1=ccol("invf", 0, 64, 96), scalar2=None,
                                                            op0=ALU.mult)), reads=[B_ang, B_cp], writes=[B_ang])
                    sc.op("dve", (lambda e: e.tensor_scalar(out=t1[R, 0:W], in0=ang[R, 0:W], scalar1=float(1.0 / (2 * np.pi)), scalar2=MAGIC,
                                                            op0=ALU.mult, op1=ALU.add)), reads=[B_ang], writes=[B_t12])
                    sc.op("dve", (lambda e: e.tensor_scalar(out=t1[R, 0:W], in0=t1[R, 0:W], scalar1=-MAGIC, scalar2=None, op0=ALU.add)),
                          reads=[B_t12], writes=[B_t12])
                    yield
                    sc.op("dve", (lambda e: e.scalar_tensor_tensor(out=t2[R, 0:W], in0=t1[R, 0:W], scalar=-C1, in1=ang[R, 0:W], op0=ALU.mult, op1=ALU.add)),
                          reads=[B_t12, B_ang], writes=[B_t12])
                    sc.op("dve", (lambda e: e.scalar_tensor_tensor(out=t2[R, 0:W], in0=t1[R, 0:W], scalar=-C2, in1=t2[R, 0:W], op0=ALU.mult, op1=ALU.add)),
                          reads=[B_t12], writes=[B_t12])
                    sc.op("dve", (lambda e: e.tensor_scalar(out=t2[R, 0:W], in0=t2[R, 0:W], scalar1=-PI, scalar2=PI, op0=ALU.max, op1=ALU.min)),
                          reads=[B_t12], writes=[B_t12])
                    sc.op("act", (lambda e: e.activation(out=st[R, 0:W], in_=t2[R, 0:W], func=AF.Sin, scale=ccol("sgn", 0, 64, 96))),
                          reads=[B_t12, B_cp], writes=[B_cs])
                    sc.op("act", (lambda e: e.activation(out=ct[R, 0:W], in_=t2[R, 0:W], func=AF.Sin, scale=0.5)),
                          reads=[B_t12], writes=[B_cs])
                    yield
                    sc.op("dve", (lambda e: e.tensor_tensor(out=ct[R, 0:W], in0=ct[R, 0:W], in1=ct[R, 0:W], op=ALU.mult)), reads=[B_cs], writes=[B_cs])
                    sc.op("dve", (lambda e: e.tensor_scalar(out=ct[R, 0:W], in0=ct[R, 0:W], scalar1=-2.0, scalar2=1.0, op0=ALU.mult, op1=ALU.add)),
                          reads=[B_cs], writes=[B_cs])
                    if ti < n_own:
                        sc.op("act", (lambda e: e.activation(out=cosT[R, T0:T0 + W], in_=ct[R, 0:W], func=AF.Copy)), reads=[B_cs], writes=[B_tab])
                        sc.op("act", (lambda e: e.activation(out=sinT[R, T0:T0 + W], in_=st[R, 0:W], func=AF.Copy)), reads=[B_cs], writes=[B_tab])
                    yield

                def gen_P(ti):
                    T0, W = all_tiles[ti]
                    hb, B_hb = hbuf[ti % 2], B_hs[ti % 2]
                    ct, st, B_cs = cts[ti % 2], sts[ti % 2], B_css[ti % 2]
                    hT = lambda c: hb[:, c, 0:W]
                    groups = [("kv", [3, 4], "kvg", 256.0)]
                    if ti < n_own:
                        groups.append(("q", [0, 1, 2], "qg", 384.0))
                    for (gname, chunks, gn, dim) in groups:
                        for j, m in enumerate(chunks):
                            bi = next_bank()
                            mm_group(bi, W, [(winA[:, m, k, :], hT(k)) for k in range(8)], [B_hb, B_winA])
                            copy_op(evac_eng(), lat[:, j, 0:W], bank(bi)[:, 0:W], [PSB[bi]], [B_lat])
                        nchk = len(chunks)
                        sc.op("act", (lambda e, nchk=nchk: e.activation(out=sq2[:, 0:nchk, 0:W], in_=lat[:, 0:nchk, 0:W], func=AF.Square)),
                              reads=[B_lat], writes=[B_sq2])
                        yield
                        stat_rstd([sq2[:, j, 0:W] for j in range(nchk)], [B_sq2], W, dim, rstd2[:, 0:W], B_rstd2)
                        yield
                        if gname == "kv":
                            dst = lambda c: ckvT[:, c, T0:T0 + W]
                            B_dst = B_ckv
                        else:
                            dst = lambda c: qnT[:, c, T0:T0 + W]
                            B_dst = B_qn
                        rms_apply(nchk, (lambda c: lat[:, c, 0:W]), B_lat, gn, 0, rstd2[:, 0:W], B_rstd2, dst, B_dst, W)
                        yield
                    b1 = next_bank()
                    mm_group(b1, W, [(winA[:, 5, k, :], hT(k)) for k in range(8)], [B_hb, B_winA])
                    b2 = next_bank()
                    mm_group(b2, W, [(winA[:, 6, k, :], hT(k)) for k in range(8)], [B_hb, B_winA])
                    sc.op("dve", (lambda e: e.tensor_tensor(out=k1[R, 0:W], in0=bank(b1)[R, 0:W], in1=ct[R, 0:W], op=ALU.mult)),
                          reads=[PSB[b1], B_cs], writes=[B_k12])
                    sc.op("dve", (lambda e: e.tensor_tensor(out=k2[R, 0:W], in0=bank(b2)[R, 0:W], in1=st[R, 0:W], op=ALU.mult)),
                          reads=[PSB[b2], B_cs], writes=[B_k12])
                    sc.op("dve", (lambda e: e.tensor_tensor(out=Kt[R, T0:T0 + W], in0=k1[R, 0:W], in1=k2[R, 0:W], op=ALU.add)),
                          reads=[B_k12], writes=[B_Kpe])
                    yield

                def drain(g):
                    for _ in g:
                        pass

                def chain(*gens):
                    for g in gens:
                        yield from g

                ntile = len(all_tiles)
                drain(gen_N(0))
                drain(gen_R(0))
                for ti in range(ntile):
                    gp = gen_P(ti)
                    gn = chain(gen_N(ti + 1), gen_R(ti + 1)) if ti + 1 < ntile else None
                    while gp is not None or gn is not None:
                        if gn is not None:
                            try:
                                next(gn)
                            except StopIteration:
                                gn = None
                        if gp is not None:
                            try:
                                next(gp)
                            except StopIteration:
                                gp = None
                sc.flush()
            with ExitStack() as pb:
                wuq = sb(pb, "wuq", [128, 8, 2, 3, 96], BF16)
                B_wuq = sc.buf("wuq")
                wukv = sb(pb, "wukv", [128, 2, 1024], BF16)
                B_wukv = sc.buf("wukv")
                o, n = woff["w_uq"]
                sc.op("sp", lambda e, o=o, n=n: e.dma_start(out=wuq[:, :, :, :, :].rearrange("p h v k j -> p (h v k j)"), in_=wbf[:, o:o + n]),
                      reads=[B_wbf["w_uq"]], writes=[B_wuq], dma_key="wuq", extra_dma={"cv_w_uq": sc.dsems["cv_w_uq"][1]})
                o, n = woff["w_ukv"]
                sc.op("sp", lambda e, o=o, n=n: e.dma_start(out=wukv[:, :, :].rearrange("p k j -> p (k j)"), in_=wbf[:, o:o + n]),
                      reads=[B_wbf["w_ukv"]], writes=[B_wukv], dma_key="wukv", extra_dma={"cv_w_ukv": sc.dsems["cv_w_ukv"][1]})

                Vt = sb(pb, "Vt", [128, NKT, 128], BF16)
                B_V = sc.buf("V")
                Pb = [sb(pb, "Pb%d" % i, [128, 3, 512], BF16) for i in range(3)]
                B_P = [sc.buf("P%d" % i) for i in range(3)]
                Kt2 = sb(pb, "Kt2", [128, S], BF16)
                Ks = [Kt, Kt2]
                B_Kns = [B_Kn, sc.buf("Kn2")]
                B_Kpes = [B_Kpe, sc.buf("Kpe2")]
                Qs = [sb(pb, "Qh%d" % i, [128, OWNH], BF16) for i in range(2)]
                B_Qs = [sc.buf("Q%d" % i) for i in range(2)]
                sqt = sb(pb, "sqt", [128, 512], BF16)
                B_sqt = sc.buf("sqt")
                mxs = [sb(pb, "mx%d" % i, [128, 64], F32) for i in range(2)]
                B_mxs = [sc.buf("mx%d" % i) for i in range(2)]
                negms = [sb(pb, "negm%d" % i, [128, 2], F32) for i in range(2)]
                B_negms = [sc.buf("negm%d" % i) for i in range(2)]
                rec = sb(pb, "rec", [128, 512], F32)
                B_rec = sc.buf("rec")
                qtmp = sb(pb, "qtmp", [128, 2, 512], F32)
                B_qtmp = sc.buf("qtmp")
                sc.op("pool", lambda e: e.memset(Vt[:, :, 64:128], 1.0), writes=[B_V])
                sc.op("pool", lambda e: e.tensor_copy(out=Kt2[R, :], in_=Kt[R, :]), reads=[B_Kpe], writes=[B_Kpes[1]])
                kblocks = split_tiles(S, 512)
                PB = 7
                OB = 6

                def prologue(h):
                    Kc, B_Kn_c, B_Kpe_c = Ks[h % 2], B_Kns[h % 2], B_Kpes[h % 2]
                    Qc, B_Qc = Qs[h % 2], B_Qs[h % 2]
                    mx, B_mx = mxs[h % 2], B_mxs[h % 2]
                    negm, B_negm = negms[h % 2], B_negms[h % 2]
                    for (k0, kw) in kblocks:
                        mm_group(PB, kw, [(wukv[:, k, h * 128:h * 128 + 64], ckvT[:, k, k0:k0 + kw]) for k in range(2)], [B_wukv, B_ckv], M=64)
                        copy_op("dve", Kc[0:64, k0:k0 + kw], bank(PB)[0:64, 0:kw], [PSB[PB]], [B_Kn_c])
                        yield
                    for (T0, W) in own_tiles:
                        mm_group(PB, W, [(wuq[:, h, 0, k, :], qnT[:, k, T0:T0 + W]) for k in range(3)], [B_wuq, B_qn], M=96)
                        copy_op("dve", Qc[0:64, T0:T0 + W], bank(PB)[0:64, 0:W], [PSB[PB]], [B_Qc])
                        sc.op("dve", (lambda e, T0=T0, W=W: e.tensor_tensor(out=qtmp[R, 0, 0:W], in0=bank(PB)[R, 0:W], in1=cosT[R, T0:T0 + W], op=ALU.mult)),
                              reads=[PSB[PB], B_tab], writes=[B_qtmp])
                        yield
                        mm_group(PB, W, [(wuq[:, h, 1, k, :], qnT[:, k, T0:T0 + W]) for k in range(3)], [B_wuq, B_qn], M=96)
                        sc.op("dve", (lambda e, T0=T0, W=W: e.tensor_tensor(out=qtmp[R, 1, 0:W], in0=bank(PB)[R, 0:W], in1=sinT[R, T0:T0 + W], op=ALU.mult)),
                              reads=[PSB[PB], B_tab], writes=[B_qtmp])
                        sc.op("pool", (lambda e, T0=T0, W=W: e.tensor_tensor(out=Qc[R, T0:T0 + W], in0=qtmp[R, 0, 0:W], in1=qtmp[R, 1, 0:W], op=ALU.add)),
                              reads=[B_qtmp], writes=[B_Qc])
                        yield
                    nq = len(own_tiles)
                    nk = len(kblocks)
                    col = 0
                    for (src_t, B_src, tiles) in ((Qc, [B_Qc], own_tiles), (Kc, [B_Kn_c, B_Kpe_c], kblocks)):
                        for (T0, W) in tiles:
                            sc.op("pool", (lambda e, src_t=src_t, T0=T0, W=W: e.tensor_tensor(out=sqt[0:96, 0:W], in0=src_t[0:96, T0:T0 + W], in1=src_t[0:96, T0:T0 + W], op=ALU.mult)),
                                  reads=B_src, writes=[B_sqt])
                            mm_group(PB, W, [(ones[0:96, :], sqt[0:96, 0:W])], [B_sqt, B_ones])
                            sc.op("dve", (lambda e, W=W, col=col: e.tensor_reduce(out=mx[:, col:col + 1], in_=bank(PB)[:, 0:W], axis=AX.X, op=ALU.max)),
                                  reads=[PSB[PB]], writes=[B_mx])
                            col += 1
                            yield
                    sc.op("dve", lambda e: e.tensor_reduce(out=negm[:, 0:1], in_=mx[:, 0:nq], axis=AX.X, op=ALU.max), reads=[B_mx], writes=[B_negm])
                    sc.op("dve", lambda e: e.tensor_reduce(out=negm[:, 1:2], in_=mx[:, nq:nq + nk], axis=AX.X, op=ALU.max), reads=[B_mx], writes=[B_negm])
                    sc.op("dve", lambda e: e.tensor_tensor(out=negm[:, 0:1], in0=negm[:, 0:1], in1=negm[:, 1:2], op=ALU.mult), reads=[B_negm], writes=[B_negm])
                    sc.op("act", lambda e: e.activation(out=negm[:, 0:1], in_=negm[:, 0:1], func=AF.Ln), reads=[B_negm], writes=[B_negm])
                    sc.op("act", lambda e: e.activation(out=negm[:, 0:1], in_=negm[:, 0:1], func=AF.Exp, scale=0.5), reads=[B_negm], writes=[B_negm])
                    sc.op("dve", lambda e: e.tensor_scalar(out=negm[:, 0:1], in0=negm[:, 0:1], scalar1=-scale, scalar2=None, op0=ALU.mult),
                          reads=[B_negm], writes=[B_negm])
                    yield

                def vgen(h):
                    for g in range(NKT // 8):
                        bi = PB if g % 2 == 0 else OB
                        for j in range(8):
                            kt = g * 8 + j
                            mm_group(bi, 64, [(ckvT[:, k, kt * 128:(kt + 1) * 128], wukv[:, k, h * 128 + 64:h * 128 + 128]) for k in range(2)],
                                     [B_wukv, B_ckv], c0=j * 64)
                        srcv = bank(bi)[:, 0:512].rearrange("p (j d) -> p j d", d=64)
                        copy_op("dve" if g % 2 == 0 else "act", Vt[:, g * 8:g * 8 + 8, 0:64], srcv, [PSB[bi]], [B_V])

                def attention(h):
                    Kc, B_Kn_c, B_Kpe_c = Ks[h % 2], B_Kns[h % 2], B_Kpes[h % 2]
                    Qc, B_Qc = Qs[h % 2], B_Qs[h % 2]
                    negm, B_negm = negms[h % 2], B_negms[h % 2]
                    groups = []
                    kt0 = 0
                    while kt0 < NKT:
                        n = min(3, NKT - kt0)
                        groups.append((kt0, n))
                        kt0 += n
                    NG = len(groups)
                    for qi, (T0, W) in enumerate(own_tiles):
                        def s_group(g, T0=T0, W=W):
                            sl = g % 2
                            kt0, n = groups[g]
                            for j in range(n):
                                kt = kt0 + j
                                sc.op("pe", (lambda e, sl=sl, j=j, kt=kt: e.matmul(bank(3 * sl + j)[:, 0:W], Kc[0:96, kt * 128:(kt + 1) * 128],
                                                                                      Qc[0:96, T0:T0 + W], start=True, stop=True)),
                                      reads=[B_Kn_c, B_Kpe_c, B_Qc], writes=[PSB[3 * sl + j]])
                            srcp = psum[:, 3 * sl * 512:(3 * sl + n) * 512].rearrange("p (j w) -> p j w", w=512)[:, :, 0:W]
                            pi = g % 3
                            sc.op("act", (lambda e, pi=pi, srcp=srcp, n=n: e.activation(out=Pb[pi][:, 0:n, 0:W], in_=srcp, func=AF.Exp, bias=negm[:, 0:1], scale=scale)),
                                  reads=[PSB[3 * sl + j] for j in range(n)] + [B_negm], writes=[B_P[pi]])

                        def pv_group(g, T0=T0, W=W):
                            pi = g % 3
                            kt0, n = groups[g]
                            for j in range(n):
                                kt = kt0 + j
                                sc.op("pe", (lambda e, pi=pi, j=j, kt=kt: e.matmul(bank(OB)[:, 0:W], Vt[:, kt, :], Pb[pi][:, j, 0:W],
                                                                                      start=(kt == 0), stop=(kt == NKT - 1))),
                                      reads=[B_V, B_P[pi]], writes=[PSB[OB]])
                        s_group(0)
                        yield
                        if NG > 1:
                            s_group(1)
                            yield
                        for g in range(NG):
                            if g + 2 < NG:
                                s_group(g + 2)
                            pv_group(g)
                            yield
                        sc.op("dve", (lambda e, W=W: e.reciprocal(out=rec[64:128, 0:W], in_=bank(OB)[64:128, 0:W])), reads=[PSB[OB]], writes=[B_rec])
                        p0 = (h % 2) * 64
                        sc.op("dve", (lambda e, T0=T0, W=W, p0=p0: e.tensor_tensor(out=ybT[p0:p0 + 64, h // 2, T0:T0 + W], in0=bank(OB)[0:64, 0:W],
                                                                                  in1=rec[64:128, 0:W], op=ALU.mult)),
                              reads=[PSB[OB], B_rec], writes=[B_yb])

                for _ in prologue(0):
                    pass
                for h in range(NH):
                    vgen(h)
                    ga = attention(h)
                    gp = prologue(h + 1) if h + 1 < NH else None
                    cnt = 0
                    for _ in ga:
                        cnt += 1
                        if gp is not None and cnt % 3 == 0:
                            try:
                                next(gp)
                            except StopIteration:
                                gp = None
                    if gp is not None:
                        for _ in gp:
                            pass
                sc.flush()

        with ExitStack() as pc:
            XW = 16 + TW + 1
            xres = [sb(pc, "xres%d" % i, [128, 8, XW], F32) for i in range(2)]
            B_xres = [sc.buf("xres%d" % i) for i in range(2)]
            NWB = 3
            wbuf = [sb(pc, "wbuf%d" % i, [128, 4096], BF16) for i in range(NWB)]
            B_wbuf = [sc.buf("wbuf%d" % i) for i in range(NWB)]
            yaT = sb(pc, "yaT", [128, 4, TW], BF16)
            B_ya = sc.buf("ya")
            gbt = sb(pc, "gbt", [128, 4, 1 + TW], BF16)
            B_gb = sc.buf("gb")
            zt = sb(pc, "zt", [128, 4, 2 + TW], BF16)
            B_z = sc.buf("z")
            acc = [sb(pc, "acc%d" % i, [128, TW], F32) for i in range(2)]
            B_acc = [sc.buf("acc%d" % i) for i in range(2)]
            ubuf = [sb(pc, "ubuf%d" % i, [128, 8, 31 + TW], BF16) for i in range(2)]
            B_u = [sc.buf("u%d" % i) for i in range(2)]
            csq = sb(pc, "csq", [128, 8, TW], BF16)
            B_csq = sc.buf("csq")
            mean = sb(pc, "mean", [128, TW], F32)
            B_mean = sc.buf("mean")

            class Ctx:
                pass
            ctxs = []
            for nm in ("S", "T"):
                cx = Ctx()
                cx.sqh = sb(pc, "sqh" + nm, [128, 8, TW + 1], BF16)
                cx.B_sq = sc.buf("sq" + nm)
                cx.B_h = cx.B_sq
                cx.rstd = sb(pc, "rstd" + nm, [128, TW + 1], F32)
                cx.B_rstd = sc.buf("rstd" + nm)
                cx.mT = sb(pc, "mT" + nm, [128, 8, TW + 1], F32)
                cx.B_m = sc.buf("m" + nm)
                cx.hid = sb(pc, "hid" + nm, [128, 16, TW], BF16)
                cx.B_hid = sc.buf("hid" + nm)
                ctxs.append(cx)
            CS, CT = ctxs

            class WStream:
                def __init__(self):
                    self.n = 0

                def get(self, name, blk, nblk_elems):
                    i = self.n % NWB
                    self.n += 1
                    o = woff[name][0] + blk * nblk_elems
                    sc.op("sp", (lambda e, i=i, o=o, n=nblk_elems: e.dma_start(out=wbuf[i][:, 0:n], in_=wbf[:, o:o + n])),
                          writes=[B_wbuf[i]], dma_key="wb%d" % i, extra_dma={"cv_" + name: sc.dsems["cv_" + name][1]})
                    return wbuf[i], B_wbuf[i]
            ws = WStream()

            def dense(wname, nk, nm, rhs_fn, rhs_bufs, W, consume, blk0=0, per=None):
                if per is None:
                    per = 4096 // (nk * 128)
                for b0 in range(0, nm, per):
                    nb = min(per, nm - b0)
                    wt, B_wt = ws.get(wname, blk0 + b0 // per, per * nk * 128)
                    for j in range(nb):
                        m = b0 + j
                        bi = next_bank()
                        mm_group(bi, W, [(wt[:, (j * nk + k) * 128:(j * nk + k + 1) * 128], rhs_fn(k)) for k in range(nk)], rhs_bufs + [B_wt])
                        consume(m, bi)
                    yield (nb * nk * 0.2, 1.0)

            def rmsnorm_pre(cx, xsrc, B_x, W, gidx):
                sqh, rstd = cx.sqh, cx.rstd
                for hf in range(2):
                    sc.op("act", (lambda e, hf=hf: e.activation(out=sqh[:, 4 * hf:4 * hf + 4, 0:W], in_=xsrc(slice(4 * hf, 4 * hf + 4)), func=AF.Square)),
                          reads=[B_x, cx.B_h], writes=[cx.B_sq])
                yield (0.0, 5.0)
                stat_rstd([sqh[:, c, 0:W] for c in range(8)], [cx.B_sq], W, float(D), rstd[:, 0:W], cx.B_rstd)
                for c in range(8):
                    sc.op("dve", (lambda e, c=c: e.scalar_tensor_tensor(out=sqh[:, c, 0:W], in0=xsrc(c), scalar=ccol("gains", gidx * 8 + c), in1=rstd[:, 0:W],
                                                                         op0=ALU.mult, op1=ALU.mult)),
                          reads=[B_x, cx.B_rstd, B_cp, cx.B_sq], writes=[cx.B_h])
                yield (1.6, 8.5)

            def post_norm_residual(cx, xdst, B_x, W, gidx):
                sqh, rstd, mT = cx.sqh, cx.rstd, cx.mT
                for hf in range(2):
                    sc.op("act", (lambda e, hf=hf: e.activation(out=sqh[:, 4 * hf:4 * hf + 4, 0:W], in_=mT[:, 4 * hf:4 * hf + 4, 0:W], func=AF.Square)),
                          reads=[cx.B_m, cx.B_h], writes=[cx.B_sq])
                yield (0.0, 5.0)
                stat_rstd([sqh[:, c, 0:W] for c in range(8)], [cx.B_sq], W, float(D), rstd[:, 0:W], cx.B_rstd)
                for c in range(8):
                    sc.op("dve", (lambda e, c=c: e.scalar_tensor_tensor(out=mT[:, c, 0:W], in0=mT[:, c, 0:W], scalar=ccol("gains", gidx * 8 + c), in1=rstd[:, 0:W],
                                                                         op0=ALU.mult, op1=ALU.mult)),
                          reads=[cx.B_m, cx.B_rstd, B_cp], writes=[cx.B_m])
                    aeng = "pool" if c % 3 != 2 else "dve"
                    sc.op(aeng, (lambda e, c=c: e.tensor_tensor(out=xdst(c), in0=xdst(c), in1=mT[:, c, 0:W], op=ALU.add)),
                          reads=[cx.B_m, B_x], writes=[B_x])
                yield (1.6, 10.0)

            def mlp_half(cx, W, layer, half):
                sqh, mT, hid = cx.sqh, cx.mT, cx.hid

                def cons1(m, bi):
                    sc.op("act", (lambda e: e.activation(out=hid[:, m, 0:W], in_=bank(bi)[:, 0:W], func=AF.Relu)), reads=[PSB[bi]], writes=[cx.B_hid])
                    sc.op("pool", (lambda e: e.tensor_tensor(out=hid[:, m, 0:W], in0=hid[:, m, 0:W], in1=hid[:, m, 0:W], op=ALU.mult)),
                          reads=[cx.B_hid], writes=[cx.B_hid])
                yield from dense("m%dw1" % layer, 8, 16, (lambda k: sqh[:, k, 0:W]), [cx.B_h], W, cons1, blk0=half * 4)

                def cons2(m, bi):
                    if half == 0:
                        copy_op(evac_eng(), mT[:, m, 0:W], bank(bi)[:, 0:W], [PSB[bi]], [cx.B_m])
                    else:
                        sc.op("dve", (lambda e: e.tensor_tensor(out=mT[:, m, 0:W], in0=bank(bi)[:, 0:W], in1=mT[:, m, 0:W], op=ALU.add)),
                              reads=[PSB[bi], cx.B_m], writes=[cx.B_m])
                yield from dense("m%dw2" % layer, 16, 8, (lambda k: hid[:, k, 0:W]), [cx.B_hid], W, cons2, blk0=half * 4)

            for c in range(2):
                sc.op("pool", lambda e, c=c: e.memset(ubuf[c][:, :, 0:31], 0.0), writes=[B_u[c]])
            sc.op("pool", lambda e: e.memset(zt[:, :, 0:1], 0.0), writes=[B_z])
            sc.op("pool", lambda e: e.memset(xres[0][:, :, 0:16], 0.0), writes=[B_xres[0]])

            ntl = len(own_tiles)

            def pipe_L0(ti):
                cx = CS
                T0, W = own_tiles[ti]
                xr = xres[ti % 2]
                B_xr = B_xres[ti % 2]
                ub = ubuf[ti % 2]
                B_ub = B_u[ti % 2]
                W1 = W + 1
                sqh, mT = cx.sqh, cx.mT
                x0 = lambda c: xr[:, c, 16:16 + W]
                x01 = lambda c: xr[:, c, 16:16 + W1]
                for hf in range(2):
                    sc.op("sp", (lambda e, hf=hf: e.dma_start(out=xr[:, 4 * hf:4 * hf + 4, 16:16 + W1], in_=xT[:, 4 * hf:4 * hf + 4, T0:T0 + W1])),
                          writes=[B_xr], dma_key="xr%d" % (ti % 2))
                yield (0.0, 12.0)
                yield from rmsnorm_pre(cx, x01, B_xr, W1, 0)
                gcs = mT

                def cons_g(m, bi):
                    grp, c = m // 4, m % 4
                    if grp == 0:
                        copy_op(evac_eng(), gbt[:, c, 0:W1], bank(bi)[:, 0:W1], [PSB[bi]], [B_gb])
                    elif grp == 1:
                        copy_op("act", gcs[:, c, 0:W1], bank(bi)[:, 0:W1], [PSB[bi]], [cx.B_m])
                    else:
                        sc.op("dve", (lambda e: e.tensor_tensor(out=zt[:, c, 1:1 + W1], in0=bank(bi)[:, 0:W1], in1=gcs[:, c, 0:W1], op=ALU.mult)),
                              reads=[PSB[bi], cx.B_m], writes=[B_z])
                yield from dense("w_inG", 8, 12, (lambda k: sqh[:, k, 0:W1]), [cx.B_h], W1, cons_g)
                for c in range(4):
                    a = acc[c % 2]
                    B_a = B_acc[c % 2]
                    sc.op("dve", (lambda e, c=c, a=a: e.tensor_scalar(out=a[:, 0:W], in0=zt[:, c, 0:W], scalar1=ccol("sck", c * 3 + 0), scalar2=None, op0=ALU.mult)),
                          reads=[B_z, B_cp], writes=[B_a])
                    for k in (1, 2):
                        sc.op("dve", (lambda e, c=c, a=a, k=k: e.scalar_tensor_tensor(out=a[:, 0:W], in0=zt[:, c, k:k + W], scalar=ccol("sck", c * 3 + k),
                                                                                      in1=a[:, 0:W], op0=ALU.mult, op1=ALU.add)),
                              reads=[B_z, B_cp, B_a], writes=[B_a])
                    sc.op("pool", (lambda e, c=c, a=a: e.tensor_tensor(out=yaT[:, c, 0:W], in0=a[:, 0:W], in1=gbt[:, c, 0:W], op=ALU.mult)),
                          reads=[B_a, B_gb], writes=[B_ya])
                sc.op("pool", (lambda e: e.tensor_copy(out=zt[:, :, 0:1], in_=zt[:, :, W:W + 1])), reads=[B_z], writes=[B_z])
                yield (0.0, 9.0)

                def rhs_mix(k):
                    return yaT[:, k, 0:W] if k < 4 else ybT[:, k - 4, T0:T0 + W]

                def cons_m(m, bi):
                    copy_op(evac_eng(), mT[:, m, 0:W], bank(bi)[:, 0:W], [PSB[bi]], [cx.B_m])
                yield from dense("w_out", 8, 8, rhs_mix, [B_ya, B_yb], W, cons_m)
                yield from post_norm_residual(cx, x0, B_xr, W, 1)
                yield from rmsnorm_pre(cx, x0, B_xr, W, 2)
                yield from mlp_half(cx, W, 0, 0)
                yield from mlp_half(cx, W, 0, 1)
                yield from post_norm_residual(cx, x0, B_xr, W, 3)
                yield from rmsnorm_pre(cx, x0, B_xr, W, 4)

                abank = [None]

                def cons_p(m, bi):
                    c, isg = m // 2, m % 2
                    if isg == 0:
                        abank[0] = bi
                    else:
                        ba = abank[0]
                        g = acc[c % 2]
                        B_g = B_acc[c % 2]
                        sc.op("act", (lambda e: e.activation(out=g[:, 0:W], in_=bank(bi)[:, 0:W], func=AF.Sigmoid, bias=ccol("bpw1", 8 + c), scale=1.0)),
                              reads=[PSB[bi], B_cp], writes=[B_g])
                        sc.op("dve", (lambda e: e.scalar_tensor_tensor(out=ub[:, c, 31:31 + W], in0=bank(ba)[:, 0:W], scalar=ccol("bpw1", c), in1=g[:, 0:W],
                                                                       op0=ALU.add, op1=ALU.mult)),
                              reads=[PSB[ba], B_g, B_cp], writes=[B_ub])
                yield from dense("pw1", 8, 16, (lambda k: sqh[:, k, 0:W]), [cx.B_h], W, cons_p)
                yield (0.0, 3.0)

            def pipe_L1(ti):
                cx = CT
                T0, W = own_tiles[ti]
                xr = xres[ti % 2]
                B_xr = B_xres[ti % 2]
                ub = ubuf[ti % 2]
                B_ub = B_u[ti % 2]
                sqh, mT, rstd = cx.sqh, cx.mT, cx.rstd
                x1 = lambda c: xr[:, c, 0:W]
                cacc = mT
                for c in range(8):
                    wt, B_wt = ws.get("dwd", c, 31 * 128)
                    bi = next_bank()
                    mm_group(bi, W, [(wt[:, k * 128:(k + 1) * 128], ub[:, c, k:k + W]) for k in range(31)], [B_ub, B_wt])
                    sc.op("dve", (lambda e, c=c, bi=bi: e.tensor_scalar(out=cacc[:, c, 0:W], in0=bank(bi)[:, 0:W], scalar1=ccol("bdw", c), scalar2=None, op0=ALU.add)),
                          reads=[PSB[bi], B_cp], writes=[cx.B_m])
                    sc.op("act", (lambda e, c=c: e.activation(out=sqh[:, c, 0:W], in_=cacc[:, c, 0:W], func=AF.Copy)), reads=[cx.B_m, cx.B_h], writes=[cx.B_sq])
                    sc.op("act", (lambda e, c=c: e.activation(out=csq[:, c, 0:W], in_=cacc[:, c, 0:W], func=AF.Square)), reads=[cx.B_m], writes=[B_csq])
                    yield (6.2, 1.0)
                yield (0.0, 3.0)
                b_mean = next_bank()
                mm_group(b_mean, W, [(ones[:, :], sqh[:, c, 0:W]) for c in range(8)], [cx.B_sq, B_ones])
                b_ex2 = next_bank()
                mm_group(b_ex2, W, [(ones[:, :], csq[:, c, 0:W]) for c in range(8)], [B_csq, B_ones])
                sc.op("dve", (lambda e: e.tensor_scalar(out=mean[:, 0:W], in0=bank(b_mean)[:, 0:W], scalar1=1.0 / D, scalar2=None, op0=ALU.mult)),
                      reads=[PSB[b_mean]], writes=[B_mean])
                sc.op("dve", (lambda e: e.tensor_tensor(out=rstd[:, 0:W], in0=mean[:, 0:W], in1=mean[:, 0:W], op=ALU.mult)), reads=[B_mean], writes=[cx.B_rstd])
                sc.op("dve", (lambda e: e.scalar_tensor_tensor(out=rstd[:, 0:W], in0=bank(b_ex2)[:, 0:W], scalar=1.0 / D, in1=rstd[:, 0:W],
                                                               op0=ALU.mult, op1=ALU.subtract)), reads=[PSB[b_ex2], cx.B_rstd], writes=[cx.B_rstd])
                sc.op("act", (lambda e: e.activation(out=rstd[:, 0:W], in_=rstd[:, 0:W], func=AF.Ln, bias=eps_col[:, 0:1], scale=1.0)),
                      reads=[cx.B_rstd, B_ones], writes=[cx.B_rstd])
                sc.op("act", (lambda e: e.activation(out=rstd[:, 0:W], in_=rstd[:, 0:W], func=AF.Exp, scale=-0.5)), reads=[cx.B_rstd], writes=[cx.B_rstd])
                for c in range(8):
                    sc.op("pool", (lambda e, c=c: e.tensor_tensor(out=cacc[:, c, 0:W], in0=cacc[:, c, 0:W], in1=mean[:, 0:W], op=ALU.subtract)),
                          reads=[cx.B_m, B_mean], writes=[cx.B_m])
                    sc.op("dve", (lambda e, c=c: e.tensor_tensor(out=cacc[:, c, 0:W], in0=cacc[:, c, 0:W], in1=rstd[:, 0:W], op=ALU.mult)),
                          reads=[cx.B_m, cx.B_rstd], writes=[cx.B_m])
                    sc.op("act", (lambda e, c=c: e.activation(out=sqh[:, c, 0:W], in_=cacc[:, c, 0:W], func=AF.Silu, bias=ccol("lnb", c), scale=ccol("lng", c))),
                          reads=[cx.B_m, B_cp, cx.B_sq], writes=[cx.B_h])
                yield (3.2, 14.0)

                def cons_p2(m, bi):
                    sc.op("dve", (lambda e: e.tensor_scalar(out=mT[:, m, 0:W], in0=bank(bi)[:, 0:W], scalar1=ccol("bpw2", m), scalar2=None, op0=ALU.add)),
                          reads=[PSB[bi], B_cp], writes=[cx.B_m])
                yield from dense("pw2", 8, 8, (lambda k: sqh[:, k, 0:W]), [cx.B_h], W, cons_p2)
                yield from post_norm_residual(cx, x1, B_xr, W, 5)
                yield from rmsnorm_pre(cx, x1, B_xr, W, 6)
                yield from mlp_half(cx, W, 1, 0)
                yield from mlp_half(cx, W, 1, 1)
                yield from post_norm_residual(cx, x1, B_xr, W, 7)
                c_lo = 16 if ti == 0 else 0
                tok0 = T0 - 16 + c_lo
                ncol = W - c_lo
                for hf in range(2):
                    sc.op("sp", (lambda e, hf=hf: e.dma_start(out=outT[:, 4 * hf:4 * hf + 4, tok0:tok0 + ncol], in_=xr[:, 4 * hf:4 * hf + 4, c_lo:c_lo + ncol])),
                          reads=[B_xr], dma_key="out")
                yield (0.0, 6.0)

            INF = 1e30
            gS = gT = None
            nS = nT = 0
            doneS = doneT = 0
            readyS = readyT = 0.0
            s_done_t = {}
            t_done_t = {}
            carried = 0
            t_pe = 0.0
            last = "T"
            while doneT < ntl:
                if gS is None and nS < ntl and (nS < 2 or doneT >= nS - 1):
                    gS = pipe_L0(nS)
                    if nS >= 2:
                        readyS = max(readyS, t_done_t[nS - 2])
                    nS += 1
                if gT is None and nT < ntl and doneS > nT and carried >= nT:
                    gT = pipe_L1(nT)
                    readyT = max(readyT, s_done_t[nT])
                    nT += 1
                cS = max(readyS, t_pe) if gS is not None else INF
                cT = max(readyT, t_pe) if gT is not None else INF
                assert cS < INF or cT < INF
                if cS < cT or (cS == cT and last == "T"):
                    pick = "S"
                else:
                    pick = "T"
                last = pick
                if pick == "S":
                    try:
                        pe_us, lat = next(gS)
                        t_pe = max(t_pe, readyS) + pe_us
                        readyS = t_pe + lat
                    except StopIteration:
                        gS = None
                        s_done_t[doneS] = readyS
                        doneS += 1
                else:
                    try:
                        pe_us, lat = next(gT)
                        t_pe = max(t_pe, readyT) + pe_us
                        readyT = t_pe + lat
                    except StopIteration:
                        gT = None
                        t_done_t[doneT] = readyT
                        doneT += 1
                while carried < ntl - 1 and doneS > carried and doneT >= carried:
                    it = carried
                    T0, W = own_tiles[it]
                    xr, nx = xres[it % 2], xres[(it + 1) % 2]
                    ub, nu = ubuf[it % 2], ubuf[(it + 1) % 2]
                    sc.op("pool", (lambda e, xr=xr, nx=nx, W=W: e.tensor_copy(out=nx[:, :, 0:16], in_=xr[:, :, W:W + 16])),
                          reads=[B_xres[it % 2]], writes=[B_xres[(it + 1) % 2]])
                    sc.op("pool", (lambda e, ub=ub, nu=nu, W=W: e.tensor_copy(out=nu[:, :, 0:31], in_=ub[:, :, W:W + 31])),
                          reads=[B_u[it % 2]], writes=[B_u[(it + 1) % 2]])
                    carried += 1
            sc.flush(final=True)
    return nc


def prep_core(inp, core, S):
    b, half = core // 2, core % 2
    idx = np.arange(S) if half == 0 else np.arange(S - 1, -1, -1)
    x = np.asarray(inp["x"][b], dtype=np.float32)[idx]
    xT = np.ascontiguousarray(x.T.reshape(8, 128, S).transpose(1, 0, 2))
    pos = np.ascontiguousarray(np.asarray(inp["positions"][b])[idx].astype(np.int32).reshape(1, S))
    return xT, pos, half


def prep_shared(inp, half):
    woff, WTOT = weight_layout()
    coff, CTOT = const_layout()
    f = lambda k: np.asarray(inp[k], dtype=np.float32)
    wp = np.zeros((128, WTOT), np.float32)

    def put(name, arr):
        o, n = woff[name]
        assert arr.shape == (128, n), (name, arr.shape, n)
        wp[:, o:o + n] = arr
    w_in = f("even_w_in")[0]
    kr = np.zeros((D, 128), np.float32)
    kr[:, 64:96] = w_in[:, 2176:2208]
    krp = np.zeros((D, 128), np.float32)
    krp[:, 64:80] = w_in[:, 2192:2208]
    krp[:, 80:96] = w_in[:, 2176:2192]
    winA = np.concatenate([w_in[:, 1536:2176], kr, krp], axis=1)
    put("w_inA", pack_lhsT(winA, 8, 7))
    put("w_inG", pack_lhsT(w_in[:, 0:1536], 8, 12))
    w_uq = f("even_w_uq")[0].reshape(384, 8, 96)
    wq = np.zeros((384, 8, 2, 96), np.float32)
    wq[:, :, 0, :] = w_uq
    wq[:, :, 1, 64:80] = w_uq[:, :, 80:96]
    wq[:, :, 1, 80:96] = w_uq[:, :, 64:80]
    wq = wq.reshape(3, 128, 8, 2, 96).transpose(1, 2, 3, 0, 4)
    put("w_uq", np.ascontiguousarray(wq).reshape(128, -1))
    w_ukv = f("even_w_ukv")[0]
    put("w_ukv", np.ascontiguousarray(w_ukv.reshape(2, 128, 1024).transpose(1, 0, 2)).reshape(128, -1))
    put("w_out", pack_lhsT(f("even_w_out")[0], 8, 8))
    put("m0w1", pack_lhsT(f("mlp_w1")[0], 8, 32))
    put("m0w2", pack_w2(f("mlp_w2")[0]))
    pw1 = f("odd_w_pw1")[0].reshape(D, 2, 8, 128).transpose(0, 2, 1, 3).reshape(D, 2048)
    put("pw1", pack_lhsT(pw1, 8, 16))
    put("pw2", pack_lhsT(f("odd_w_pw2")[0], 8, 8))
    put("m1w1", pack_lhsT(f("mlp_w1")[1], 8, 32))
    put("m1w2", pack_w2(f("mlp_w2")[1]))

    cpk = np.zeros((128, CTOT), np.float32)

    def putc(name, arr):
        o, n = coff[name]
        assert arr.shape == (128, n), (name, arr.shape, n)
        cpk[:, o:o + n] = arr
    putc("gains", colpack(f("sandwich_gains").reshape(-1)))
    putc("qg", colpack(f("even_q_norm")[0]))
    putc("kvg", colpack(f("even_kv_norm")[0]))
    sck = f("even_sc_kernel")[0]
    wdw = f("odd_w_dw")[0]
    if half == 1:
        sck = sck[::-1]
        wdw = wdw[::-1]
    putc("sck", np.ascontiguousarray(sck.reshape(3, 4, 128).transpose(2, 1, 0)).reshape(128, 12))
    dwd = np.zeros((128, 8, 31, 128), np.float32)
    pidx = np.arange(128)
    dwd[pidx, :, :, pidx] = wdw.reshape(31, 8, 128).transpose(2, 1, 0)
    put("dwd", dwd.reshape(128, -1))
    putc("wdw", np.ascontiguousarray(wdw.reshape(31, 8, 128).transpose(2, 1, 0)).reshape(128, 248))
    putc("bpw1", colpack(f("odd_b_pw1")[0]))
    putc("bdw", colpack(f("odd_b_dw")[0]))
    putc("lng", colpack(f("odd_ln_g")[0]))
    putc("lnb", colpack(f("odd_ln_b")[0]))
    putc("bpw2", colpack(f("odd_b_pw2")[0]))
    half_d = 16
    inv_freq = (1.0 / (np.float32(10000.0) ** (np.arange(half_d, dtype=np.float32) / np.float32(half_d)))).astype(np.float32)
    invf = np.zeros((128, 1), np.float32)
    invf[64:80, 0] = inv_freq
    invf[80:96, 0] = inv_freq
    putc("invf", invf)
    sgn = np.ones((128, 1), np.float32)
    sgn[64:80, 0] = -1.0
    putc("sgn", sgn)
    return wp, cpk


_NC_CACHE = {}


def kernel(**inputs):
    x = np.asarray(inputs["x"])
    B, S, _ = x.shape
    ncores = 2 * B
    if S not in _NC_CACHE:
        _NC_CACHE[S] = build(S)
    nc = _NC_CACHE[S]
    shared = [prep_shared(inputs, h) for h in range(2)]
    in_maps = []
    for c in range(ncores):
        xT, pos, half = prep_core(inputs, c, S)
        wp, cpk = shared[half]
        in_maps.append({"xT": xT, "pos": pos, "wpack": wp, "cpack": cpk})
    res = run_bass_kernel_spmd(nc, in_maps, core_ids=list(range(ncores)))
    OWN = S // 2
    out = np.empty((B, S, D), np.float32)
    for c in range(ncores):
        b, half = c // 2, c % 2
        oT = np.asarray(res.results[c]["outT"])
        o = oT.transpose(2, 1, 0).reshape(OWN, D)
        if half == 0:
            out[b, :OWN] = o
        else:
            out[b, OWN:] = o[::-1]
    return out
```

```python
import numpy as np
from contextlib import ExitStack
import concourse.bass as bass
import concourse.mybir as mybir
from concourse.bass_utils import run_bass_kernel_spmd

F32 = mybir.dt.float32
BF16 = mybir.dt.bfloat16
I32 = mybir.dt.int32
AF = mybir.ActivationFunctionType
ALU = mybir.AluOpType
AX = mybir.AxisListType

D = 1024
NH = 8
HALO = 16
TW = 464
EPS = 1e-6
SEQ_FULL = 8192
SEM_LIM = 30000


class Buf:
    __slots__ = ("name", "last_w", "readers")

    def __init__(self, name):
        self.name = name
        self.last_w = None
        self.readers = []


class Op:
    __slots__ = ("eng", "fn", "deps", "dma", "sig", "idx", "dma_deps", "bar")

    def __init__(self, eng, fn, dma):
        self.eng = eng
        self.fn = fn
        self.dma = dma
        self.deps = set()
        self.dma_deps = {}
        self.sig = None
        self.bar = None


ENGS = ("pe", "act", "dve", "pool", "sp")


class Sched:
    def __init__(self, nc, stack):
        self.nc = nc
        self.stack = stack
        self.ops = []
        self.allbufs = []
        self.sig_count = {e: 0 for e in ENGS}
        self.esems = {e: [] for e in ENGS}
        self.dsems = {}
        self.known = {e: {} for e in ENGS}
        self.known_d = {e: {} for e in ENGS}
        self.pending_bar = None
        self.out_sems = []

    def buf(self, name):
        b = Buf(name)
        self.allbufs.append(b)
        return b

    def dsem(self, key):
        if key not in self.dsems:
            s = self.stack.enter_context(self.nc.semaphore("d_" + key))
            self.dsems[key] = [s, 0]
        return self.dsems[key]

    def esem(self, eng, epoch):
        lst = self.esems[eng]
        while len(lst) <= epoch:
            lst.append(self.stack.enter_context(self.nc.semaphore("e_%s_%d" % (eng, len(lst)))))
        return lst[epoch]

    def op(self, eng, fn, reads=(), writes=(), dma_key=None, extra_dma=None):
        o = Op(eng, fn, None)
        oid = len(self.ops)
        if dma_key is not None:
            d = self.dsem(dma_key)
            d[1] += 16
            o.dma = (dma_key, d[1])
        deps = []
        for b in reads:
            if b.last_w is not None:
                deps.append((b.last_w, "raw"))
        for b in writes:
            if b.last_w is not None:
                deps.append((b.last_w, "waw"))
            for r in b.readers:
                deps.append((r, "war"))
        for (pid, kind) in deps:
            if pid == oid:
                continue
            p = self.ops[pid]
            if p.dma is not None:
                k = p.dma[0]
                v = self.dsems[k][1] if o.dma is None or o.dma[0] != k else p.dma[1]
                o.dma_deps[k] = max(o.dma_deps.get(k, 0), v)
            else:
                if p.eng == eng and o.dma is None:
                    if eng == "pe":
                        continue
                o.deps.add(pid)
        if extra_dma:
            for k, v in extra_dma.items():
                o.dma_deps[k] = max(o.dma_deps.get(k, 0), v)
        for b in reads:
            b.readers.append(oid)
        for b in writes:
            b.last_w = oid
            b.readers = []
        self.ops.append(o)
        return oid

    def flush(self, final=False):
        ops = self.ops
        nc = self.nc
        needed = set()
        for o in ops:
            for pid in o.deps:
                needed.add(pid)
        last_of = {}
        for i, o in enumerate(ops):
            if o.dma is None:
                last_of[o.eng] = i
        for e, i in last_of.items():
            needed.add(i)
        for i, o in enumerate(ops):
            if i in needed and o.dma is None:
                self.sig_count[o.eng] += 1
                o.sig = self.sig_count[o.eng]
        bar = self.pending_bar
        per_eng = {e: [] for e in ENGS}
        for i, o in enumerate(ops):
            per_eng[o.eng].append(o)
        sched = self

        def emit(engname, engine):
            known = sched.known[engname]
            known_d = sched.known_d[engname]

            def wait_sig(e2, sig):
                if known.get(e2, 0) >= sig:
                    return
                known[e2] = sig
                ep, val = (sig - 1) // SEM_LIM, (sig - 1) % SEM_LIM + 1
                engine.wait_ge(sched.esem(e2, ep), val)

            def wait_dma(k, v):
                if known_d.get(k, 0) >= v:
                    return
                known_d[k] = v
                engine.wait_ge(sched.dsems[k][0], v)

            if bar is not None:
                for e2, sig in bar[0].items():
                    if e2 != engname and sig > 0:
                        wait_sig(e2, sig)
                for k, v in bar[1].items():
                    if not k.startswith("cv_"):
                        wait_dma(k, v)
            for o in per_eng[engname]:
                for pid in sorted(o.deps):
                    p = ops[pid]
                    wait_sig(p.eng, p.sig)
                for k, v in o.dma_deps.items():
                    wait_dma(k, v)
                ins = o.fn(engine)
                if o.dma is not None:
                    ins.then_inc(sched.dsems[o.dma[0]][0], 16)
                elif o.sig is not None:
                    ep = (o.sig - 1) // SEM_LIM
                    ins.then_inc(sched.esem(o.eng, ep), 1)
            if final and engname == "sp":
                for k, d in sched.dsems.items():
                    wait_dma(k, d[1])

        for e in ENGS:
            self.esem(e, max(0, (self.sig_count[e] - 1)) // SEM_LIM)
        with nc.Block() as block:
            @block.tensor
            def _(eng):
                emit("pe", eng)

            @block.scalar
            def _(eng):
                emit("act", eng)

            @block.vector
            def _(eng):
                emit("dve", eng)

            @block.gpsimd
            def _(eng):
                emit("pool", eng)

            @block.sync
            def _(eng):
                emit("sp", eng)
        self.pending_bar = ({e: self.sig_count[e] for e in ENGS}, {k: d[1] for k, d in self.dsems.items()})
        self.ops = []
        for b in self.allbufs:
            b.last_w = None
            b.readers = []


def weight_layout():
    off = {}
    cur = 0

    def add(name, n):
        nonlocal cur
        off[name] = (cur, n)
        cur += n
    add("w_inA", 7 * 8 * 128)
    add("w_uq", 8 * 2 * 3 * 96)
    add("w_ukv", 2 * 1024)
    add("w_inG", 12 * 8 * 128)
    add("w_out", 8 * 8 * 128)
    add("m0w1", 32 * 8 * 128)
    add("m0w2", 8 * 32 * 128)
    add("pw1", 16 * 8 * 128)
    add("pw2", 8 * 8 * 128)
    add("m1w1", 32 * 8 * 128)
    add("m1w2", 8 * 32 * 128)
    add("dwd", 8 * 31 * 128)
    return off, cur


def const_layout():
    off = {}
    cur = 0

    def add(name, n):
        nonlocal cur
        off[name] = (cur, n)
        cur += n
    add("gains", 64)
    add("qg", 3)
    add("kvg", 2)
    add("sck", 12)
    add("bpw1", 16)
    add("wdw", 8 * 31)
    add("bdw", 8)
    add("lng", 8)
    add("lnb", 8)
    add("bpw2", 8)
    add("invf", 1)
    add("sgn", 1)
    return off, cur


def pack_lhsT(w, nk, nm, mw=128):
    return np.ascontiguousarray(w.reshape(nk, 128, nm, mw).transpose(1, 2, 0, 3)).reshape(128, -1)


def pack_w2(w):
    return np.concatenate([pack_lhsT(w[h * 2048:(h + 1) * 2048], 16, 8) for h in range(2)], axis=1)


def colpack(v):
    return np.ascontiguousarray(v.reshape(-1, 128).T)


def split_tiles(n, w):
    out = []
    t0 = 0
    while t0 < n:
        ww = min(w, n - t0)
        out.append((t0, ww))
        t0 += ww
    return out


def build(S):
    OWN = S // 2
    OWNH = OWN + HALO
    own_tiles = split_tiles(OWNH, TW)
    oth_tiles = [(OWNH + a, w) for (a, w) in split_tiles(S - OWNH, 512)]
    NKT = S // 128
    woff, WTOT = weight_layout()
    coff, CTOT = const_layout()
    scale = float((64 + 32) ** -0.5)

    nc = bass.Bass("TRN2", target_bir_lowering=False)
    xT = nc.dram_tensor("xT", [128, 8, S], F32, kind="ExternalInput").ap()
    posd = nc.dram_tensor("pos", [1, S], I32, kind="ExternalInput").ap()
    wpack = nc.dram_tensor("wpack", [128, WTOT], F32, kind="ExternalInput").ap()
    cpackd = nc.dram_tensor("cpack", [128, CTOT], F32, kind="ExternalInput").ap()
    wbf = nc.dram_tensor("wbf", [128, WTOT], BF16, kind="Internal").ap()
    outT = nc.dram_tensor("outT", [128, 8, OWN], F32, kind="ExternalOutput").ap()

    with ExitStack() as top:
        sc = Sched(nc, top)

        def sb(stack, name, shape, dt):
            return stack.enter_context(nc.sbuf_tensor(name, shape, dt))

        psum = top.enter_context(nc.psum_tensor("ps", [128, 8 * 512], F32))
        PSB = [sc.buf("psb%d" % i) for i in range(8)]
        ps_rr = [0]

        def bank(i):
            return psum[:, i * 512:(i + 1) * 512]

        def next_bank(lo=0, hi=8):
            i = lo + ps_rr[0] % (hi - lo)
            ps_rr[0] += 1
            return i

        cp = sb(top, "cp", [128, CTOT], F32)
        B_cp = sc.buf("cp")
        ones = sb(top, "ones", [128, 128], BF16)
        B_ones = sc.buf("ones")
        ybT = sb(top, "ybT", [128, 4, OWNH], BF16)
        B_yb = sc.buf("yb")

        def ccol(name, i=0, p0=0, p1=128):
            o = coff[name][0] + i
            return cp[p0:p1, o:o + 1]

        sc.op("sp", lambda e: e.dma_start(out=cp[:, :], in_=cpackd[:, :]), writes=[B_cp], dma_key="cp")
        sc.op("dve", lambda e: e.memset(ones[:, :], 1.0), writes=[B_ones])
        eps_col = sb(top, "eps_col", [128, 1], F32)
        sc.op("dve", lambda e: e.memset(eps_col[:, :], EPS), writes=[B_ones])
        B_wbf = {}
        order = ["w_inA", "w_uq", "w_ukv", "w_inG", "w_out", "m0w1", "m0w2", "pw1", "dwd", "pw2", "m1w1", "m1w2"]
        for name in order:
            o, n = woff[name]
            B_wbf[name] = sc.buf("wbf_" + name)
            step = 8192
            for a in range(0, n, step):
                b = min(n, a + step)
                sc.op("pool", (lambda e, a=a, b=b, o=o: e.dma_start(out=wbf[:, o + a:o + b], in_=wpack[:, o + a:o + b])),
                      writes=[B_wbf[name]], dma_key="cv_" + name)

        evac_rr = [0]

        def evac_eng():
            evac_rr[0] += 1
            return "act" if evac_rr[0] % 2 else "dve"

        def copy_op(eng, out, in_, reads, writes):
            if eng == "act":
                sc.op("act", lambda e: e.activation(out=out, in_=in_, func=AF.Copy), reads=reads, writes=writes)
            elif eng == "dve":
                sc.op("dve", lambda e: e.tensor_copy(out=out, in_=in_), reads=reads, writes=writes)
            else:
                sc.op("pool", lambda e: e.tensor_copy(out=out, in_=in_), reads=reads, writes=writes)

        def mm_group(bi, W, pairs, reads, M=128, c0=0):
            n = len(pairs)
            for i, (l, r) in enumerate(pairs):
                sc.op("pe", (lambda e, l=l, r=r, i=i: e.matmul(bank(bi)[0:M, c0:c0 + W], l, r, start=(i == 0), stop=(i == n - 1))),
                      reads=reads, writes=[PSB[bi]])

        def stat_rstd(sq_chunks, reads, W, dim, rstd_ap, B_rstd, mean_out=None):
            bi = next_bank()
            mm_group(bi, W, [(ones[:, :], s) for s in sq_chunks], reads + [B_ones])
            sc.op("act", lambda e: e.activation(out=rstd_ap, in_=bank(bi)[:, 0:W], func=AF.Ln, bias=eps_col[:, 0:1], scale=1.0 / dim),
                  reads=[PSB[bi], B_ones], writes=[B_rstd])
            sc.op("act", lambda e: e.activation(out=rstd_ap, in_=rstd_ap, func=AF.Exp, scale=-0.5), reads=[B_rstd], writes=[B_rstd])

        def rms_apply(nch, src, B_src, gname, gidx0, rstd_ap, B_rstd, dst, B_dst, W, engs=("dve",)):
            for c in range(nch):
                eng = engs[c % len(engs)]
                sc.op(eng, (lambda e, c=c: e.scalar_tensor_tensor(out=dst(c), in0=src(c), scalar=ccol(gname, gidx0 + c),
                                                                   in1=rstd_ap, op0=ALU.mult, op1=ALU.mult)),
                      reads=[B_src, B_rstd, B_cp], writes=[B_dst])

        with ExitStack() as ab:
            qnT = sb(ab, "qnT", [128, 3, OWNH], BF16)
            B_qn = sc.buf("qn")
            cosT = sb(ab, "cosT", [128, OWNH], BF16)
            sinT = sb(ab, "sinT", [128, OWNH], BF16)
            B_tab = sc.buf("tab")
            ckvT = sb(ab, "ckvT", [128, 2, S], BF16)
            B_ckv = sc.buf("ckv")
            Kt = sb(ab, "Kt", [128, S], BF16)
            B_Kpe = sc.buf("Kpe")
            B_Kn = sc.buf("Kn")
            with ExitStack() as pa:
                xt = [sb(pa, "xtA", [128, 4, 512], F32), sb(pa, "xtB", [128, 4, 512], F32)]
                B_xt = [sc.buf("xtA"), sc.buf("xtB")]
                sqa = sb(pa, "sqa", [128, 8, 512], BF16)
                B_sq = sc.buf("sq")
                hbuf = [sb(pa, "hbuf%d" % i, [128, 8, 512], BF16) for i in range(2)]
                B_hs = [sc.buf("h%d" % i) for i in range(2)]
                rstd = sb(pa, "rstd", [128, 512], F32)
                B_rstd = sc.buf("rstd")
                winA = sb(pa, "winA", [128, 7, 8, 128], BF16)
                B_winA = sc.buf("winA")
                lat = sb(pa, "lat", [128, 3, 512], F32)
                B_lat = sc.buf("lat")
                sq2 = sb(pa, "sq2", [128, 3, 512], BF16)
                B_sq2 = sc.buf("sq2")
                rstd2 = sb(pa, "rstd2", [128, 512], F32)
                B_rstd2 = sc.buf("rstd2")
                posi = sb(pa, "posi", [128, 512], I32)
                B_pos = sc.buf("pos")
                ang = sb(pa, "ang", [128, 512], F32)
                B_ang = sc.buf("ang")
                cts = [sb(pa, "ct%d" % i, [128, 512], F32) for i in range(2)]
                sts = [sb(pa, "st%d" % i, [128, 512], F32) for i in range(2)]
                B_css = [sc.buf("cs%d" % i) for i in range(2)]
                t1 = sb(pa, "t1", [128, 512], F32)
                t2 = sb(pa, "t2", [128, 512], F32)
                B_t12 = sc.buf("t12")
                k1, k2, B_k12 = lat[:, 0, :], lat[:, 1, :], B_lat
                o, n = woff["w_inA"]
                sc.op("sp", lambda e, o=o, n=n: e.dma_start(out=winA[:, :, :, :].rearrange("p m k j -> p (m k j)"), in_=wbf[:, o:o + n]),
                      reads=[B_wbf["w_inA"]], writes=[B_winA], dma_key="winA")

                R = slice(64, 96)
                PI = float(np.pi)
                MAGIC = 12582912.0
                C1 = 6.28125
                C2 = float(2 * np.pi - 6.28125)
                all_tiles = own_tiles + oth_tiles
                n_own = len(own_tiles)

                def gen_N(ti):
                    T0, W = all_tiles[ti]
                    hb, B_hb = hbuf[ti % 2], B_hs[ti % 2]
                    for hf in range(2):
                        sc.op("sp", (lambda e, hf=hf: e.dma_start(out=xt[hf][:, :, 0:W], in_=xT[:, 4 * hf:4 * hf + 4, T0:T0 + W])),
                              writes=[B_xt[hf]], dma_key="xt%d" % hf)
                    for hf in range(2):
                        sc.op("act", (lambda e, hf=hf: e.activation(out=sqa[:, 4 * hf:4 * hf + 4, 0:W], in_=xt[hf][:, :, 0:W], func=AF.Square)),
                              reads=[B_xt[hf]], writes=[B_sq])
                    yield
                    stat_rstd([sqa[:, c, 0:W] for c in range(8)], [B_sq], W, float(D), rstd[:, 0:W], B_rstd)
                    yield
                    for c in range(8):
                        sc.op("dve", (lambda e, c=c: e.scalar_tensor_tensor(out=hb[:, c, 0:W], in0=xt[c // 4][:, c % 4, 0:W],
                                                                             scalar=ccol("gains", c), in1=rstd[:, 0:W],
                                                                             op0=ALU.mult, op1=ALU.mult)),
                              reads=[B_xt[c // 4], B_rstd, B_cp], writes=[B_hb])
                    yield

                def gen_R(ti):
                    T0, W = all_tiles[ti]
                    ct, st, B_cs = cts[ti % 2], sts[ti % 2], B_css[ti % 2]
                    sc.op("sp", (lambda e: e.dma_start(out=posi[64:96, 0:W], in_=posd[0:1, T0:T0 + W].broadcast_to([32, W]))),
                          writes=[B_pos], dma_key="pos")
                    sc.op("dve", (lambda e: e.tensor_copy(out=ang[R, 0:W], in_=posi[R, 0:W])), reads=[B_pos], writes=[B_ang])
                    sc.op("dve", (lambda e: e.tensor_scalar(out=ang[R, 0:W], in0=ang[R, 0:W], scalar1=ccol("invf", 0, 64, 96), scalar2=None,
                                                            op0=ALU.mult)), reads=[B_ang, B_cp], writes=[B_ang])
                    sc.op("dve", (lambda e: e.tensor_scalar(out=t1[R, 0:W], in0=ang[R, 0:W], scalar1=float(1.0 / (2 * np.pi)), scalar2=MAGIC,
                                                            op0=ALU.mult, op1=ALU.add)), reads=[B_ang], writes=[B_t12])
                    sc.op("dve", (lambda e: e.tensor_scalar(out=t1[R, 0:W], in0=t1[R, 0:W], scalar1=-MAGIC, scalar2=None, op0=ALU.add)),
                          reads=[B_t12], writes=[B_t12])
                    yield
                    sc.op("dve", (lambda e: e.scalar_tensor_tensor(out=t2[R, 0:W], in0=t1[R, 0:W], scalar=-C1, in1=ang[R, 0:W], op0=ALU.mult, op1=ALU.add)),
                          reads=[B_t12, B_ang], writes=[B_t12])
                    sc.op("dve", (lambda e: e.scalar_tensor_tensor(out=t2[R, 0:W], in0=t1[R, 0:W], scalar=-C2, in1=t2[R, 0:W], op0=ALU.mult, op1=ALU.add)),
                          reads=[B_t12], writes=[B_t12])
                    sc.op("dve", (lambda e: e.tensor_scalar(out=t2[R, 0:W], in0=t2[R, 0:W], scalar1=-PI, scalar2=PI, op0=ALU.max, op1=ALU.min)),
                          reads=[B_t12], writes=[B_t12])
                    sc.op("act", (lambda e: e.activation(out=st[R, 0:W], in_=t2[R, 0:W], func=AF.Sin, scale=ccol("sgn", 0, 64, 96))),
                          reads=[B_t12, B_cp], writes=[B_cs])
                    sc.op("act", (lambda e: e.activation(out=ct[R, 0:W], in_=t2[R, 0:W], func=AF.Sin, scale=0.5)),
                          reads=[B_t12], writes=[B_cs])
                    yield
                    sc.op("dve", (lambda e: e.tensor_tensor(out=ct[R, 0:W], in0=ct[R, 0:W], in1=ct[R, 0:W], op=ALU.mult)), reads=[B_cs], writes=[B_cs])
                    sc.op("dve", (lambda e: e.tensor_scalar(out=ct[R, 0:W], in0=ct[R, 0:W], scalar1=-2.0, scalar2=1.0, op0=ALU.mult, op1=ALU.add)),
                          reads=[B_cs], writes=[B_cs])
                    if ti < n_own:
                        sc.op("act", (lambda e: e.activation(out=cosT[R, T0:T0 + W], in_=ct[R, 0:W], func=AF.Copy)), reads=[B_cs], writes=[B_tab])
                        sc.op("act", (lambda e: e.activation(out=sinT[R, T0:T0 + W], in_=st[R, 0:W], func=AF.Copy)), reads=[B_cs], writes=[B_tab])
                    yield

                def gen_P(ti):
                    T0, W = all_tiles[ti]
                    hb, B_hb = hbuf[ti % 2], B_hs[ti % 2]
                    ct, st, B_cs = cts[ti % 2], sts[ti % 2], B_css[ti % 2]
                    hT = lambda c: hb[:, c, 0:W]
                    groups = [("kv", [3, 4], "kvg", 256.0)]
                    if ti < n_own:
                        groups.append(("q", [0, 1, 2], "qg", 384.0))
                    for (gname, chunks, gn, dim) in groups:
                        for j, m in enumerate(chunks):
                            bi = next_bank()
                            mm_group(bi, W, [(winA[:, m, k, :], hT(k)) for k in range(8)], [B_hb, B_winA])
                            copy_op(evac_eng(), lat[:, j, 0:W], bank(bi)[:, 0:W], [PSB[bi]], [B_lat])
                        nchk = len(chunks)
                        sc.op("act", (lambda e, nchk=nchk: e.activation(out=sq2[:, 0:nchk, 0:W], in_=lat[:, 0:nchk, 0:W], func=AF.Square)),
                              reads=[B_lat], writes=[B_sq2])
                        yield
                        stat_rstd([sq2[:, j, 0:W] for j in range(nchk)], [B_sq2], W, dim, rstd2[:, 0:W], B_rstd2)
                        yield
                        if gname == "kv":
                            dst = lambda c: ckvT[:, c, T0:T0 + W]
                            B_dst = B_ckv
                        else:
                            dst = lambda c: qnT[:, c, T0:T0 + W]
                            B_dst = B_qn
                        rms_apply(nchk, (lambda c: lat[:, c, 0:W]), B_lat, gn, 0, rstd2[:, 0:W], B_rstd2, dst, B_dst, W)
                        yield
                    b1 = next_bank()
                    mm_group(b1, W, [(winA[:, 5, k, :], hT(k)) for k in range(8)], [B_hb, B_winA])
                    b2 = next_bank()
                    mm_group(b2, W, [(winA[:, 6, k, :], hT(k)) for k in range(8)], [B_hb, B_winA])
                    sc.op("dve", (lambda e: e.tensor_tensor(out=k1[R, 0:W], in0=bank(b1)[R, 0:W], in1=ct[R, 0:W], op=ALU.mult)),
                          reads=[PSB[b1], B_cs], writes=[B_k12])
                    sc.op("dve", (lambda e: e.tensor_tensor(out=k2[R, 0:W], in0=bank(b2)[R, 0:W], in1=st[R, 0:W], op=ALU.mult)),
                          reads=[PSB[b2], B_cs], writes=[B_k12])
                    sc.op("dve", (lambda e: e.tensor_tensor(out=Kt[R, T0:T0 + W], in0=k1[R, 0:W], in1=k2[R, 0:W], op=ALU.add)),
                          reads=[B_k12], writes=[B_Kpe])
                    yield

                def drain(g):
                    for _ in g:
                        pass

                def chain(*gens):
                    for g in gens:
                        yield from g

                ntile = len(all_tiles)
                drain(gen_N(0))
                drain(gen_R(0))
                for ti in range(ntile):
                    gp = gen_P(ti)
                    gn = chain(gen_N(ti + 1), gen_R(ti + 1)) if ti + 1 < ntile else None
                    while gp is not None or gn is not None:
                        if gn is not None:
                            try:
                                next(gn)
                            except StopIteration:
                                gn = None
                        if gp is not None:
                            try:
                                next(gp)
                            except StopIteration:
                                gp = None
                sc.flush()
            with ExitStack() as pb:
                wuq = sb(pb, "wuq", [128, 8, 2, 3, 96], BF16)
                B_wuq = sc.buf("wuq")
                wukv = sb(pb, "wukv", [128, 2, 1024], BF16)
                B_wukv = sc.buf("wukv")
                o, n = woff["w_uq"]
                sc.op("sp", lambda e, o=o, n=n: e.dma_start(out=wuq[:, :, :, :, :].rearrange("p h v k j -> p (h v k j)"), in_=wbf[:, o:o + n]),
                      reads=[B_wbf["w_uq"]], writes=[B_wuq], dma_key="wuq", extra_dma={"cv_w_uq": sc.dsems["cv_w_uq"][1]})
                o, n = woff["w_ukv"]
                sc.op("sp", lambda e, o=o, n=n: e.dma_start(out=wukv[:, :, :].rearrange("p k j -> p (k j)"), in_=wbf[:, o:o + n]),
                      reads=[B_wbf["w_ukv"]], writes=[B_wukv], dma_key="wukv", extra_dma={"cv_w_ukv": sc.dsems["cv_w_ukv"][1]})

                Vt = sb(pb, "Vt", [128, NKT, 128], BF16)
                B_V = sc.buf("V")
                Pb = [sb(pb, "Pb%d" % i, [128, 3, 512], BF16) for i in range(3)]
                B_P = [sc.buf("P%d" % i) for i in range(3)]
                Kt2 = sb(pb, "Kt2", [128, S], BF16)
                Ks = [Kt, Kt2]
                B_Kns = [B_Kn, sc.buf("Kn2")]
                B_Kpes = [B_Kpe, sc.buf("Kpe2")]
                Qs = [sb(pb, "Qh%d" % i, [128, OWNH], BF16) for i in range(2)]
                B_Qs = [sc.buf("Q%d" % i) for i in range(2)]
                sqt = sb(pb, "sqt", [128, 512], BF16)
                B_sqt = sc.buf("sqt")
                mxs = [sb(pb, "mx%d" % i, [128, 64], F32) for i in range(2)]
                B_mxs = [sc.buf("mx%d" % i) for i in range(2)]
                negms = [sb(pb, "negm%d" % i, [128, 2], F32) for i in range(2)]
                B_negms = [sc.buf("negm%d" % i) for i in range(2)]
                rec = sb(pb, "rec", [128, 512], F32)
                B_rec = sc.buf("rec")
                qtmp = sb(pb, "qtmp", [128, 2, 512], F32)
                B_qtmp = sc.buf("qtmp")
                sc.op("pool", lambda e: e.memset(Vt[:, :, 64:128], 1.0), writes=[B_V])
                sc.op("pool", lambda e: e.tensor_copy(out=Kt2[R, :], in_=Kt[R, :]), reads=[B_Kpe], writes=[B_Kpes[1]])
                kblocks = split_tiles(S, 512)
                PB = 7
                OB = 6

                def prologue(h):
                    Kc, B_Kn_c, B_Kpe_c = Ks[h % 2], B_Kns[h % 2], B_Kpes[h % 2]
                    Qc, B_Qc = Qs[h % 2], B_Qs[h % 2]
                    mx, B_mx = mxs[h % 2], B_mxs[h % 2]
                    negm, B_negm = negms[h % 2], B_negms[h % 2]
                    for (k0, kw) in kblocks:
                        mm_group(PB, kw, [(wukv[:, k, h * 128:h * 128 + 64], ckvT[:, k, k0:k0 + kw]) for k in range(2)], [B_wukv, B_ckv], M=64)
                        copy_op("dve", Kc[0:64, k0:k0 + kw], bank(PB)[0:64, 0:kw], [PSB[PB]], [B_Kn_c])
                        yield
                    for (T0, W) in own_tiles:
                        mm_group(PB, W, [(wuq[:, h, 0, k, :], qnT[:, k, T0:T0 + W]) for k in range(3)], [B_wuq, B_qn], M=96)
                        copy_op("dve", Qc[0:64, T0:T0 + W], bank(PB)[0:64, 0:W], [PSB[PB]], [B_Qc])
                        sc.op("dve", (lambda e, T0=T0, W=W: e.tensor_tensor(out=qtmp[R, 0, 0:W], in0=bank(PB)[R, 0:W], in1=cosT[R, T0:T0 + W], op=ALU.mult)),
                              reads=[PSB[PB], B_tab], writes=[B_qtmp])
                        yield
                        mm_group(PB, W, [(wuq[:, h, 1, k, :], qnT[:, k, T0:T0 + W]) for k in range(3)], [B_wuq, B_qn], M=96)
                        sc.op("dve", (lambda e, T0=T0, W=W: e.tensor_tensor(out=qtmp[R, 1, 0:W], in0=bank(PB)[R, 0:W], in1=sinT[R, T0:T0 + W], op=ALU.mult)),
                              reads=[PSB[PB], B_tab], writes=[B_qtmp])
                        sc.op("pool", (lambda e, T0=T0, W=W: e.tensor_tensor(out=Qc[R, T0:T0 + W], in0=qtmp[R, 0, 0:W], in1=qtmp[R, 1, 0:W], op=ALU.add)),
                              reads=[B_qtmp], writes=[B_Qc])
                        yield
                    nq = len(own_tiles)
                    nk = len(kblocks)
                    col = 0
                    for (src_t, B_src, tiles) in ((Qc, [B_Qc], own_tiles), (Kc, [B_Kn_c, B_Kpe_c], kblocks)):
                        for (T0, W) in tiles:
                            sc.op("pool", (lambda e, src_t=src_t, T0=T0, W=W: e.tensor_tensor(out=sqt[0:96, 0:W], in0=src_t[0:96, T0:T0 + W], in1=src_t[0:96, T0:T0 + W], op=ALU.mult)),
                                  reads=B_src, writes=[B_sqt])
                            mm_group(PB, W, [(ones[0:96, :], sqt[0:96, 0:W])], [B_sqt, B_ones])
                            sc.op("dve", (lambda e, W=W, col=col: e.tensor_reduce(out=mx[:, col:col + 1], in_=bank(PB)[:, 0:W], axis=AX.X, op=ALU.max)),
                                  reads=[PSB[PB]], writes=[B_mx])
                            col += 1
                            yield
                    sc.op("dve", lambda e: e.tensor_reduce(out=negm[:, 0:1], in_=mx[:, 0:nq], axis=AX.X, op=ALU.max), reads=[B_mx], writes=[B_negm])
                    sc.op("dve", lambda e: e.tensor_reduce(out=negm[:, 1:2], in_=mx[:, nq:nq + nk], axis=AX.X, op=ALU.max), reads=[B_mx], writes=[B_negm])
                    sc.op("dve", lambda e: e.tensor_tensor(out=negm[:, 0:1], in0=negm[:, 0:1], in1=negm[:, 1:2], op=ALU.mult), reads=[B_negm], writes=[B_negm])
                    sc.op("act", lambda e: e.activation(out=negm[:, 0:1], in_=negm[:, 0:1], func=AF.Ln), reads=[B_negm], writes=[B_negm])
                    sc.op("act", lambda e: e.activation(out=negm[:, 0:1], in_=negm[:, 0:1], func=AF.Exp, scale=0.5), reads=[B_negm], writes=[B_negm])
                    sc.op("dve", lambda e: e.tensor_scalar(out=negm[:, 0:1], in0=negm[:, 0:1], scalar1=-scale, scalar2=None, op0=ALU.mult),
                          reads=[B_negm], writes=[B_negm])
                    yield

                def vgen(h):
                    for g in range(NKT // 8):
                        bi = PB if g % 2 == 0 else OB
                        for j in range(8):
                            kt = g * 8 + j
                            mm_group(bi, 64, [(ckvT[:, k, kt * 128:(kt + 1) * 128], wukv[:, k, h * 128 + 64:h * 128 + 128]) for k in range(2)],
                                     [B_wukv, B_ckv], c0=j * 64)
                        srcv = bank(bi)[:, 0:512].rearrange("p (j d) -> p j d", d=64)
                        copy_op("dve" if g % 2 == 0 else "act", Vt[:, g * 8:g * 8 + 8, 0:64], srcv, [PSB[bi]], [B_V])

                def attention(h):
                    Kc, B_Kn_c, B_Kpe_c = Ks[h % 2], B_Kns[h % 2], B_Kpes[h % 2]
                    Qc, B_Qc = Qs[h % 2], B_Qs[h % 2]
                    negm, B_negm = negms[h % 2], B_negms[h % 2]
                    groups = []
                    kt0 = 0
                    while kt0 < NKT:
                        n = min(3, NKT - kt0)
                        groups.append((kt0, n))
                        kt0 += n
                    NG = len(groups)
                    for qi, (T0, W) in enumerate(own_tiles):
                        def s_group(g, T0=T0, W=W):
                            sl = g % 2
                            kt0, n = groups[g]
                            for j in range(n):
                                kt = kt0 + j
                                sc.op("pe", (lambda e, sl=sl, j=j, kt=kt: e.matmul(bank(3 * sl + j)[:, 0:W], Kc[0:96, kt * 128:(kt + 1) * 128],
                                                                                      Qc[0:96, T0:T0 + W], start=True, stop=True)),
                                      reads=[B_Kn_c, B_Kpe_c, B_Qc], writes=[PSB[3 * sl + j]])
                            srcp = psum[:, 3 * sl * 512:(3 * sl + n) * 512].rearrange("p (j w) -> p j w", w=512)[:, :, 0:W]
                            pi = g % 3
                            sc.op("act", (lambda e, pi=pi, srcp=srcp, n=n: e.activation(out=Pb[pi][:, 0:n, 0:W], in_=srcp, func=AF.Exp, bias=negm[:, 0:1], scale=scale)),
                                  reads=[PSB[3 * sl + j] for j in range(n)] + [B_negm], writes=[B_P[pi]])

                        def pv_group(g, T0=T0, W=W):
                            pi = g % 3
                            kt0, n = groups[g]
                            for j in range(n):
                                kt = kt0 + j
                                sc.op("pe", (lambda e, pi=pi, j=j, kt=kt: e.matmul(bank(OB)[:, 0:W], Vt[:, kt, :], Pb[pi][:, j, 0:W],
                                                                                      start=(kt == 0), stop=(kt == NKT - 1))),
                                      reads=[B_V, B_P[pi]], writes=[PSB[OB]])
                        s_group(0)
                        yield
                        if NG > 1:
                            s_group(1)
                            yield
                        for g in range(NG):
                            if g + 2 < NG:
                                s_group(g + 2)
                            pv_group(g)
                            yield
                        sc.op("dve", (lambda e, W=W: e.reciprocal(out=rec[64:128, 0:W], in_=bank(OB)[64:128, 0:W])), reads=[PSB[OB]], writes=[B_rec])
                        p0 = (h % 2) * 64
                        sc.op("dve", (lambda e, T0=T0, W=W, p0=p0: e.tensor_tensor(out=ybT[p0:p0 + 64, h // 2, T0:T0 + W], in0=bank(OB)[0:64, 0:W],
                                                                                  in1=rec[64:128, 0:W], op=ALU.mult)),
                              reads=[PSB[OB], B_rec], writes=[B_yb])

                for _ in prologue(0):
                    pass
                for h in range(NH):
                    vgen(h)
                    ga = attention(h)
                    gp = prologue(h + 1) if h + 1 < NH else None
                    cnt = 0
                    for _ in ga:
                        cnt += 1
                        if gp is not None and cnt % 3 == 0:
                            try:
                                next(gp)
                            except StopIteration:
                                gp = None
                    if gp is not None:
                        for _ in gp:
                            pass
                sc.flush()

        with ExitStack() as pc:
            XW = 16 + TW + 1
            xres = [sb(pc, "xres%d" % i, [128, 8, XW], F32) for i in range(2)]
            B_xres = [sc.buf("xres%d" % i) for i in range(2)]
            NWB = 3
            wbuf = [sb(pc, "wbuf%d" % i, [128, 4096], BF16) for i in range(NWB)]
            B_wbuf = [sc.buf("wbuf%d" % i) for i in range(NWB)]
            yaT = sb(pc, "yaT", [128, 4, TW], BF16)
            B_ya = sc.buf("ya")
            gbt = sb(pc, "gbt", [128, 4, 1 + TW], BF16)
            B_gb = sc.buf("gb")
            zt = sb(pc, "zt", [128, 4, 2 + TW], BF16)
            B_z = sc.buf("z")
            acc = [sb(pc, "acc%d" % i, [128, TW], F32) for i in range(2)]
            B_acc = [sc.buf("acc%d" % i) for i in range(2)]
            ubuf = [sb(pc, "ubuf%d" % i, [128, 8, 31 + TW], BF16) for i in range(2)]
            B_u = [sc.buf("u%d" % i) for i in range(2)]
            csq = sb(pc, "csq", [128, 8, TW], BF16)
            B_csq = sc.buf("csq")
            mean = sb(pc, "mean", [128, TW], F32)
            B_mean = sc.buf("mean")

            class Ctx:
                pass
            ctxs = []
            for nm in ("S", "T"):
                cx = Ctx()
                cx.sqh = sb(pc, "sqh" + nm, [128, 8, TW + 1], BF16)
                cx.B_sq = sc.buf("sq" + nm)
                cx.B_h = cx.B_sq
                cx.rstd = sb(pc, "rstd" + nm, [128, TW + 1], F32)
                cx.B_rstd = sc.buf("rstd" + nm)
                cx.mT = sb(pc, "mT" + nm, [128, 8, TW + 1], F32)
                cx.B_m = sc.buf("m" + nm)
                cx.hid = sb(pc, "hid" + nm, [128, 16, TW], BF16)
                cx.B_hid = sc.buf("hid" + nm)
                ctxs.append(cx)
            CS, CT = ctxs

            class WStream:
                def __init__(self):
                    self.n = 0

                def get(self, name, blk, nblk_elems):
                    i = self.n % NWB
                    self.n += 1
                    o = woff[name][0] + blk * nblk_elems
                    sc.op("sp", (lambda e, i=i, o=o, n=nblk_elems: e.dma_start(out=wbuf[i][:, 0:n], in_=wbf[:, o:o + n])),
                          writes=[B_wbuf[i]], dma_key="wb%d" % i, extra_dma={"cv_" + name: sc.dsems["cv_" + name][1]})
                    return wbuf[i], B_wbuf[i]
            ws = WStream()

            def dense(wname, nk, nm, rhs_fn, rhs_bufs, W, consume, blk0=0, per=None):
                if per is None:
                    per = 4096 // (nk * 128)
                for b0 in range(0, nm, per):
                    nb = min(per, nm - b0)
                    wt, B_wt = ws.get(wname, blk0 + b0 // per, per * nk * 128)
                    for j in range(nb):
                        m = b0 + j
                        bi = next_bank()
                        mm_group(bi, W, [(wt[:, (j * nk + k) * 128:(j * nk + k + 1) * 128], rhs_fn(k)) for k in range(nk)], rhs_bufs + [B_wt])
                        consume(m, bi)
                    yield (nb * nk * 0.2, 0.0)

            def rmsnorm_pre(cx, xsrc, B_x, W, gidx):
                sqh, rstd = cx.sqh, cx.rstd
                for hf in range(2):
                    sc.op("act", (lambda e, hf=hf: e.activation(out=sqh[:, 4 * hf:4 * hf + 4, 0:W], in_=xsrc(slice(4 * hf, 4 * hf + 4)), func=AF.Square)),
                          reads=[B_x, cx.B_h], writes=[cx.B_sq])
                yield (0.0, 6.0)
                stat_rstd([sqh[:, c, 0:W] for c in range(8)], [cx.B_sq], W, float(D), rstd[:, 0:W], cx.B_rstd)
                for c in range(8):
                    sc.op("dve", (lambda e, c=c: e.scalar_tensor_tensor(out=sqh[:, c, 0:W], in0=xsrc(c), scalar=ccol("gains", gidx * 8 + c), in1=rstd[:, 0:W],
                                                                         op0=ALU.mult, op1=ALU.mult)),
                          reads=[B_x, cx.B_rstd, B_cp, cx.B_sq], writes=[cx.B_h])
                yield (1.6, 11.0)

            def post_norm_residual(cx, xdst, B_x, W, gidx):
                sqh, rstd, mT = cx.sqh, cx.rstd, cx.mT
                for hf in range(2):
                    sc.op("act", (lambda e, hf=hf: e.activation(out=sqh[:, 4 * hf:4 * hf + 4, 0:W], in_=mT[:, 4 * hf:4 * hf + 4, 0:W], func=AF.Square)),
                          reads=[cx.B_m, cx.B_h], writes=[cx.B_sq])
                yield (0.0, 6.0)
                stat_rstd([sqh[:, c, 0:W] for c in range(8)], [cx.B_sq], W, float(D), rstd[:, 0:W], cx.B_rstd)
                for c in range(8):
                    sc.op("dve", (lambda e, c=c: e.scalar_tensor_tensor(out=mT[:, c, 0:W], in0=mT[:, c, 0:W], scalar=ccol("gains", gidx * 8 + c), in1=rstd[:, 0:W],
                                                                         op0=ALU.mult, op1=ALU.mult)),
                          reads=[cx.B_m, cx.B_rstd, B_cp], writes=[cx.B_m])
                    aeng = "pool" if c % 3 != 2 else "dve"
                    sc.op(aeng, (lambda e, c=c: e.tensor_tensor(out=xdst(c), in0=xdst(c), in1=mT[:, c, 0:W], op=ALU.add)),
                          reads=[cx.B_m, B_x], writes=[B_x])
                yield (1.6, 14.0)

            def mlp_half(cx, W, layer, half):
                sqh, mT, hid = cx.sqh, cx.mT, cx.hid

                def cons1(m, bi):
                    sc.op("act", (lambda e: e.activation(out=hid[:, m, 0:W], in_=bank(bi)[:, 0:W], func=AF.Relu)), reads=[PSB[bi]], writes=[cx.B_hid])
                    sc.op("pool", (lambda e: e.tensor_tensor(out=hid[:, m, 0:W], in0=hid[:, m, 0:W], in1=hid[:, m, 0:W], op=ALU.mult)),
                          reads=[cx.B_hid], writes=[cx.B_hid])
                yield from dense("m%dw1" % layer, 8, 16, (lambda k: sqh[:, k, 0:W]), [cx.B_h], W, cons1, blk0=half * 4)

                def cons2(m, bi):
                    if half == 0:
                        copy_op(evac_eng(), mT[:, m, 0:W], bank(bi)[:, 0:W], [PSB[bi]], [cx.B_m])
                    else:
                        sc.op("dve", (lambda e: e.tensor_tensor(out=mT[:, m, 0:W], in0=bank(bi)[:, 0:W], in1=mT[:, m, 0:W], op=ALU.add)),
                              reads=[PSB[bi], cx.B_m], writes=[cx.B_m])
                yield from dense("m%dw2" % layer, 16, 8, (lambda k: hid[:, k, 0:W]), [cx.B_hid], W, cons2, blk0=half * 4)

            for c in range(2):
                sc.op("pool", lambda e, c=c: e.memset(ubuf[c][:, :, 0:31], 0.0), writes=[B_u[c]])
            sc.op("pool", lambda e: e.memset(zt[:, :, 0:1], 0.0), writes=[B_z])
            sc.op("pool", lambda e: e.memset(xres[0][:, :, 0:16], 0.0), writes=[B_xres[0]])

            ntl = len(own_tiles)

            def pipe_L0(ti):
                cx = CS
                T0, W = own_tiles[ti]
                xr = xres[ti % 2]
                B_xr = B_xres[ti % 2]
                ub = ubuf[ti % 2]
                B_ub = B_u[ti % 2]
                W1 = W + 1
                sqh, mT = cx.sqh, cx.mT
                x0 = lambda c: xr[:, c, 16:16 + W]
                x01 = lambda c: xr[:, c, 16:16 + W1]
                for hf in range(2):
                    sc.op("sp", (lambda e, hf=hf: e.dma_start(out=xr[:, 4 * hf:4 * hf + 4, 16:16 + W1], in_=xT[:, 4 * hf:4 * hf + 4, T0:T0 + W1])),
                          writes=[B_xr], dma_key="xr%d" % (ti % 2))
                yield (0.0, 12.0)
                yield from rmsnorm_pre(cx, x01, B_xr, W1, 0)
                gcs = mT

                def cons_g(m, bi):
                    grp, c = m // 4, m % 4
                    if grp == 0:
                        copy_op(evac_eng(), gbt[:, c, 0:W1], bank(bi)[:, 0:W1], [PSB[bi]], [B_gb])
                    elif grp == 1:
                        copy_op("act", gcs[:, c, 0:W1], bank(bi)[:, 0:W1], [PSB[bi]], [cx.B_m])
                    else:
                        sc.op("dve", (lambda e: e.tensor_tensor(out=zt[:, c, 1:1 + W1], in0=bank(bi)[:, 0:W1], in1=gcs[:, c, 0:W1], op=ALU.mult)),
                              reads=[PSB[bi], cx.B_m], writes=[B_z])
                yield from dense("w_inG", 8, 12, (lambda k: sqh[:, k, 0:W1]), [cx.B_h], W1, cons_g)
                for c in range(4):
                    a = acc[c % 2]
                    B_a = B_acc[c % 2]
                    sc.op("dve", (lambda e, c=c, a=a: e.tensor_scalar(out=a[:, 0:W], in0=zt[:, c, 0:W], scalar1=ccol("sck", c * 3 + 0), scalar2=None, op0=ALU.mult)),
                          reads=[B_z, B_cp], writes=[B_a])
                    for k in (1, 2):
                        sc.op("dve", (lambda e, c=c, a=a, k=k: e.scalar_tensor_tensor(out=a[:, 0:W], in0=zt[:, c, k:k + W], scalar=ccol("sck", c * 3 + k),
                                                                                      in1=a[:, 0:W], op0=ALU.mult, op1=ALU.add)),
                              reads=[B_z, B_cp, B_a], writes=[B_a])
                    sc.op("pool", (lambda e, c=c, a=a: e.tensor_tensor(out=yaT[:, c, 0:W], in0=a[:, 0:W], in1=gbt[:, c, 0:W], op=ALU.mult)),
                          reads=[B_a, B_gb], writes=[B_ya])
                sc.op("pool", (lambda e: e.tensor_copy(out=zt[:, :, 0:1], in_=zt[:, :, W:W + 1])), reads=[B_z], writes=[B_z])
                yield (0.0, 9.0)

                def rhs_mix(k):
                    return yaT[:, k, 0:W] if k < 4 else ybT[:, k - 4, T0:T0 + W]

                def cons_m(m, bi):
                    copy_op(evac_eng(), mT[:, m, 0:W], bank(bi)[:, 0:W], [PSB[bi]], [cx.B_m])
                yield from dense("w_out", 8, 8, rhs_mix, [B_ya, B_yb], W, cons_m)
                yield from post_norm_residual(cx, x0, B_xr, W, 1)
                yield from rmsnorm_pre(cx, x0, B_xr, W, 2)
                yield from mlp_half(cx, W, 0, 0)
                yield from mlp_half(cx, W, 0, 1)
                yield from post_norm_residual(cx, x0, B_xr, W, 3)
                yield from rmsnorm_pre(cx, x0, B_xr, W, 4)

                abank = [None]

                def cons_p(m, bi):
                    c, isg = m // 2, m % 2
                    if isg == 0:
                        abank[0] = bi
                    else:
                        ba = abank[0]
                        g = acc[c % 2]
                        B_g = B_acc[c % 2]
                        sc.op("act", (lambda e: e.activation(out=g[:, 0:W], in_=bank(bi)[:, 0:W], func=AF.Sigmoid, bias=ccol("bpw1", 8 + c), scale=1.0)),
                              reads=[PSB[bi], B_cp], writes=[B_g])
                        sc.op("dve", (lambda e: e.scalar_tensor_tensor(out=ub[:, c, 31:31 + W], in0=bank(ba)[:, 0:W], scalar=ccol("bpw1", c), in1=g[:, 0:W],
                                                                       op0=ALU.add, op1=ALU.mult)),
                              reads=[PSB[ba], B_g, B_cp], writes=[B_ub])
                yield from dense("pw1", 8, 16, (lambda k: sqh[:, k, 0:W]), [cx.B_h], W, cons_p)
                yield (0.0, 3.0)

            def pipe_L1(ti):
                cx = CT
                T0, W = own_tiles[ti]
                xr = xres[ti % 2]
                B_xr = B_xres[ti % 2]
                ub = ubuf[ti % 2]
                B_ub = B_u[ti % 2]
                sqh, mT, rstd = cx.sqh, cx.mT, cx.rstd
                x1 = lambda c: xr[:, c, 0:W]
                cacc = mT
                for c in range(8):
                    wt, B_wt = ws.get("dwd", c, 31 * 128)
                    bi = next_bank()
                    mm_group(bi, W, [(wt[:, k * 128:(k + 1) * 128], ub[:, c, k:k + W]) for k in range(31)], [B_ub, B_wt])
                    sc.op("dve", (lambda e, c=c, bi=bi: e.tensor_scalar(out=cacc[:, c, 0:W], in0=bank(bi)[:, 0:W], scalar1=ccol("bdw", c), scalar2=None, op0=ALU.add)),
                          reads=[PSB[bi], B_cp], writes=[cx.B_m])
                    sc.op("act", (lambda e, c=c: e.activation(out=sqh[:, c, 0:W], in_=cacc[:, c, 0:W], func=AF.Copy)), reads=[cx.B_m, cx.B_h], writes=[cx.B_sq])
                    sc.op("act", (lambda e, c=c: e.activation(out=csq[:, c, 0:W], in_=cacc[:, c, 0:W], func=AF.Square)), reads=[cx.B_m], writes=[B_csq])
                    yield (6.2, 1.0)
                yield (0.0, 3.0)
                b_mean = next_bank()
                mm_group(b_mean, W, [(ones[:, :], sqh[:, c, 0:W]) for c in range(8)], [cx.B_sq, B_ones])
                b_ex2 = next_bank()
                mm_group(b_ex2, W, [(ones[:, :], csq[:, c, 0:W]) for c in range(8)], [B_csq, B_ones])
                sc.op("dve", (lambda e: e.tensor_scalar(out=mean[:, 0:W], in0=bank(b_mean)[:, 0:W], scalar1=1.0 / D, scalar2=None, op0=ALU.mult)),
                      reads=[PSB[b_mean]], writes=[B_mean])
                sc.op("dve", (lambda e: e.tensor_tensor(out=rstd[:, 0:W], in0=mean[:, 0:W], in1=mean[:, 0:W], op=ALU.mult)), reads=[B_mean], writes=[cx.B_rstd])
                sc.op("dve", (lambda e: e.scalar_tensor_tensor(out=rstd[:, 0:W], in0=bank(b_ex2)[:, 0:W], scalar=1.0 / D, in1=rstd[:, 0:W],
                                                               op0=ALU.mult, op1=ALU.subtract)), reads=[PSB[b_ex2], cx.B_rstd], writes=[cx.B_rstd])
                sc.op("act", (lambda e: e.activation(out=rstd[:, 0:W], in_=rstd[:, 0:W], func=AF.Ln, bias=eps_col[:, 0:1], scale=1.0)),
                      reads=[cx.B_rstd, B_ones], writes=[cx.B_rstd])
                sc.op("act", (lambda e: e.activation(out=rstd[:, 0:W], in_=rstd[:, 0:W], func=AF.Exp, scale=-0.5)), reads=[cx.B_rstd], writes=[cx.B_rstd])
                for c in range(8):
                    sc.op("pool", (lambda e, c=c: e.tensor_tensor(out=cacc[:, c, 0:W], in0=cacc[:, c, 0:W], in1=mean[:, 0:W], op=ALU.subtract)),
                          reads=[cx.B_m, B_mean], writes=[cx.B_m])
                    sc.op("dve", (lambda e, c=c: e.tensor_tensor(out=cacc[:, c, 0:W], in0=cacc[:, c, 0:W], in1=rstd[:, 0:W], op=ALU.mult)),
                          reads=[cx.B_m, cx.B_rstd], writes=[cx.B_m])
                    sc.op("act", (lambda e, c=c: e.activation(out=sqh[:, c, 0:W], in_=cacc[:, c, 0:W], func=AF.Silu, bias=ccol("lnb", c), scale=ccol("lng", c))),
                          reads=[cx.B_m, B_cp, cx.B_sq], writes=[cx.B_h])
                yield (3.2, 14.0)

                def cons_p2(m, bi):
                    sc.op("dve", (lambda e: e.tensor_scalar(out=mT[:, m, 0:W], in0=bank(bi)[:, 0:W], scalar1=ccol("bpw2", m), scalar2=None, op0=ALU.add)),
                          reads=[PSB[bi], B_cp], writes=[cx.B_m])
                yield from dense("pw2", 8, 8, (lambda k: sqh[:, k, 0:W]), [cx.B_h], W, cons_p2)
                yield from post_norm_residual(cx, x1, B_xr, W, 5)
                yield from rmsnorm_pre(cx, x1, B_xr, W, 6)
                yield from mlp_half(cx, W, 1, 0)
                yield from mlp_half(cx, W, 1, 1)
                yield from post_norm_residual(cx, x1, B_xr, W, 7)
                c_lo = 16 if ti == 0 else 0
                tok0 = T0 - 16 + c_lo
                ncol = W - c_lo
                for hf in range(2):
                    sc.op("sp", (lambda e, hf=hf: e.dma_start(out=outT[:, 4 * hf:4 * hf + 4, tok0:tok0 + ncol], in_=xr[:, 4 * hf:4 * hf + 4, c_lo:c_lo + ncol])),
                          reads=[B_xr], dma_key="out")
                yield (0.0, 6.0)

            INF = 1e30
            gS = gT = None
            nS = nT = 0
            doneS = doneT = 0
            readyS = readyT = 0.0
            s_done_t = {}
            t_done_t = {}
            carried = 0
            t_pe = 0.0
            last = "T"
            while doneT < ntl:
                if gS is None and nS < ntl and (nS < 2 or doneT >= nS - 1):
                    gS = pipe_L0(nS)
                    if nS >= 2:
                        readyS = max(readyS, t_done_t[nS - 2])
                    nS += 1
                if gT is None and nT < ntl and doneS > nT and carried >= nT:
                    gT = pipe_L1(nT)
                    readyT = max(readyT, s_done_t[nT])
                    nT += 1
                cS = max(readyS, t_pe) if gS is not None else INF
                cT = max(readyT, t_pe) if gT is not None else INF
                assert cS < INF or cT < INF
                if cS < cT or (cS == cT and last == "T"):
                    pick = "S"
                else:
                    pick = "T"
                last = pick
                if pick == "S":
                    try:
                        pe_us, lat = next(gS)
                        t_pe = max(t_pe, readyS) + pe_us
                        readyS = t_pe + lat
                    except StopIteration:
                        gS = None
                        s_done_t[doneS] = readyS
                        doneS += 1
                else:
                    try:
                        pe_us, lat = next(gT)
                        t_pe = max(t_pe, readyT) + pe_us
                        readyT = t_pe + lat
                    except StopIteration:
                        gT = None
                        t_done_t[doneT] = readyT
                        doneT += 1
                while carried < ntl - 1 and doneS > carried and doneT >= carried:
                    it = carried
                    T0, W = own_tiles[it]
                    xr, nx = xres[it % 2], xres[(it + 1) % 2]
                    ub, nu = ubuf[it % 2], ubuf[(it + 1) % 2]
                    sc.op("pool", (lambda e, xr=xr, nx=nx, W=W: e.tensor_copy(out=nx[:, :, 0:16], in_=xr[:, :, W:W + 16])),
                          reads=[B_xres[it % 2]], writes=[B_xres[(it + 1) % 2]])
                    sc.op("pool", (lambda e, ub=ub, nu=nu, W=W: e.tensor_copy(out=nu[:, :, 0:31], in_=ub[:, :, W:W + 31])),
                          reads=[B_u[it % 2]], writes=[B_u[(it + 1) % 2]])
                    carried += 1
            sc.flush(final=True)
    return nc


def prep_core(inp, core, S):
    b, half = core // 2, core % 2
    idx = np.arange(S) if half == 0 else np.arange(S - 1, -1, -1)
    x = np.asarray(inp["x"][b], dtype=np.float32)[idx]
    xT = np.ascontiguousarray(x.T.reshape(8, 128, S).transpose(1, 0, 2))
    pos = np.ascontiguousarray(np.asarray(inp["positions"][b])[idx].astype(np.int32).reshape(1, S))
    return xT, pos, half


def prep_shared(inp, half):
    woff, WTOT = weight_layout()
    coff, CTOT = const_layout()
    f = lambda k: np.asarray(inp[k], dtype=np.float32)
    wp = np.zeros((128, WTOT), np.float32)

    def put(name, arr):
        o, n = woff[name]
        assert arr.shape == (128, n), (name, arr.shape, n)
        wp[:, o:o + n] = arr
    w_in = f("even_w_in")[0]
    kr = np.zeros((D, 128), np.float32)
    kr[:, 64:96] = w_in[:, 2176:2208]
    krp = np.zeros((D, 128), np.float32)
    krp[:, 64:80] = w_in[:, 2192:2208]
    krp[:, 80:96] = w_in[:, 2176:2192]
    winA = np.concatenate([w_in[:, 1536:2176], kr, krp], axis=1)
    put("w_inA", pack_lhsT(winA, 8, 7))
    put("w_inG", pack_lhsT(w_in[:, 0:1536], 8, 12))
    w_uq = f("even_w_uq")[0].reshape(384, 8, 96)
    wq = np.zeros((384, 8, 2, 96), np.float32)
    wq[:, :, 0, :] = w_uq
    wq[:, :, 1, 64:80] = w_uq[:, :, 80:96]
    wq[:, :, 1, 80:96] = w_uq[:, :, 64:80]
    wq = wq.reshape(3, 128, 8, 2, 96).transpose(1, 2, 3, 0, 4)
    put("w_uq", np.ascontiguousarray(wq).reshape(128, -1))
    w_ukv = f("even_w_ukv")[0]
    put("w_ukv", np.ascontiguousarray(w_ukv.reshape(2, 128, 1024).transpose(1, 0, 2)).reshape(128, -1))
    put("w_out", pack_lhsT(f("even_w_out")[0], 8, 8))
    put("m0w1", pack_lhsT(f("mlp_w1")[0], 8, 32))
    put("m0w2", pack_w2(f("mlp_w2")[0]))
    pw1 = f("odd_w_pw1")[0].reshape(D, 2, 8, 128).transpose(0, 2, 1, 3).reshape(D, 2048)
    put("pw1", pack_lhsT(pw1, 8, 16))
    put("pw2", pack_lhsT(f("odd_w_pw2")[0], 8, 8))
    put("m1w1", pack_lhsT(f("mlp_w1")[1], 8, 32))
    put("m1w2", pack_w2(f("mlp_w2")[1]))

    cpk = np.zeros((128, CTOT), np.float32)

    def putc(name, arr):
        o, n = coff[name]
        assert arr.shape == (128, n), (name, arr.shape, n)
        cpk[:, o:o + n] = arr
    putc("gains", colpack(f("sandwich_gains").reshape(-1)))
    putc("qg", colpack(f("even_q_norm")[0]))
    putc("kvg", colpack(f("even_kv_norm")[0]))
    sck = f("even_sc_kernel")[0]
    wdw = f("odd_w_dw")[0]
    if half == 1:
        sck = sck[::-1]
        wdw = wdw[::-1]
    putc("sck", np.ascontiguousarray(sck.reshape(3, 4, 128).transpose(2, 1, 0)).reshape(128, 12))
    dwd = np.zeros((128, 8, 31, 128), np.float32)
    pidx = np.arange(128)
    dwd[pidx, :, :, pidx] = wdw.reshape(31, 8, 128).transpose(2, 1, 0)
    put("dwd", dwd.reshape(128, -1))
    putc("wdw", np.ascontiguousarray(wdw.reshape(31, 8, 128).transpose(2, 1, 0)).reshape(128, 248))
    putc("bpw1", colpack(f("odd_b_pw1")[0]))
    putc("bdw", colpack(f("odd_b_dw")[0]))
    putc("lng", colpack(f("odd_ln_g")[0]))
    putc("lnb", colpack(f("odd_ln_b")[0]))
    putc("bpw2", colpack(f("odd_b_pw2")[0]))
    half_d = 16
    inv_freq = (1.0 / (np.float32(10000.0) ** (np.arange(half_d, dtype=np.float32) / np.float32(half_d)))).astype(np.float32)
    invf = np.zeros((128, 1), np.float32)
    invf[64:80, 0] = inv_freq
    invf[80:96, 0] = inv_freq
    putc("invf", invf)
    sgn = np.ones((128, 1), np.float32)
    sgn[64:80, 0] = -1.0
    putc("sgn", sgn)
    return wp, cpk


_NC_CACHE = {}


def kernel(**inputs):
    x = np.asarray(inputs["x"])
    B, S, _ = x.shape
    ncores = 2 * B
    if S not in _NC_CACHE:
        _NC_CACHE[S] = build(S)
    nc = _NC_CACHE[S]
    shared = [prep_shared(inputs, h) for h in range(2)]
    in_maps = []
    for c in range(ncores):
        xT, pos, half = prep_core(inputs, c, S)
        wp, cpk = shared[half]
        in_maps.append({"xT": xT, "pos": pos, "wpack": wp, "cpack": cpk})
    res = run_bass_kernel_spmd(nc, in_maps, core_ids=list(range(ncores)))
    OWN = S // 2
    out = np.empty((B, S, D), np.float32)
    for c in range(ncores):
        b, half = c // 2, c % 2
        oT = np.asarray(res.results[c]["outT"])
        o = oT.transpose(2, 1, 0).reshape(OWN, D)
        if half == 0:
            out[b, :OWN] = o
        else:
            out[b, OWN:] = o[::-1]
    return out
```

```python
import numpy as np
from contextlib import ExitStack
import concourse.bass as bass
import concourse.mybir as mybir
from concourse.bass_utils import run_bass_kernel_spmd

F32 = mybir.dt.float32
BF16 = mybir.dt.bfloat16
I32 = mybir.dt.int32
AF = mybir.ActivationFunctionType
ALU = mybir.AluOpType
AX = mybir.AxisListType

D = 1024
NH = 8
HALO = 16
TW = 464
EPS = 1e-6
SEQ_FULL = 8192
SEM_LIM = 30000


class Buf:
    __slots__ = ("name", "last_w", "readers")

    def __init__(self, name):
        self.name = name
        self.last_w = None
        self.readers = []


class Op:
    __slots__ = ("eng", "fn", "deps", "dma", "sig", "idx", "dma_deps", "bar")

    def __init__(self, eng, fn, dma):
        self.eng = eng
        self.fn = fn
        self.dma = dma
        self.deps = set()
        self.dma_deps = {}
        self.sig = None
        self.bar = None


ENGS = ("pe", "act", "dve", "pool", "sp")


class Sched:
    def __init__(self, nc, stack):
        self.nc = nc
        self.stack = stack
        self.ops = []
        self.allbufs = []
        self.sig_count = {e: 0 for e in ENGS}
        self.esems = {e: [] for e in ENGS}
        self.dsems = {}
        self.known = {e: {} for e in ENGS}
        self.known_d = {e: {} for e in ENGS}
        self.pending_bar = None
        self.out_sems = []

    def buf(self, name):
        b = Buf(name)
        self.allbufs.append(b)
        return b

    def dsem(self, key):
        if key not in self.dsems:
            s = self.stack.enter_context(self.nc.semaphore("d_" + key))
            self.dsems[key] = [s, 0]
        return self.dsems[key]

    def esem(self, eng, epoch):
        lst = self.esems[eng]
        while len(lst) <= epoch:
            lst.append(self.stack.enter_context(self.nc.semaphore("e_%s_%d" % (eng, len(lst)))))
        return lst[epoch]

    def op(self, eng, fn, reads=(), writes=(), dma_key=None, extra_dma=None):
        o = Op(eng, fn, None)
        oid = len(self.ops)
        if dma_key is not None:
            d = self.dsem(dma_key)
            d[1] += 16
            o.dma = (dma_key, d[1])
        deps = []
        for b in reads:
            if b.last_w is not None:
                deps.append((b.last_w, "raw"))
        for b in writes:
            if b.last_w is not None:
                deps.append((b.last_w, "waw"))
            for r in b.readers:
                deps.append((r, "war"))
        for (pid, kind) in deps:
            if pid == oid:
                continue
            p = self.ops[pid]
            if p.dma is not None:
                k = p.dma[0]
                v = self.dsems[k][1] if o.dma is None or o.dma[0] != k else p.dma[1]
                o.dma_deps[k] = max(o.dma_deps.get(k, 0), v)
            else:
                if p.eng == eng and o.dma is None:
                    if eng == "pe":
                        continue
                o.deps.add(pid)
        if extra_dma:
            for k, v in extra_dma.items():
                o.dma_deps[k] = max(o.dma_deps.get(k, 0), v)
        for b in reads:
            b.readers.append(oid)
        for b in writes:
            b.last_w = oid
            b.readers = []
        self.ops.append(o)
        return oid

    def flush(self, final=False):
        ops = self.ops
        nc = self.nc
        needed = set()
        for o in ops:
            for pid in o.deps:
                needed.add(pid)
        last_of = {}
        for i, o in enumerate(ops):
            if o.dma is None:
                last_of[o.eng] = i
        for e, i in last_of.items():
            needed.add(i)
        for i, o in enumerate(ops):
            if i in needed and o.dma is None:
                self.sig_count[o.eng] += 1
                o.sig = self.sig_count[o.eng]
        bar = self.pending_bar
        per_eng = {e: [] for e in ENGS}
        for i, o in enumerate(ops):
            per_eng[o.eng].append(o)
        sched = self

        def emit(engname, engine):
            known = sched.known[engname]
            known_d = sched.known_d[engname]

            def wait_sig(e2, sig):
                if known.get(e2, 0) >= sig:
                    return
                known[e2] = sig
                ep, val = (sig - 1) // SEM_LIM, (sig - 1) % SEM_LIM + 1
                engine.wait_ge(sched.esem(e2, ep), val)

            def wait_dma(k, v):
                if known_d.get(k, 0) >= v:
                    return
                known_d[k] = v
                engine.wait_ge(sched.dsems[k][0], v)

            if bar is not None:
                for e2, sig in bar[0].items():
                    if e2 != engname and sig > 0:
                        wait_sig(e2, sig)
                for k, v in bar[1].items():
                    if not k.startswith("cv_"):
                        wait_dma(k, v)
            for o in per_eng[engname]:
                for pid in sorted(o.deps):
                    p = ops[pid]
                    wait_sig(p.eng, p.sig)
                for k, v in o.dma_deps.items():
                    wait_dma(k, v)
                ins = o.fn(engine)
                if o.dma is not None:
                    ins.then_inc(sched.dsems[o.dma[0]][0], 16)
                elif o.sig is not None:
                    ep = (o.sig - 1) // SEM_LIM
                    ins.then_inc(sched.esem(o.eng, ep), 1)
            if final and engname == "sp":
                for k, d in sched.dsems.items():
                    wait_dma(k, d[1])

        for e in ENGS:
            self.esem(e, max(0, (self.sig_count[e] - 1)) // SEM_LIM)
        with nc.Block() as block:
            @block.tensor
            def _(eng):
                emit("pe", eng)

            @block.scalar
            def _(eng):
                emit("act", eng)

            @block.vector
            def _(eng):
                emit("dve", eng)

            @block.gpsimd
            def _(eng):
                emit("pool", eng)

            @block.sync
            def _(eng):
                emit("sp", eng)
        self.pending_bar = ({e: self.sig_count[e] for e in ENGS}, {k: d[1] for k, d in self.dsems.items()})
        self.ops = []
        for b in self.allbufs:
            b.last_w = None
            b.readers = []


def weight_layout():
    off = {}
    cur = 0

    def add(name, n):
        nonlocal cur
        off[name] = (cur, n)
        cur += n
    add("w_inA", 7 * 8 * 128)
    add("w_uq", 8 * 2 * 3 * 96)
    add("w_ukv", 2 * 1024)
    add("w_inG", 12 * 8 * 128)
    add("w_out", 8 * 8 * 128)
    add("m0w1", 32 * 8 * 128)
    add("m0w2", 8 * 32 * 128)
    add("pw1", 16 * 8 * 128)
    add("pw2", 8 * 8 * 128)
    add("m1w1", 32 * 8 * 128)
    add("m1w2", 8 * 32 * 128)
    add("dwd", 8 * 31 * 128)
    return off, cur


def const_layout():
    off = {}
    cur = 0

    def add(name, n):
        nonlocal cur
        off[name] = (cur, n)
        cur += n
    add("gains", 64)
    add("qg", 3)
    add("kvg", 2)
    add("sck", 12)
    add("bpw1", 16)
    add("wdw", 8 * 31)
    add("bdw", 8)
    add("lng", 8)
    add("lnb", 8)
    add("bpw2", 8)
    add("invf", 1)
    add("sgn", 1)
    return off, cur


def pack_lhsT(w, nk, nm, mw=128):
    return np.ascontiguousarray(w.reshape(nk, 128, nm, mw).transpose(1, 2, 0, 3)).reshape(128, -1)


def pack_w2(w):
    return np.concatenate([pack_lhsT(w[h * 2048:(h + 1) * 2048], 16, 8) for h in range(2)], axis=1)


def colpack(v):
    return np.ascontiguousarray(v.reshape(-1, 128).T)


def split_tiles(n, w):
    out = []
    t0 = 0
    while t0 < n:
        ww = min(w, n - t0)
        out.append((t0, ww))
        t0 += ww
    return out


def build(S):
    OWN = S // 2
    OWNH = OWN + HALO
    own_tiles = split_tiles(OWNH, TW)
    oth_tiles = [(OWNH + a, w) for (a, w) in split_tiles(S - OWNH, 512)]
    NKT = S // 128
    woff, WTOT = weight_layout()
    coff, CTOT = const_layout()
    scale = float((64 + 32) ** -0.5)

    nc = bass.Bass("TRN2", target_bir_lowering=False)
    xT = nc.dram_tensor("xT", [128, 8, S], F32, kind="ExternalInput").ap()
    posd = nc.dram_tensor("pos", [1, S], I32, kind="ExternalInput").ap()
    wpack = nc.dram_tensor("wpack", [128, WTOT], F32, kind="ExternalInput").ap()
    cpackd = nc.dram_tensor("cpack", [128, CTOT], F32, kind="ExternalInput").ap()
    wbf = nc.dram_tensor("wbf", [128, WTOT], BF16, kind="Internal").ap()
    outT = nc.dram_tensor("outT", [128, 8, OWN], F32, kind="ExternalOutput").ap()

    with ExitStack() as top:
        sc = Sched(nc, top)

        def sb(stack, name, shape, dt):
            return stack.enter_context(nc.sbuf_tensor(name, shape, dt))

        psum = top.enter_context(nc.psum_tensor("ps", [128, 8 * 512], F32))
        PSB = [sc.buf("psb%d" % i) for i in range(8)]
        ps_rr = [0]

        def bank(i):
            return psum[:, i * 512:(i + 1) * 512]

        def next_bank(lo=0, hi=8):
            i = lo + ps_rr[0] % (hi - lo)
            ps_rr[0] += 1
            return i

        cp = sb(top, "cp", [128, CTOT], F32)
        B_cp = sc.buf("cp")
        ones = sb(top, "ones", [128, 128], BF16)
        B_ones = sc.buf("ones")
        ybT = sb(top, "ybT", [128, 4, OWNH], BF16)
        B_yb = sc.buf("yb")

        def ccol(name, i=0, p0=0, p1=128):
            o = coff[name][0] + i
            return cp[p0:p1, o:o + 1]

        sc.op("sp", lambda e: e.dma_start(out=cp[:, :], in_=cpackd[:, :]), writes=[B_cp], dma_key="cp")
        sc.op("dve", lambda e: e.memset(ones[:, :], 1.0), writes=[B_ones])
        eps_col = sb(top, "eps_col", [128, 1], F32)
        sc.op("dve", lambda e: e.memset(eps_col[:, :], EPS), writes=[B_ones])
        B_wbf = {}
        order = ["w_inA", "w_uq", "w_ukv", "w_inG", "w_out", "m0w1", "m0w2", "pw1", "dwd", "pw2", "m1w1", "m1w2"]
        for name in order:
            o, n = woff[name]
            B_wbf[name] = sc.buf("wbf_" + name)
            step = 8192
            for a in range(0, n, step):
                b = min(n, a + step)
                sc.op("pool", (lambda e, a=a, b=b, o=o: e.dma_start(out=wbf[:, o + a:o + b], in_=wpack[:, o + a:o + b])),
                      writes=[B_wbf[name]], dma_key="cv_" + name)

        evac_rr = [0]

        def evac_eng():
            evac_rr[0] += 1
            return "act" if evac_rr[0] % 2 else "dve"

        def copy_op(eng, out, in_, reads, writes):
            if eng == "act":
                sc.op("act", lambda e: e.activation(out=out, in_=in_, func=AF.Copy), reads=reads, writes=writes)
            elif eng == "dve":
                sc.op("dve", lambda e: e.tensor_copy(out=out, in_=in_), reads=reads, writes=writes)
            else:
                sc.op("pool", lambda e: e.tensor_copy(out=out, in_=in_), reads=reads, writes=writes)

        def mm_group(bi, W, pairs, reads, M=128, c0=0):
            n = len(pairs)
            for i, (l, r) in enumerate(pairs):
                sc.op("pe", (lambda e, l=l, r=r, i=i: e.matmul(bank(bi)[0:M, c0:c0 + W], l, r, start=(i == 0), stop=(i == n - 1))),
                      reads=reads, writes=[PSB[bi]])

        def stat_rstd(sq_chunks, reads, W, dim, rstd_ap, B_rstd, mean_out=None):
            bi = next_bank()
            mm_group(bi, W, [(ones[:, :], s) for s in sq_chunks], reads + [B_ones])
            sc.op("act", lambda e: e.activation(out=rstd_ap, in_=bank(bi)[:, 0:W], func=AF.Ln, bias=eps_col[:, 0:1], scale=1.0 / dim),
                  reads=[PSB[bi], B_ones], writes=[B_rstd])
            sc.op("act", lambda e: e.activation(out=rstd_ap, in_=rstd_ap, func=AF.Exp, scale=-0.5), reads=[B_rstd], writes=[B_rstd])

        def rms_apply(nch, src, B_src, gname, gidx0, rstd_ap, B_rstd, dst, B_dst, W, engs=("dve",)):
            for c in range(nch):
                eng = engs[c % len(engs)]
                sc.op(eng, (lambda e, c=c: e.scalar_tensor_tensor(out=dst(c), in0=src(c), scalar=ccol(gname, gidx0 + c),
                                                                   in1=rstd_ap, op0=ALU.mult, op1=ALU.mult)),
                      reads=[B_src, B_rstd, B_cp], writes=[B_dst])

        with ExitStack() as ab:
            qnT = sb(ab, "qnT", [128, 3, OWNH], BF16)
            B_qn = sc.buf("qn")
            cosT = sb(ab, "cosT", [128, OWNH], BF16)
            sinT = sb(ab, "sinT", [128, OWNH], BF16)
            B_tab = sc.buf("tab")
            ckvT = sb(ab, "ckvT", [128, 2, S], BF16)
            B_ckv = sc.buf("ckv")
            Kt = sb(ab, "Kt", [128, S], BF16)
            B_Kpe = sc.buf("Kpe")
            B_Kn = sc.buf("Kn")
            with ExitStack() as pa:
                xt = [sb(pa, "xtA", [128, 4, 512], F32), sb(pa, "xtB", [128, 4, 512], F32)]
                B_xt = [sc.buf("xtA"), sc.buf("xtB")]
                sqa = sb(pa, "sqa", [128, 8, 512], BF16)
                B_sq = sc.buf("sq")
                hbuf = [sb(pa, "hbuf%d" % i, [128, 8, 512], BF16) for i in range(2)]
                B_hs = [sc.buf("h%d" % i) for i in range(2)]
                rstd = sb(pa, "rstd", [128, 512], F32)
                B_rstd = sc.buf("rstd")
                winA = sb(pa, "winA", [128, 7, 8, 128], BF16)
                B_winA = sc.buf("winA")
                lat = sb(pa, "lat", [128, 3, 512], F32)
                B_lat = sc.buf("lat")
                sq2 = sb(pa, "sq2", [128, 3, 512], BF16)
                B_sq2 = sc.buf("sq2")
                rstd2 = sb(pa, "rstd2", [128, 512], F32)
                B_rstd2 = sc.buf("rstd2")
                posi = sb(pa, "posi", [128, 512], I32)
                B_pos = sc.buf("pos")
                ang = sb(pa, "ang", [128, 512], F32)
                B_ang = sc.buf("ang")
                cts = [sb(pa, "ct%d" % i, [128, 512], F32) for i in range(2)]
                sts = [sb(pa, "st%d" % i, [128, 512], F32) for i in range(2)]
                B_css = [sc.buf("cs%d" % i) for i in range(2)]
                t1 = sb(pa, "t1", [128, 512], F32)
                t2 = sb(pa, "t2", [128, 512], F32)
                B_t12 = sc.buf("t12")
                k1, k2, B_k12 = lat[:, 0, :], lat[:, 1, :], B_lat
                o, n = woff["w_inA"]
                sc.op("sp", lambda e, o=o, n=n: e.dma_start(out=winA[:, :, :, :].rearrange("p m k j -> p (m k j)"), in_=wbf[:, o:o + n]),
                      reads=[B_wbf["w_inA"]], writes=[B_winA], dma_key="winA")

                R = slice(64, 96)
                PI = float(np.pi)
                MAGIC = 12582912.0
                C1 = 6.28125
                C2 = float(2 * np.pi - 6.28125)
                all_tiles = own_tiles + oth_tiles
                n_own = len(own_tiles)

                def gen_N(ti):
                    T0, W = all_tiles[ti]
                    hb, B_hb = hbuf[ti % 2], B_hs[ti % 2]
                    for hf in range(2):
                        sc.op("sp", (lambda e, hf=hf: e.dma_start(out=xt[hf][:, :, 0:W], in_=xT[:, 4 * hf:4 * hf + 4, T0:T0 + W])),
                              writes=[B_xt[hf]], dma_key="xt%d" % hf)
                    for hf in range(2):
                        sc.op("act", (lambda e, hf=hf: e.activation(out=sqa[:, 4 * hf:4 * hf + 4, 0:W], in_=xt[hf][:, :, 0:W], func=AF.Square)),
                              reads=[B_xt[hf]], writes=[B_sq])
                    yield
                    stat_rstd([sqa[:, c, 0:W] for c in range(8)], [B_sq], W, float(D), rstd[:, 0:W], B_rstd)
                    yield
                    for c in range(8):
                        sc.op("dve", (lambda e, c=c: e.scalar_tensor_tensor(out=hb[:, c, 0:W], in0=xt[c // 4][:, c % 4, 0:W],
                                                                             scalar=ccol("gains", c), in1=rstd[:, 0:W],
                                                                             op0=ALU.mult, op1=ALU.mult)),
                              reads=[B_xt[c // 4], B_rstd, B_cp], writes=[B_hb])
                    yield

                def gen_R(ti):
                    T0, W = all_tiles[ti]
                    ct, st, B_cs = cts[ti % 2], sts[ti % 2], B_css[ti % 2]
                    sc.op("sp", (lambda e: e.dma_start(out=posi[64:96, 0:W], in_=posd[0:1, T0:T0 + W].broadcast_to([32, W]))),
                          writes=[B_pos], dma_key="pos")
                    sc.op("dve", (lambda e: e.tensor_copy(out=ang[R, 0:W], in_=posi[R, 0:W])), reads=[B_pos], writes=[B_ang])
                    sc.op("dve", (lambda e: e.tensor_scalar(out=ang[R, 0:W], in0=ang[R, 0:W], scalar1=ccol("invf", 0, 64, 96), scalar2=None,
                                                            op0=ALU.mult)), reads=[B_ang, B_cp], writes=[B_ang])
                    sc.op("dve", (lambda e: e.tensor_scalar(out=t1[R, 0:W], in0=ang[R, 0:W], scalar1=float(1.0 / (2 * np.pi)), scalar2=MAGIC,
                                                            op0=ALU.mult, op1=ALU.add)), reads=[B_ang], writes=[B_t12])
                    sc.op("dve", (lambda e: e.tensor_scalar(out=t1[R, 0:W], in0=t1[R, 0:W], scalar1=-MAGIC, scalar2=None, op0=ALU.add)),
                          reads=[B_t12], writes=[B_t12])
                    yield
                    sc.op("dve", (lambda e: e.scalar_tensor_tensor(out=t2[R, 0:W], in0=t1[R, 0:W], scalar=-C1, in1=ang[R, 0:W], op0=ALU.mult, op1=ALU.add)),
                          reads=[B_t12, B_ang], writes=[B_t12])
                    sc.op("dve", (lambda e: e.scalar_tensor_tensor(out=t2[R, 0:W], in0=t1[R, 0:W], scalar=-C2, in1=t2[R, 0:W], op0=ALU.mult, op1=ALU.add)),
                          reads=[B_t12], writes=[B_t12])
                    sc.op("dve", (lambda e: e.tensor_scalar(out=t2[R, 0:W], in0=t2[R, 0:W], scalar1=-PI, scalar2=PI, op0=ALU.max, op1=ALU.min)),
                          reads=[B_t12], writes=[B_t12])
                    sc.op("act", (lambda e: e.activation(out=st[R, 0:W], in_=t2[R, 0:W], func=AF.Sin, scale=ccol("sgn", 0, 64, 96))),
                          reads=[B_t12, B_cp], writes=[B_cs])
                    sc.op("act", (lambda e: e.activation(out=ct[R, 0:W], in_=t2[R, 0:W], func=AF.Sin, scale=0.5)),
                          reads=[B_t12], writes=[B_cs])
                    yield
                    sc.op("dve", (lambda e: e.tensor_tensor(out=ct[R, 0:W], in0=ct[R, 0:W], in1=ct[R, 0:W], op=ALU.mult)), reads=[B_cs], writes=[B_cs])
                    sc.op("dve", (lambda e: e.tensor_scalar(out=ct[R, 0:W], in0=ct[R, 0:W], scalar1=-2.0, scalar2=1.0, op0=ALU.mult, op1=ALU.add)),
                          reads=[B_cs], writes=[B_cs])
                    if ti < n_own:
                        sc.op("act", (lambda e: e.activation(out=cosT[R, T0:T0 + W], in_=ct[R, 0:W], func=AF.Copy)), reads=[B_cs], writes=[B_tab])
                        sc.op("act", (lambda e: e.activation(out=sinT[R, T0:T0 + W], in_=st[R, 0:W], func=AF.Copy)), reads=[B_cs], writes=[B_tab])
                    yield

                def gen_P(ti):
                    T0, W = all_tiles[ti]
                    hb, B_hb = hbuf[ti % 2], B_hs[ti % 2]
                    ct, st, B_cs = cts[ti % 2], sts[ti % 2], B_css[ti % 2]
                    hT = lambda c: hb[:, c, 0:W]
                    groups = [("kv", [3, 4], "kvg", 256.0)]
                    if ti < n_own:
                        groups.append(("q", [0, 1, 2], "qg", 384.0))
                    for (gname, chunks, gn, dim) in groups:
                        for j, m in enumerate(chunks):
                            bi = next_bank()
                            mm_group(bi, W, [(winA[:, m, k, :], hT(k)) for k in range(8)], [B_hb, B_winA])
                            copy_op(evac_eng(), lat[:, j, 0:W], bank(bi)[:, 0:W], [PSB[bi]], [B_lat])
                        nchk = len(chunks)
                        sc.op("act", (lambda e, nchk=nchk: e.activation(out=sq2[:, 0:nchk, 0:W], in_=lat[:, 0:nchk, 0:W], func=AF.Square)),
                              reads=[B_lat], writes=[B_sq2])
                        yield
                        stat_rstd([sq2[:, j, 0:W] for j in range(nchk)], [B_sq2], W, dim, rstd2[:, 0:W], B_rstd2)
                        yield
                        if gname == "kv":
                            dst = lambda c: ckvT[:, c, T0:T0 + W]
                            B_dst = B_ckv
                        else:
                            dst = lambda c: qnT[:, c, T0:T0 + W]
                            B_dst = B_qn
                        rms_apply(nchk, (lambda c: lat[:, c, 0:W]), B_lat, gn, 0, rstd2[:, 0:W], B_rstd2, dst, B_dst, W)
                        yield
                    b1 = next_bank()
                    mm_group(b1, W, [(winA[:, 5, k, :], hT(k)) for k in range(8)], [B_hb, B_winA])
                    b2 = next_bank()
                    mm_group(b2, W, [(winA[:, 6, k, :], hT(k)) for k in range(8)], [B_hb, B_winA])
                    sc.op("dve", (lambda e: e.tensor_tensor(out=k1[R, 0:W], in0=bank(b1)[R, 0:W], in1=ct[R, 0:W], op=ALU.mult)),
                          reads=[PSB[b1], B_cs], writes=[B_k12])
                    sc.op("dve", (lambda e: e.tensor_tensor(out=k2[R, 0:W], in0=bank(b2)[R, 0:W], in1=st[R, 0:W], op=ALU.mult)),
                          reads=[PSB[b2], B_cs], writes=[B_k12])
                    sc.op("dve", (lambda e: e.tensor_tensor(out=Kt[R, T0:T0 + W], in0=k1[R, 0:W], in1=k2[R, 0:W], op=ALU.add)),
                          reads=[B_k12], writes=[B_Kpe])
                    yield

                def drain(g):
                    for _ in g:
                        pass

                def chain(*gens):
                    for g in gens:
                        yield from g

                ntile = len(all_tiles)
                drain(gen_N(0))
                drain(gen_R(0))
                for ti in range(ntile):
                    gp = gen_P(ti)
                    gn = chain(gen_N(ti + 1), gen_R(ti + 1)) if ti + 1 < ntile else None
                    while gp is not None or gn is not None:
                        if gn is not None:
                            try:
                                next(gn)
                            except StopIteration:
                                gn = None
                        if gp is not None:
                            try:
                                next(gp)
                            except StopIteration:
                                gp = None
                sc.flush()
            with ExitStack() as pb:
                wuq = sb(pb, "wuq", [128, 8, 2, 3, 96], BF16)
                B_wuq = sc.buf("wuq")
                wukv = sb(pb, "wukv", [128, 2, 1024], BF16)
                B_wukv = sc.buf("wukv")
                o, n = woff["w_uq"]
                sc.op("sp", lambda e, o=o, n=n: e.dma_start(out=wuq[:, :, :, :, :].rearrange("p h v k j -> p (h v k j)"), in_=wbf[:, o:o + n]),
                      reads=[B_wbf["w_uq"]], writes=[B_wuq], dma_key="wuq", extra_dma={"cv_w_uq": sc.dsems["cv_w_uq"][1]})
                o, n = woff["w_ukv"]
                sc.op("sp", lambda e, o=o, n=n: e.dma_start(out=wukv[:, :, :].rearrange("p k j -> p (k j)"), in_=wbf[:, o:o + n]),
                      reads=[B_wbf["w_ukv"]], writes=[B_wukv], dma_key="wukv", extra_dma={"cv_w_ukv": sc.dsems["cv_w_ukv"][1]})

                Vt = sb(pb, "Vt", [128, NKT, 128], BF16)
                B_V = sc.buf("V")
                Pb = [sb(pb, "Pb%d" % i, [128, 3, 512], BF16) for i in range(3)]
                B_P = [sc.buf("P%d" % i) for i in range(3)]
                Kt2 = sb(pb, "Kt2", [128, S], BF16)
                Ks = [Kt, Kt2]
                B_Kns = [B_Kn, sc.buf("Kn2")]
                B_Kpes = [B_Kpe, sc.buf("Kpe2")]
                Qs = [sb(pb, "Qh%d" % i, [128, OWNH], BF16) for i in range(2)]
                B_Qs = [sc.buf("Q%d" % i) for i in range(2)]
                sqt = sb(pb, "sqt", [128, 512], BF16)
                B_sqt = sc.buf("sqt")
                mxs = [sb(pb, "mx%d" % i, [128, 64], F32) for i in range(2)]
                B_mxs = [sc.buf("mx%d" % i) for i in range(2)]
                negms = [sb(pb, "negm%d" % i, [128, 2], F32) for i in range(2)]
                B_negms = [sc.buf("negm%d" % i) for i in range(2)]
                rec = sb(pb, "rec", [128, 512], F32)
                B_rec = sc.buf("rec")
                qtmp = sb(pb, "qtmp", [128, 2, 512], F32)
                B_qtmp = sc.buf("qtmp")
                sc.op("pool", lambda e: e.memset(Vt[:, :, 64:128], 1.0), writes=[B_V])
                sc.op("pool", lambda e: e.tensor_copy(out=Kt2[R, :], in_=Kt[R, :]), reads=[B_Kpe], writes=[B_Kpes[1]])
                kblocks = split_tiles(S, 512)
                PB = 7
                OB = 6

                def prologue(h):
                    Kc, B_Kn_c, B_Kpe_c = Ks[h % 2], B_Kns[h % 2], B_Kpes[h % 2]
                    Qc, B_Qc = Qs[h % 2], B_Qs[h % 2]
                    mx, B_mx = mxs[h % 2], B_mxs[h % 2]
                    negm, B_negm = negms[h % 2], B_negms[h % 2]
                    for (k0, kw) in kblocks:
                        mm_group(PB, kw, [(wukv[:, k, h * 128:h * 128 + 64], ckvT[:, k, k0:k0 + kw]) for k in range(2)], [B_wukv, B_ckv], M=64)
                        copy_op("dve", Kc[0:64, k0:k0 + kw], bank(PB)[0:64, 0:kw], [PSB[PB]], [B_Kn_c])
                        yield
                    for (T0, W) in own_tiles:
                        mm_group(PB, W, [(wuq[:, h, 0, k, :], qnT[:, k, T0:T0 + W]) for k in range(3)], [B_wuq, B_qn], M=96)
                        copy_op("dve", Qc[0:64, T0:T0 + W], bank(PB)[0:64, 0:W], [PSB[PB]], [B_Qc])
                        sc.op("dve", (lambda e, T0=T0, W=W: e.tensor_tensor(out=qtmp[R, 0, 0:W], in0=bank(PB)[R, 0:W], in1=cosT[R, T0:T0 + W], op=ALU.mult)),
                              reads=[PSB[PB], B_tab], writes=[B_qtmp])
                        yield
                        mm_group(PB, W, [(wuq[:, h, 1, k, :], qnT[:, k, T0:T0 + W]) for k in range(3)], [B_wuq, B_qn], M=96)
                        sc.op("dve", (lambda e, T0=T0, W=W: e.tensor_tensor(out=qtmp[R, 1, 0:W], in0=bank(PB)[R, 0:W], in1=sinT[R, T0:T0 + W], op=ALU.mult)),
                              reads=[PSB[PB], B_tab], writes=[B_qtmp])
                        sc.op("pool", (lambda e, T0=T0, W=W: e.tensor_tensor(out=Qc[R, T0:T0 + W], in0=qtmp[R, 0, 0:W], in1=qtmp[R, 1, 0:W], op=ALU.add)),
                              reads=[B_qtmp], writes=[B_Qc])
                        yield
                    nq = len(own_tiles)
                    nk = len(kblocks)
                    col = 0
                    for (src_t, B_src, tiles) in ((Qc, [B_Qc], own_tiles), (Kc, [B_Kn_c, B_Kpe_c], kblocks)):
                        for (T0, W) in tiles:
                            sc.op("pool", (lambda e, src_t=src_t, T0=T0, W=W: e.tensor_tensor(out=sqt[0:96, 0:W], in0=src_t[0:96, T0:T0 + W], in1=src_t[0:96, T0:T0 + W], op=ALU.mult)),
                                  reads=B_src, writes=[B_sqt])
                            mm_group(PB, W, [(ones[0:96, :], sqt[0:96, 0:W])], [B_sqt, B_ones])
                            sc.op("dve", (lambda e, W=W, col=col: e.tensor_reduce(out=mx[:, col:col + 1], in_=bank(PB)[:, 0:W], axis=AX.X, op=ALU.max)),
                                  reads=[PSB[PB]], writes=[B_mx])
                            col += 1
                            yield
                    sc.op("dve", lambda e: e.tensor_reduce(out=negm[:, 0:1], in_=mx[:, 0:nq], axis=AX.X, op=ALU.max), reads=[B_mx], writes=[B_negm])
                    sc.op("dve", lambda e: e.tensor_reduce(out=negm[:, 1:2], in_=mx[:, nq:nq + nk], axis=AX.X, op=ALU.max), reads=[B_mx], writes=[B_negm])
                    sc.op("dve", lambda e: e.tensor_tensor(out=negm[:, 0:1], in0=negm[:, 0:1], in1=negm[:, 1:2], op=ALU.mult), reads=[B_negm], writes=[B_negm])
                    sc.op("act", lambda e: e.activation(out=negm[:, 0:1], in_=negm[:, 0:1], func=AF.Ln), reads=[B_negm], writes=[B_negm])
                    sc.op("act", lambda e: e.activation(out=negm[:, 0:1], in_=negm[:, 0:1], func=AF.Exp, scale=0.5), reads=[B_negm], writes=[B_negm])
                    sc.op("dve", lambda e: e.tensor_scalar(out=negm[:, 0:1], in0=negm[:, 0:1], scalar1=-scale, scalar2=None, op0=ALU.mult),
                          reads=[B_negm], writes=[B_negm])
                    yield

                def vgen(h):
                    for g in range(NKT // 8):
                        bi = PB if g % 2 == 0 else OB
                        for j in range(8):
                            kt = g * 8 + j
                            mm_group(bi, 64, [(ckvT[:, k, kt * 128:(kt + 1) * 128], wukv[:, k, h * 128 + 64:h * 128 + 128]) for k in range(2)],
                                     [B_wukv, B_ckv], c0=j * 64)
                        srcv = bank(bi)[:, 0:512].rearrange("p (j d) -> p j d", d=64)
                        copy_op("dve" if g % 2 == 0 else "act", Vt[:, g * 8:g * 8 + 8, 0:64], srcv, [PSB[bi]], [B_V])

                def attention(h):
                    Kc, B_Kn_c, B_Kpe_c = Ks[h % 2], B_Kns[h % 2], B_Kpes[h % 2]
                    Qc, B_Qc = Qs[h % 2], B_Qs[h % 2]
                    negm, B_negm = negms[h % 2], B_negms[h % 2]
                    groups = []
                    kt0 = 0
                    while kt0 < NKT:
                        n = min(3, NKT - kt0)
                        groups.append((kt0, n))
                        kt0 += n
                    NG = len(groups)
                    for qi, (T0, W) in enumerate(own_tiles):
                        def s_group(g, T0=T0, W=W):
                            sl = g % 2
                            kt0, n = groups[g]
                            for j in range(n):
                                kt = kt0 + j
                                sc.op("pe", (lambda e, sl=sl, j=j, kt=kt: e.matmul(bank(3 * sl + j)[:, 0:W], Kc[0:96, kt * 128:(kt + 1) * 128],
                                                                                      Qc[0:96, T0:T0 + W], start=True, stop=True)),
                                      reads=[B_Kn_c, B_Kpe_c, B_Qc], writes=[PSB[3 * sl + j]])
                            srcp = psum[:, 3 * sl * 512:(3 * sl + n) * 512].rearrange("p (j w) -> p j w", w=512)[:, :, 0:W]
                            pi = g % 3
                            sc.op("act", (lambda e, pi=pi, srcp=srcp, n=n: e.activation(out=Pb[pi][:, 0:n, 0:W], in_=srcp, func=AF.Exp, bias=negm[:, 0:1], scale=scale)),
                                  reads=[PSB[3 * sl + j] for j in range(n)] + [B_negm], writes=[B_P[pi]])

                        def pv_group(g, T0=T0, W=W):
                            pi = g % 3
                            kt0, n = groups[g]
                            for j in range(n):
                                kt = kt0 + j
                                sc.op("pe", (lambda e, pi=pi, j=j, kt=kt: e.matmul(bank(OB)[:, 0:W], Vt[:, kt, :], Pb[pi][:, j, 0:W],
                                                                                      start=(kt == 0), stop=(kt == NKT - 1))),
                                      reads=[B_V, B_P[pi]], writes=[PSB[OB]])
                        s_group(0)
                        yield
                        if NG > 1:
                            s_group(1)
                            yield
                        for g in range(NG):
                            if g + 2 < NG:
                                s_group(g + 2)
                            pv_group(g)
                            yield
                        sc.op("dve", (lambda e, W=W: e.reciprocal(out=rec[64:128, 0:W], in_=bank(OB)[64:128, 0:W])), reads=[PSB[OB]], writes=[B_rec])
                        p0 = (h % 2) * 64
                        sc.op("dve", (lambda e, T0=T0, W=W, p0=p0: e.tensor_tensor(out=ybT[p0:p0 + 64, h // 2, T0:T0 + W], in0=bank(OB)[0:64, 0:W],
                                                                                  in1=rec[64:128, 0:W], op=ALU.mult)),
                              reads=[PSB[OB], B_rec], writes=[B_yb])

                for _ in prologue(0):
                    pass
                for h in range(NH):
                    vgen(h)
                    ga = attention(h)
                    gp = prologue(h + 1) if h + 1 < NH else None
                    cnt = 0
                    for _ in ga:
                        cnt += 1
                        if gp is not None and cnt % 3 == 0:
                            try:
                                next(gp)
                            except StopIteration:
                                gp = None
                    if gp is not None:
                        for _ in gp:
                            pass
                sc.flush()

        with ExitStack() as pc:
            XW = 16 + TW + 1
            xres = [sb(pc, "xres%d" % i, [128, 8, XW], F32) for i in range(2)]
            B_xres = [sc.buf("xres%d" % i) for i in range(2)]
            NWB = 3
            wbuf = [sb(pc, "wbuf%d" % i, [128, 4096], BF16) for i in range(NWB)]
            B_wbuf = [sc.buf("wbuf%d" % i) for i in range(NWB)]
            yaT = sb(pc, "yaT", [128, 4, TW], BF16)
            B_ya = sc.buf("ya")
            gbt = sb(pc, "gbt", [128, 4, 1 + TW], BF16)
            B_gb = sc.buf("gb")
            zt = sb(pc, "zt", [128, 4, 2 + TW], BF16)
            B_z = sc.buf("z")
            acc = [sb(pc, "acc%d" % i, [128, TW], F32) for i in range(2)]
            B_acc = [sc.buf("acc%d" % i) for i in range(2)]
            ubuf = [sb(pc, "ubuf%d" % i, [128, 8, 31 + TW], BF16) for i in range(2)]
            B_u = [sc.buf("u%d" % i) for i in range(2)]
            csq = sb(pc, "csq", [128, 8, TW], BF16)
            B_csq = sc.buf("csq")
            mean = sb(pc, "mean", [128, TW], F32)
            B_mean = sc.buf("mean")

            class Ctx:
                pass
            ctxs = []
            for nm in ("S", "T"):
                cx = Ctx()
                cx.sqh = sb(pc, "sqh" + nm, [128, 8, TW + 1], BF16)
                cx.B_sq = sc.buf("sq" + nm)
                cx.B_h = cx.B_sq
                cx.rstd = sb(pc, "rstd" + nm, [128, TW + 1], F32)
                cx.B_rstd = sc.buf("rstd" + nm)
                cx.mT = sb(pc, "mT" + nm, [128, 8, TW + 1], F32)
                cx.B_m = sc.buf("m" + nm)
                cx.hid = sb(pc, "hid" + nm, [128, 16, TW], BF16)
                cx.B_hid = sc.buf("hid" + nm)
                ctxs.append(cx)
            CS, CT = ctxs

            class WStream:
                def __init__(self):
                    self.n = 0

                def get(self, name, blk, nblk_elems):
                    i = self.n % NWB
                    self.n += 1
                    o = woff[name][0] + blk * nblk_elems
                    sc.op("sp", (lambda e, i=i, o=o, n=nblk_elems: e.dma_start(out=wbuf[i][:, 0:n], in_=wbf[:, o:o + n])),
                          writes=[B_wbuf[i]], dma_key="wb%d" % i, extra_dma={"cv_" + name: sc.dsems["cv_" + name][1]})
                    return wbuf[i], B_wbuf[i]
            ws = WStream()

            def dense(wname, nk, nm, rhs_fn, rhs_bufs, W, consume, blk0=0, per=None):
                if per is None:
                    per = 4096 // (nk * 128)
                for b0 in range(0, nm, per):
                    nb = min(per, nm - b0)
                    wt, B_wt = ws.get(wname, blk0 + b0 // per, per * nk * 128)
                    for j in range(nb):
                        m = b0 + j
                        bi = next_bank()
                        mm_group(bi, W, [(wt[:, (j * nk + k) * 128:(j * nk + k + 1) * 128], rhs_fn(k)) for k in range(nk)], rhs_bufs + [B_wt])
                        consume(m, bi)
                    yield (nb * nk * 0.2, 1.0)

            def rmsnorm_pre(cx, xsrc, B_x, W, gidx, presq=False):
                sqh, rstd = cx.sqh, cx.rstd
                if not presq:
                    for hf in range(2):
                        sc.op("act", (lambda e, hf=hf: e.activation(out=sqh[:, 4 * hf:4 * hf + 4, 0:W], in_=xsrc(slice(4 * hf, 4 * hf + 4)), func=AF.Square)),
                              reads=[B_x, cx.B_h], writes=[cx.B_sq])
                    yield (0.0, 5.0)
                else:
                    yield (0.0, 2.0)
                stat_rstd([sqh[:, c, 0:W] for c in range(8)], [cx.B_sq], W, float(D), rstd[:, 0:W], cx.B_rstd)
                for c in range(8):
                    sc.op("dve", (lambda e, c=c: e.scalar_tensor_tensor(out=sqh[:, c, 0:W], in0=xsrc(c), scalar=ccol("gains", gidx * 8 + c), in1=rstd[:, 0:W],
                                                                         op0=ALU.mult, op1=ALU.mult)),
                          reads=[B_x, cx.B_rstd, B_cp, cx.B_sq], writes=[cx.B_h])
                yield (1.6, 8.5)

            def post_norm_residual(cx, xdst, B_x, W, gidx, presq=False, sq_next=False):
                sqh, rstd, mT = cx.sqh, cx.rstd, cx.mT
                if not presq:
                    for hf in range(2):
                        sc.op("act", (lambda e, hf=hf: e.activation(out=sqh[:, 4 * hf:4 * hf + 4, 0:W], in_=mT[:, 4 * hf:4 * hf + 4, 0:W], func=AF.Square)),
                              reads=[cx.B_m, cx.B_h], writes=[cx.B_sq])
                    yield (0.0, 5.0)
                else:
                    yield (0.0, 2.0)
                stat_rstd([sqh[:, c, 0:W] for c in range(8)], [cx.B_sq], W, float(D), rstd[:, 0:W], cx.B_rstd)
                for c in range(8):
                    sc.op("dve", (lambda e, c=c: e.scalar_tensor_tensor(out=mT[:, c, 0:W], in0=mT[:, c, 0:W], scalar=ccol("gains", gidx * 8 + c), in1=rstd[:, 0:W],
                                                                         op0=ALU.mult, op1=ALU.mult)),
                          reads=[cx.B_m, cx.B_rstd, B_cp], writes=[cx.B_m])
                    aeng = "pool" if c % 3 != 2 else "dve"
                    sc.op(aeng, (lambda e, c=c: e.tensor_tensor(out=xdst(c), in0=xdst(c), in1=mT[:, c, 0:W], op=ALU.add)),
                          reads=[cx.B_m, B_x], writes=[B_x])
                    if sq_next:
                        sc.op("act", (lambda e, c=c: e.activation(out=sqh[:, c, 0:W], in_=xdst(c), func=AF.Square)), reads=[B_x, cx.B_h], writes=[cx.B_sq])
                yield (1.6, 10.0)

            def mlp_half(cx, W, layer, half):
                sqh, mT, hid = cx.sqh, cx.mT, cx.hid

                def cons1(m, bi):
                    sc.op("act", (lambda e: e.activation(out=hid[:, m, 0:W], in_=bank(bi)[:, 0:W], func=AF.Relu)), reads=[PSB[bi]], writes=[cx.B_hid])
                    sc.op("pool", (lambda e: e.tensor_tensor(out=hid[:, m, 0:W], in0=hid[:, m, 0:W], in1=hid[:, m, 0:W], op=ALU.mult)),
                          reads=[cx.B_hid], writes=[cx.B_hid])
                yield from dense("m%dw1" % layer, 8, 16, (lambda k: sqh[:, k, 0:W]), [cx.B_h], W, cons1, blk0=half * 4)

                def cons2(m, bi):
                    if half == 0:
                        copy_op(evac_eng(), mT[:, m, 0:W], bank(bi)[:, 0:W], [PSB[bi]], [cx.B_m])
                    else:
                        sc.op("dve", (lambda e: e.tensor_tensor(out=mT[:, m, 0:W], in0=bank(bi)[:, 0:W], in1=mT[:, m, 0:W], op=ALU.add)),
                              reads=[PSB[bi], cx.B_m], writes=[cx.B_m])
                        sc.op("act", (lambda e: e.activation(out=sqh[:, m, 0:W], in_=mT[:, m, 0:W], func=AF.Square)), reads=[cx.B_m, cx.B_h], writes=[cx.B_sq])
                yield from dense("m%dw2" % layer, 16, 8, (lambda k: hid[:, k, 0:W]), [cx.B_hid], W, cons2, blk0=half * 4)

            for c in range(2):
                sc.op("pool", lambda e, c=c: e.memset(ubuf[c][:, :, 0:31], 0.0), writes=[B_u[c]])
            sc.op("pool", lambda e: e.memset(zt[:, :, 0:1], 0.0), writes=[B_z])
            sc.op("pool", lambda e: e.memset(xres[0][:, :, 0:16], 0.0), writes=[B_xres[0]])

            ntl = len(own_tiles)

            def pipe_L0(ti):
                cx = CS
                T0, W = own_tiles[ti]
                xr = xres[ti % 2]
                B_xr = B_xres[ti % 2]
                ub = ubuf[ti % 2]
                B_ub = B_u[ti % 2]
                W1 = W + 1
                sqh, mT = cx.sqh, cx.mT
                x0 = lambda c: xr[:, c, 16:16 + W]
                x01 = lambda c: xr[:, c, 16:16 + W1]
                for hf in range(2):
                    sc.op("sp", (lambda e, hf=hf: e.dma_start(out=xr[:, 4 * hf:4 * hf + 4, 16:16 + W1], in_=xT[:, 4 * hf:4 * hf + 4, T0:T0 + W1])),
                          writes=[B_xr], dma_key="xr%d" % (ti % 2))
                yield (0.0, 12.0)
                yield from rmsnorm_pre(cx, x01, B_xr, W1, 0)
                gcs = mT

                def cons_g(m, bi):
                    grp, c = m // 4, m % 4
                    if grp == 0:
                        copy_op(evac_eng(), gbt[:, c, 0:W1], bank(bi)[:, 0:W1], [PSB[bi]], [B_gb])
                    elif grp == 1:
                        copy_op("act", gcs[:, c, 0:W1], bank(bi)[:, 0:W1], [PSB[bi]], [cx.B_m])
                    else:
                        sc.op("dve", (lambda e: e.tensor_tensor(out=zt[:, c, 1:1 + W1], in0=bank(bi)[:, 0:W1], in1=gcs[:, c, 0:W1], op=ALU.mult)),
                              reads=[PSB[bi], cx.B_m], writes=[B_z])
                yield from dense("w_inG", 8, 12, (lambda k: sqh[:, k, 0:W1]), [cx.B_h], W1, cons_g)
                for c in range(4):
                    a = acc[c % 2]
                    B_a = B_acc[c % 2]
                    sc.op("dve", (lambda e, c=c, a=a: e.tensor_scalar(out=a[:, 0:W], in0=zt[:, c, 0:W], scalar1=ccol("sck", c * 3 + 0), scalar2=None, op0=ALU.mult)),
                          reads=[B_z, B_cp], writes=[B_a])
                    for k in (1, 2):
                        sc.op("dve", (lambda e, c=c, a=a, k=k: e.scalar_tensor_tensor(out=a[:, 0:W], in0=zt[:, c, k:k + W], scalar=ccol("sck", c * 3 + k),
                                                                                      in1=a[:, 0:W], op0=ALU.mult, op1=ALU.add)),
                              reads=[B_z, B_cp, B_a], writes=[B_a])
                    sc.op("pool", (lambda e, c=c, a=a: e.tensor_tensor(out=yaT[:, c, 0:W], in0=a[:, 0:W], in1=gbt[:, c, 0:W], op=ALU.mult)),
                          reads=[B_a, B_gb], writes=[B_ya])
                sc.op("pool", (lambda e: e.tensor_copy(out=zt[:, :, 0:1], in_=zt[:, :, W:W + 1])), reads=[B_z], writes=[B_z])
                yield (0.0, 9.0)

                def rhs_mix(k):
                    return yaT[:, k, 0:W] if k < 4 else ybT[:, k - 4, T0:T0 + W]

                def cons_m(m, bi):
                    copy_op(evac_eng(), mT[:, m, 0:W], bank(bi)[:, 0:W], [PSB[bi]], [cx.B_m])
                    sc.op("act", (lambda e: e.activation(out=sqh[:, m, 0:W], in_=mT[:, m, 0:W], func=AF.Square)), reads=[cx.B_m, cx.B_h], writes=[cx.B_sq])
                yield from dense("w_out", 8, 8, rhs_mix, [B_ya, B_yb], W, cons_m)
                yield from post_norm_residual(cx, x0, B_xr, W, 1, presq=True, sq_next=True)
                yield from rmsnorm_pre(cx, x0, B_xr, W, 2, presq=True)
                yield from mlp_half(cx, W, 0, 0)
                yield from mlp_half(cx, W, 0, 1)
                yield from post_norm_residual(cx, x0, B_xr, W, 3, presq=True, sq_next=True)
                yield from rmsnorm_pre(cx, x0, B_xr, W, 4, presq=True)

                abank = [None]

                def cons_p(m, bi):
                    c, isg = m // 2, m % 2
                    if isg == 0:
                        abank[0] = bi
                    else:
                        ba = abank[0]
                        g = acc[c % 2]
                        B_g = B_acc[c % 2]
                        sc.op("act", (lambda e: e.activation(out=g[:, 0:W], in_=bank(bi)[:, 0:W], func=AF.Sigmoid, bias=ccol("bpw1", 8 + c), scale=1.0)),
                              reads=[PSB[bi], B_cp], writes=[B_g])
                        sc.op("dve", (lambda e: e.scalar_tensor_tensor(out=ub[:, c, 31:31 + W], in0=bank(ba)[:, 0:W], scalar=ccol("bpw1", c), in1=g[:, 0:W],
                                                                       op0=ALU.add, op1=ALU.mult)),
                              reads=[PSB[ba], B_g, B_cp], writes=[B_ub])
                yield from dense("pw1", 8, 16, (lambda k: sqh[:, k, 0:W]), [cx.B_h], W, cons_p)
                yield (0.0, 3.0)

            def pipe_L1(ti):
                cx = CT
                T0, W = own_tiles[ti]
                xr = xres[ti % 2]
                B_xr = B_xres[ti % 2]
                ub = ubuf[ti % 2]
                B_ub = B_u[ti % 2]
                sqh, mT, rstd = cx.sqh, cx.mT, cx.rstd
                x1 = lambda c: xr[:, c, 0:W]
                cacc = mT
                for c in range(8):
                    wt, B_wt = ws.get("dwd", c, 31 * 128)
                    bi = next_bank()
                    mm_group(bi, W, [(wt[:, k * 128:(k + 1) * 128], ub[:, c, k:k + W]) for k in range(31)], [B_ub, B_wt])
                    sc.op("dve", (lambda e, c=c, bi=bi: e.tensor_scalar(out=cacc[:, c, 0:W], in0=bank(bi)[:, 0:W], scalar1=ccol("bdw", c), scalar2=None, op0=ALU.add)),
                          reads=[PSB[bi], B_cp], writes=[cx.B_m])
                    sc.op("act", (lambda e, c=c: e.activation(out=sqh[:, c, 0:W], in_=cacc[:, c, 0:W], func=AF.Copy)), reads=[cx.B_m, cx.B_h], writes=[cx.B_sq])
                    sc.op("act", (lambda e, c=c: e.activation(out=csq[:, c, 0:W], in_=cacc[:, c, 0:W], func=AF.Square)), reads=[cx.B_m], writes=[B_csq])
                    yield (6.2, 1.0)
                yield (0.0, 3.0)
                b_mean = next_bank()
                mm_group(b_mean, W, [(ones[:, :], sqh[:, c, 0:W]) for c in range(8)], [cx.B_sq, B_ones])
                b_ex2 = next_bank()
                mm_group(b_ex2, W, [(ones[:, :], csq[:, c, 0:W]) for c in range(8)], [B_csq, B_ones])
                sc.op("dve", (lambda e: e.tensor_scalar(out=mean[:, 0:W], in0=bank(b_mean)[:, 0:W], scalar1=1.0 / D, scalar2=None, op0=ALU.mult)),
                      reads=[PSB[b_mean]], writes=[B_mean])
                sc.op("dve", (lambda e: e.tensor_tensor(out=rstd[:, 0:W], in0=mean[:, 0:W], in1=mean[:, 0:W], op=ALU.mult)), reads=[B_mean], writes=[cx.B_rstd])
                sc.op("dve", (lambda e: e.scalar_tensor_tensor(out=rstd[:, 0:W], in0=bank(b_ex2)[:, 0:W], scalar=1.0 / D, in1=rstd[:, 0:W],
                                                               op0=ALU.mult, op1=ALU.subtract)), reads=[PSB[b_ex2], cx.B_rstd], writes=[cx.B_rstd])
                sc.op("act", (lambda e: e.activation(out=rstd[:, 0:W], in_=rstd[:, 0:W], func=AF.Ln, bias=eps_col[:, 0:1], scale=1.0)),
                      reads=[cx.B_rstd, B_ones], writes=[cx.B_rstd])
                sc.op("act", (lambda e: e.activation(out=rstd[:, 0:W], in_=rstd[:, 0:W], func=AF.Exp, scale=-0.5)), reads=[cx.B_rstd], writes=[cx.B_rstd])
                for c in range(8):
                    sc.op("pool", (lambda e, c=c: e.tensor_tensor(out=cacc[:, c, 0:W], in0=cacc[:, c, 0:W], in1=mean[:, 0:W], op=ALU.subtract)),
                          reads=[cx.B_m, B_mean], writes=[cx.B_m])
                    sc.op("dve", (lambda e, c=c: e.tensor_tensor(out=cacc[:, c, 0:W], in0=cacc[:, c, 0:W], in1=rstd[:, 0:W], op=ALU.mult)),
                          reads=[cx.B_m, cx.B_rstd], writes=[cx.B_m])
                    sc.op("act", (lambda e, c=c: e.activation(out=sqh[:, c, 0:W], in_=cacc[:, c, 0:W], func=AF.Silu, bias=ccol("lnb", c), scale=ccol("lng", c))),
                          reads=[cx.B_m, B_cp, cx.B_sq], writes=[cx.B_h])
                yield (3.2, 14.0)

                def cons_p2(m, bi):
                    sc.op("dve", (lambda e: e.tensor_scalar(out=mT[:, m, 0:W], in0=bank(bi)[:, 0:W], scalar1=ccol("bpw2", m), scalar2=None, op0=ALU.add)),
                          reads=[PSB[bi], B_cp], writes=[cx.B_m])
                yield from dense("pw2", 8, 8, (lambda k: sqh[:, k, 0:W]), [cx.B_h], W, cons_p2)
                yield from post_norm_residual(cx, x1, B_xr, W, 5, sq_next=True)
                yield from rmsnorm_pre(cx, x1, B_xr, W, 6, presq=True)
                yield from mlp_half(cx, W, 1, 0)
                yield from mlp_half(cx, W, 1, 1)
                yield from post_norm_residual(cx, x1, B_xr, W, 7, presq=True)
                c_lo = 16 if ti == 0 else 0
                tok0 = T0 - 16 + c_lo
                ncol = W - c_lo
                for hf in range(2):
                    sc.op("sp", (lambda e, hf=hf: e.dma_start(out=outT[:, 4 * hf:4 * hf + 4, tok0:tok0 + ncol], in_=xr[:, 4 * hf:4 * hf + 4, c_lo:c_lo + ncol])),
                          reads=[B_xr], dma_key="out")
                yield (0.0, 6.0)

            INF = 1e30
            gS = gT = None
            nS = nT = 0
            doneS = doneT = 0
            readyS = readyT = 0.0
            s_done_t = {}
            t_done_t = {}
            carried = 0
            t_pe = 0.0
            last = "T"
            while doneT < ntl:
                if gS is None and nS < ntl and (nS < 2 or doneT >= nS - 1):
                    gS = pipe_L0(nS)
                    if nS >= 2:
                        readyS = max(readyS, t_done_t[nS - 2])
                    nS += 1
                if gT is None and nT < ntl and doneS > nT and carried >= nT:
                    gT = pipe_L1(nT)
                    readyT = max(readyT, s_done_t[nT])
                    nT += 1
                cS = max(readyS, t_pe) if gS is not None else INF
                cT = max(readyT, t_pe) if gT is not None else INF
                assert cS < INF or cT < INF
                if cS < cT or (cS == cT and last == "T"):
                    pick = "S"
                else:
                    pick = "T"
                last = pick
                if pick == "S":
                    try:
                        pe_us, lat = next(gS)
                        t_pe = max(t_pe, readyS) + pe_us
                        readyS = t_pe + lat
                    except StopIteration:
                        gS = None
                        s_done_t[doneS] = readyS
                        doneS += 1
                else:
                    try:
                        pe_us, lat = next(gT)
                        t_pe = max(t_pe, readyT) + pe_us
                        readyT = t_pe + lat
                    except StopIteration:
                        gT = None
                        t_done_t[doneT] = readyT
                        doneT += 1
                while carried < ntl - 1 and doneS > carried and doneT >= carried:
                    it = carried
                    T0, W = own_tiles[it]
                    xr, nx = xres[it % 2], xres[(it + 1) % 2]
                    ub, nu = ubuf[it % 2], ubuf[(it + 1) % 2]
                    sc.op("pool", (lambda e, xr=xr, nx=nx, W=W: e.tensor_copy(out=nx[:, :, 0:16], in_=xr[:, :, W:W + 16])),
                          reads=[B_xres[it % 2]], writes=[B_xres[(it + 1) % 2]])
                    sc.op("pool", (lambda e, ub=ub, nu=nu, W=W: e.tensor_copy(out=nu[:, :, 0:31], in_=ub[:, :, W:W + 31])),
                          reads=[B_u[it % 2]], writes=[B_u[(it + 1) % 2]])
                    carried += 1
            sc.flush(final=True)
    return nc


def prep_core(inp, core, S):
    b, half = core // 2, core % 2
    idx = np.arange(S) if half == 0 else np.arange(S - 1, -1, -1)
    x = np.asarray(inp["x"][b], dtype=np.float32)[idx]
    xT = np.ascontiguousarray(x.T.reshape(8, 128, S).transpose(1, 0, 2))
    pos = np.ascontiguousarray(np.asarray(inp["positions"][b])[idx].astype(np.int32).reshape(1, S))
    return xT, pos, half


def prep_shared(inp, half):
    woff, WTOT = weight_layout()
    coff, CTOT = const_layout()
    f = lambda k: np.asarray(inp[k], dtype=np.float32)
    wp = np.zeros((128, WTOT), np.float32)

    def put(name, arr):
        o, n = woff[name]
        assert arr.shape == (128, n), (name, arr.shape, n)
        wp[:, o:o + n] = arr
    w_in = f("even_w_in")[0]
    kr = np.zeros((D, 128), np.float32)
    kr[:, 64:96] = w_in[:, 2176:2208]
    krp = np.zeros((D, 128), np.float32)
    krp[:, 64:80] = w_in[:, 2192:2208]
    krp[:, 80:96] = w_in[:, 2176:2192]
    winA = np.concatenate([w_in[:, 1536:2176], kr, krp], axis=1)
    put("w_inA", pack_lhsT(winA, 8, 7))
    put("w_inG", pack_lhsT(w_in[:, 0:1536], 8, 12))
    w_uq = f("even_w_uq")[0].reshape(384, 8, 96)
    wq = np.zeros((384, 8, 2, 96), np.float32)
    wq[:, :, 0, :] = w_uq
    wq[:, :, 1, 64:80] = w_uq[:, :, 80:96]
    wq[:, :, 1, 80:96] = w_uq[:, :, 64:80]
    wq = wq.reshape(3, 128, 8, 2, 96).transpose(1, 2, 3, 0, 4)
    put("w_uq", np.ascontiguousarray(wq).reshape(128, -1))
    w_ukv = f("even_w_ukv")[0]
    put("w_ukv", np.ascontiguousarray(w_ukv.reshape(2, 128, 1024).transpose(1, 0, 2)).reshape(128, -1))
    put("w_out", pack_lhsT(f("even_w_out")[0], 8, 8))
    put("m0w1", pack_lhsT(f("mlp_w1")[0], 8, 32))
    put("m0w2", pack_w2(f("mlp_w2")[0]))
    pw1 = f("odd_w_pw1")[0].reshape(D, 2, 8, 128).transpose(0, 2, 1, 3).reshape(D, 2048)
    put("pw1", pack_lhsT(pw1, 8, 16))
    put("pw2", pack_lhsT(f("odd_w_pw2")[0], 8, 8))
    put("m1w1", pack_lhsT(f("mlp_w1")[1], 8, 32))
    put("m1w2", pack_w2(f("mlp_w2")[1]))

    cpk = np.zeros((128, CTOT), np.float32)

    def putc(name, arr):
        o, n = coff[name]
        assert arr.shape == (128, n), (name, arr.shape, n)
        cpk[:, o:o + n] = arr
    putc("gains", colpack(f("sandwich_gains").reshape(-1)))
    putc("qg", colpack(f("even_q_norm")[0]))
    putc("kvg", colpack(f("even_kv_norm")[0]))
    sck = f("even_sc_kernel")[0]
    wdw = f("odd_w_dw")[0]
    if half == 1:
        sck = sck[::-1]
        wdw = wdw[::-1]
    putc("sck", np.ascontiguousarray(sck.reshape(3, 4, 128).transpose(2, 1, 0)).reshape(128, 12))
    dwd = np.zeros((128, 8, 31, 128), np.float32)
    pidx = np.arange(128)
    dwd[pidx, :, :, pidx] = wdw.reshape(31, 8, 128).transpose(2, 1, 0)
    put("dwd", dwd.reshape(128, -1))
    putc("wdw", np.ascontiguousarray(wdw.reshape(31, 8, 128).transpose(2, 1, 0)).reshape(128, 248))
    putc("bpw1", colpack(f("odd_b_pw1")[0]))
    putc("bdw", colpack(f("odd_b_dw")[0]))
    putc("lng", colpack(f("odd_ln_g")[0]))
    putc("lnb", colpack(f("odd_ln_b")[0]))
    putc("bpw2", colpack(f("odd_b_pw2")[0]))
    half_d = 16
    inv_freq = (1.0 / (np.float32(10000.0) ** (np.arange(half_d, dtype=np.float32) / np.float32(half_d)))).astype(np.float32)
    invf = np.zeros((128, 1), np.float32)
    invf[64:80, 0] = inv_freq
    invf[80:96, 0] = inv_freq
    putc("invf", invf)
    sgn = np.ones((128, 1), np.float32)
    sgn[64:80, 0] = -1.0
    putc("sgn", sgn)
    return wp, cpk


_NC_CACHE = {}


def kernel(**inputs):
    x = np.asarray(inputs["x"])
    B, S, _ = x.shape
    ncores = 2 * B
    if S not in _NC_CACHE:
        _NC_CACHE[S] = build(S)
    nc = _NC_CACHE[S]
    shared = [prep_shared(inputs, h) for h in range(2)]
    in_maps = []
    for c in range(ncores):
        xT, pos, half = prep_core(inputs, c, S)
        wp, cpk = shared[half]
        in_maps.append({"xT": xT, "pos": pos, "wpack": wp, "cpack": cpk})
    res = run_bass_kernel_spmd(nc, in_maps, core_ids=list(range(ncores)))
    OWN = S // 2
    out = np.empty((B, S, D), np.float32)
    for c in range(ncores):
        b, half = c // 2, c % 2
        oT = np.asarray(res.results[c]["outT"])
        o = oT.transpose(2, 1, 0).reshape(OWN, D)
        if half == 0:
            out[b, :OWN] = o
        else:
            out[b, OWN:] = o[::-1]
    return out
```

```python
import numpy as np
from contextlib import ExitStack
import concourse.bass as bass
import concourse.mybir as mybir
from concourse.bass_utils import run_bass_kernel_spmd

F32 = mybir.dt.float32
BF16 = mybir.dt.bfloat16
I32 = mybir.dt.int32
AF = mybir.ActivationFunctionType
ALU = mybir.AluOpType
AX = mybir.AxisListType

D = 1024
NH = 8
HALO = 16
TW = 464
EPS = 1e-6
SEQ_FULL = 8192
SEM_LIM = 30000


class Buf:
    __slots__ = ("name", "last_w", "readers")

    def __init__(self, name):
        self.name = name
        self.last_w = None
        self.readers = []


class Op:
    __slots__ = ("eng", "fn", "deps", "dma", "sig", "idx", "dma_deps", "bar")

    def __init__(self, eng, fn, dma):
        self.eng = eng
        self.fn = fn
        self.dma = dma
        self.deps = set()
        self.dma_deps = {}
        self.sig = None
        self.bar = None


ENGS = ("pe", "act", "dve", "pool", "sp")


class Sched:
    def __init__(self, nc, stack):
        self.nc = nc
        self.stack = stack
        self.ops = []
        self.allbufs = []
        self.sig_count = {e: 0 for e in ENGS}
        self.esems = {e: [] for e in ENGS}
        self.dsems = {}
        self.known = {e: {} for e in ENGS}
        self.known_d = {e: {} for e in ENGS}
        self.pending_bar = None
        self.out_sems = []

    def buf(self, name):
        b = Buf(name)
        self.allbufs.append(b)
        return b

    def dsem(self, key):
        if key not in self.dsems:
            s = self.stack.enter_context(self.nc.semaphore("d_" + key))
            self.dsems[key] = [s, 0]
        return self.dsems[key]

    def esem(self, eng, epoch):
        lst = self.esems[eng]
        while len(lst) <= epoch:
            lst.append(self.stack.enter_context(self.nc.semaphore("e_%s_%d" % (eng, len(lst)))))
        return lst[epoch]

    def op(self, eng, fn, reads=(), writes=(), dma_key=None, extra_dma=None):
        o = Op(eng, fn, None)
        oid = len(self.ops)
        if dma_key is not None:
            d = self.dsem(dma_key)
            d[1] += 16
            o.dma = (dma_key, d[1])
        deps = []
        for b in reads:
            if b.last_w is not None:
                deps.append((b.last_w, "raw"))
        for b in writes:
            if b.last_w is not None:
                deps.append((b.last_w, "waw"))
            for r in b.readers:
                deps.append((r, "war"))
        for (pid, kind) in deps:
            if pid == oid:
                continue
            p = self.ops[pid]
            if p.dma is not None:
                k = p.dma[0]
                v = self.dsems[k][1] if o.dma is None or o.dma[0] != k else p.dma[1]
                o.dma_deps[k] = max(o.dma_deps.get(k, 0), v)
            else:
                if p.eng == eng and o.dma is None:
                    if eng == "pe":
                        continue
                o.deps.add(pid)
        if extra_dma:
            for k, v in extra_dma.items():
                o.dma_deps[k] = max(o.dma_deps.get(k, 0), v)
        for b in reads:
            b.readers.append(oid)
        for b in writes:
            b.last_w = oid
            b.readers = []
        self.ops.append(o)
        return oid

    def flush(self, final=False):
        ops = self.ops
        nc = self.nc
        needed = set()
        for o in ops:
            for pid in o.deps:
                needed.add(pid)
        last_of = {}
        for i, o in enumerate(ops):
            if o.dma is None:
                last_of[o.eng] = i
        for e, i in last_of.items():
            needed.add(i)
        for i, o in enumerate(ops):
            if i in needed and o.dma is None:
                self.sig_count[o.eng] += 1
                o.sig = self.sig_count[o.eng]
        bar = self.pending_bar
        per_eng = {e: [] for e in ENGS}
        for i, o in enumerate(ops):
            per_eng[o.eng].append(o)
        sched = self

        def emit(engname, engine):
            known = sched.known[engname]
            known_d = sched.known_d[engname]

            def wait_sig(e2, sig):
                if known.get(e2, 0) >= sig:
                    return
                known[e2] = sig
                ep, val = (sig - 1) // SEM_LIM, (sig - 1) % SEM_LIM + 1
                engine.wait_ge(sched.esem(e2, ep), val)

            def wait_dma(k, v):
                if known_d.get(k, 0) >= v:
                    return
                known_d[k] = v
                engine.wait_ge(sched.dsems[k][0], v)

            if bar is not None:
                for e2, sig in bar[0].items():
                    if e2 != engname and sig > 0:
                        wait_sig(e2, sig)
                for k, v in bar[1].items():
                    if not k.startswith("cv_"):
                        wait_dma(k, v)
            for o in per_eng[engname]:
                for pid in sorted(o.deps):
                    p = ops[pid]
                    wait_sig(p.eng, p.sig)
                for k, v in o.dma_deps.items():
                    wait_dma(k, v)
                ins = o.fn(engine)
                if o.dma is not None:
                    ins.then_inc(sched.dsems[o.dma[0]][0], 16)
                elif o.sig is not None:
                    ep = (o.sig - 1) // SEM_LIM
                    ins.then_inc(sched.esem(o.eng, ep), 1)
            if final and engname == "sp":
                for k, d in sched.dsems.items():
                    wait_dma(k, d[1])

        for e in ENGS:
            self.esem(e, max(0, (self.sig_count[e] - 1)) // SEM_LIM)
        with nc.Block() as block:
            @block.tensor
            def _(eng):
                emit("pe", eng)

            @block.scalar
            def _(eng):
                emit("act", eng)

            @block.vector
            def _(eng):
                emit("dve", eng)

            @block.gpsimd
            def _(eng):
                emit("pool", eng)

            @block.sync
            def _(eng):
                emit("sp", eng)
        self.pending_bar = ({e: self.sig_count[e] for e in ENGS}, {k: d[1] for k, d in self.dsems.items()})
        self.ops = []
        for b in self.allbufs:
            b.last_w = None
            b.readers = []


def weight_layout():
    off = {}
    cur = 0

    def add(name, n):
        nonlocal cur
        off[name] = (cur, n)
        cur += n
    add("w_inA", 7 * 8 * 128)
    add("w_uq", 8 * 2 * 3 * 96)
    add("w_ukv", 2 * 1024)
    add("w_inG", 12 * 8 * 128)
    add("w_out", 8 * 8 * 128)
    add("m0w1", 32 * 8 * 128)
    add("m0w2", 8 * 32 * 128)
    add("pw1", 16 * 8 * 128)
    add("pw2", 8 * 8 * 128)
    add("m1w1", 32 * 8 * 128)
    add("m1w2", 8 * 32 * 128)
    add("dwd", 8 * 31 * 128)
    return off, cur


def const_layout():
    off = {}
    cur = 0

    def add(name, n):
        nonlocal cur
        off[name] = (cur, n)
        cur += n
    add("gains", 64)
    add("qg", 3)
    add("kvg", 2)
    add("sck", 12)
    add("bpw1", 16)
    add("wdw", 8 * 31)
    add("bdw", 8)
    add("lng", 8)
    add("lnb", 8)
    add("bpw2", 8)
    add("invf", 1)
    add("sgn", 1)
    return off, cur


def pack_lhsT(w, nk, nm, mw=128):
    return np.ascontiguousarray(w.reshape(nk, 128, nm, mw).transpose(1, 2, 0, 3)).reshape(128, -1)


def pack_w2(w):
    return np.concatenate([pack_lhsT(w[h * 2048:(h + 1) * 2048], 16, 8) for h in range(2)], axis=1)


def colpack(v):
    return np.ascontiguousarray(v.reshape(-1, 128).T)


def split_tiles(n, w):
    out = []
    t0 = 0
    while t0 < n:
        ww = min(w, n - t0)
        out.append((t0, ww))
        t0 += ww
    return out


def build(S):
    OWN = S // 2
    OWNH = OWN + HALO
    own_tiles = split_tiles(OWNH, TW)
    oth_tiles = [(OWNH + a, w) for (a, w) in split_tiles(S - OWNH, 512)]
    NKT = S // 128
    woff, WTOT = weight_layout()
    coff, CTOT = const_layout()
    scale = float((64 + 32) ** -0.5)

    nc = bass.Bass("TRN2", target_bir_lowering=False)
    xT = nc.dram_tensor("xT", [128, 8, S], F32, kind="ExternalInput").ap()
    posd = nc.dram_tensor("pos", [1, S], I32, kind="ExternalInput").ap()
    wpack = nc.dram_tensor("wpack", [128, WTOT], F32, kind="ExternalInput").ap()
    cpackd = nc.dram_tensor("cpack", [128, CTOT], F32, kind="ExternalInput").ap()
    wbf = nc.dram_tensor("wbf", [128, WTOT], BF16, kind="Internal").ap()
    outT = nc.dram_tensor("outT", [128, 8, OWN], F32, kind="ExternalOutput").ap()

    with ExitStack() as top:
        sc = Sched(nc, top)

        def sb(stack, name, shape, dt):
            return stack.enter_context(nc.sbuf_tensor(name, shape, dt))

        psum = top.enter_context(nc.psum_tensor("ps", [128, 8 * 512], F32))
        PSB = [sc.buf("psb%d" % i) for i in range(8)]
        ps_rr = [0]

        def bank(i):
            return psum[:, i * 512:(i + 1) * 512]

        def next_bank(lo=0, hi=8):
            i = lo + ps_rr[0] % (hi - lo)
            ps_rr[0] += 1
            return i

        cp = sb(top, "cp", [128, CTOT], F32)
        B_cp = sc.buf("cp")
        ones = sb(top, "ones", [128, 128], BF16)
        B_ones = sc.buf("ones")
        ybT = sb(top, "ybT", [128, 4, OWNH], BF16)
        B_yb = sc.buf("yb")

        def ccol(name, i=0, p0=0, p1=128):
            o = coff[name][0] + i
            return cp[p0:p1, o:o + 1]

        sc.op("sp", lambda e: e.dma_start(out=cp[:, :], in_=cpackd[:, :]), writes=[B_cp], dma_key="cp")
        sc.op("dve", lambda e: e.memset(ones[:, :], 1.0), writes=[B_ones])
        eps_col = sb(top, "eps_col", [128, 1], F32)
        sc.op("dve", lambda e: e.memset(eps_col[:, :], EPS), writes=[B_ones])
        B_wbf = {}
        order = ["w_inA", "w_uq", "w_ukv", "w_inG", "w_out", "m0w1", "m0w2", "pw1", "dwd", "pw2", "m1w1", "m1w2"]
        for name in order:
            o, n = woff[name]
            B_wbf[name] = sc.buf("wbf_" + name)
            step = 8192
            for a in range(0, n, step):
                b = min(n, a + step)
                sc.op("pool", (lambda e, a=a, b=b, o=o: e.dma_start(out=wbf[:, o + a:o + b], in_=wpack[:, o + a:o + b])),
                      writes=[B_wbf[name]], dma_key="cv_" + name)

        evac_rr = [0]

        def evac_eng():
            evac_rr[0] += 1
            return "act" if evac_rr[0] % 2 else "dve"

        def copy_op(eng, out, in_, reads, writes):
            if eng == "act":
                sc.op("act", lambda e: e.activation(out=out, in_=in_, func=AF.Copy), reads=reads, writes=writes)
            elif eng == "dve":
                sc.op("dve", lambda e: e.tensor_copy(out=out, in_=in_), reads=reads, writes=writes)
            else:
                sc.op("pool", lambda e: e.tensor_copy(out=out, in_=in_), reads=reads, writes=writes)

        def mm_group(bi, W, pairs, reads, M=128, c0=0):
            n = len(pairs)
            for i, (l, r) in enumerate(pairs):
                sc.op("pe", (lambda e, l=l, r=r, i=i: e.matmul(bank(bi)[0:M, c0:c0 + W], l, r, start=(i == 0), stop=(i == n - 1))),
                      reads=reads, writes=[PSB[bi]])

        def stat_rstd(sq_chunks, reads, W, dim, rstd_ap, B_rstd, mean_out=None):
            bi = next_bank()
            mm_group(bi, W, [(ones[:, :], s) for s in sq_chunks], reads + [B_ones])
            sc.op("act", lambda e: e.activation(out=rstd_ap, in_=bank(bi)[:, 0:W], func=AF.Ln, bias=eps_col[:, 0:1], scale=1.0 / dim),
                  reads=[PSB[bi], B_ones], writes=[B_rstd])
            sc.op("act", lambda e: e.activation(out=rstd_ap, in_=rstd_ap, func=AF.Exp, scale=-0.5), reads=[B_rstd], writes=[B_rstd])

        def rms_apply(nch, src, B_src, gname, gidx0, rstd_ap, B_rstd, dst, B_dst, W, engs=("dve",)):
            for c in range(nch):
                eng = engs[c % len(engs)]
                sc.op(eng, (lambda e, c=c: e.scalar_tensor_tensor(out=dst(c), in0=src(c), scalar=ccol(gname, gidx0 + c),
                                                                   in1=rstd_ap, op0=ALU.mult, op1=ALU.mult)),
                      reads=[B_src, B_rstd, B_cp], writes=[B_dst])

        with ExitStack() as ab:
            qnT = sb(ab, "qnT", [128, 3, OWNH], BF16)
            B_qn = sc.buf("qn")
            cosT = sb(ab, "cosT", [128, OWNH], BF16)
            sinT = sb(ab, "sinT", [128, OWNH], BF16)
            B_tab = sc.buf("tab")
            ckvT = sb(ab, "ckvT", [128, 2, S], BF16)
            B_ckv = sc.buf("ckv")
            Kt = sb(ab, "Kt", [128, S], BF16)
            B_Kpe = sc.buf("Kpe")
            B_Kn = sc.buf("Kn")
            with ExitStack() as pa:
                xt = [sb(pa, "xtA", [128, 4, 512], F32), sb(pa, "xtB", [128, 4, 512], F32)]
                B_xt = [sc.buf("xtA"), sc.buf("xtB")]
                sqa = sb(pa, "sqa", [128, 8, 512], BF16)
                B_sq = sc.buf("sq")
                hbuf = [sb(pa, "hbuf%d" % i, [128, 8, 512], BF16) for i in range(2)]
                B_hs = [sc.buf("h%d" % i) for i in range(2)]
                rstd = sb(pa, "rstd", [128, 512], F32)
                B_rstd = sc.buf("rstd")
                winA = sb(pa, "winA", [128, 7, 8, 128], BF16)
                B_winA = sc.buf("winA")
                lat = sb(pa, "lat", [128, 3, 512], F32)
                B_lat = sc.buf("lat")
                sq2 = sb(pa, "sq2", [128, 3, 512], BF16)
                B_sq2 = sc.buf("sq2")
                rstd2 = sb(pa, "rstd2", [128, 512], F32)
                B_rstd2 = sc.buf("rstd2")
                posi = sb(pa, "posi", [128, 512], I32)
                B_pos = sc.buf("pos")
                ang = sb(pa, "ang", [128, 512], F32)
                B_ang = sc.buf("ang")
                cts = [sb(pa, "ct%d" % i, [128, 512], F32) for i in range(2)]
                sts = [sb(pa, "st%d" % i, [128, 512], F32) for i in range(2)]
                B_css = [sc.buf("cs%d" % i) for i in range(2)]
                t1 = sb(pa, "t1", [128, 512], F32)
                t2 = sb(pa, "t2", [128, 512], F32)
                B_t12 = sc.buf("t12")
                k1, k2, B_k12 = lat[:, 0, :], lat[:, 1, :], B_lat
                o, n = woff["w_inA"]
                sc.op("sp", lambda e, o=o, n=n: e.dma_start(out=winA[:, :, :, :].rearrange("p m k j -> p (m k j)"), in_=wbf[:, o:o + n]),
                      reads=[B_wbf["w_inA"]], writes=[B_winA], dma_key="winA")

                R = slice(64, 96)
                PI = float(np.pi)
                MAGIC = 12582912.0
                C1 = 6.28125
                C2 = float(2 * np.pi - 6.28125)
                all_tiles = own_tiles + oth_tiles
                n_own = len(own_tiles)

                def gen_N(ti):
                    T0, W = all_tiles[ti]
                    hb, B_hb = hbuf[ti % 2], B_hs[ti % 2]
                    for hf in range(2):
                        sc.op("sp", (lambda e, hf=hf: e.dma_start(out=xt[hf][:, :, 0:W], in_=xT[:, 4 * hf:4 * hf + 4, T0:T0 + W])),
                              writes=[B_xt[hf]], dma_key="xt%d" % hf)
                    for hf in range(2):
                        sc.op("act", (lambda e, hf=hf: e.activation(out=sqa[:, 4 * hf:4 * hf + 4, 0:W], in_=xt[hf][:, :, 0:W], func=AF.Square)),
                              reads=[B_xt[hf]], writes=[B_sq])
                    yield
                    stat_rstd([sqa[:, c, 0:W] for c in range(8)], [B_sq], W, float(D), rstd[:, 0:W], B_rstd)
                    yield
                    for c in range(8):
                        sc.op("dve", (lambda e, c=c: e.scalar_tensor_tensor(out=hb[:, c, 0:W], in0=xt[c // 4][:, c % 4, 0:W],
                                                                             scalar=ccol("gains", c), in1=rstd[:, 0:W],
                                                                             op0=ALU.mult, op1=ALU.mult)),
                              reads=[B_xt[c // 4], B_rstd, B_cp], writes=[B_hb])
                    yield

                def gen_R(ti):
                    T0, W = all_tiles[ti]
                    ct, st, B_cs = cts[ti % 2], sts[ti % 2], B_css[ti % 2]
                    sc.op("sp", (lambda e: e.dma_start(out=posi[64:96, 0:W], in_=posd[0:1, T0:T0 + W].broadcast_to([32, W]))),
                          writes=[B_pos], dma_key="pos")
                    sc.op("dve", (lambda e: e.tensor_copy(out=ang[R, 0:W], in_=posi[R, 0:W])), reads=[B_pos], writes=[B_ang])
                    sc.op("dve", (lambda e: e.tensor_scalar(out=ang[R, 0:W], in0=ang[R, 0:W], scalar1=ccol("invf", 0, 64, 96), scalar2=None,
                                                            op0=ALU.mult)), reads=[B_ang, B_cp], writes=[B_ang])
                    sc.op("dve", (lambda e: e.tensor_scalar(out=t1[R, 0:W], in0=ang[R, 0:W], scalar1=float(1.0 / (2 * np.pi)), scalar2=MAGIC,
                                                            op0=ALU.mult, op1=ALU.add)), reads=[B_ang], writes=[B_t12])
                    sc.op("dve", (lambda e: e.tensor_scalar(out=t1[R, 0:W], in0=t1[R, 0:W], scalar1=-MAGIC, scalar2=None, op0=ALU.add)),
                          reads=[B_t12], writes=[B_t12])
                    yield
                    sc.op("dve", (lambda e: e.scalar_tensor_tensor(out=t2[R, 0:W], in0=t1[R, 0:W], scalar=-C1, in1=ang[R, 0:W], op0=ALU.mult, op1=ALU.add)),
                          reads=[B_t12, B_ang], writes=[B_t12])
                    sc.op("dve", (lambda e: e.scalar_tensor_tensor(out=t2[R, 0:W], in0=t1[R, 0:W], scalar=-C2, in1=t2[R, 0:W], op0=ALU.mult, op1=ALU.add)),
                          reads=[B_t12], writes=[B_t12])
                    sc.op("dve", (lambda e: e.tensor_scalar(out=t2[R, 0:W], in0=t2[R, 0:W], scalar1=-PI, scalar2=PI, op0=ALU.max, op1=ALU.min)),
                          reads=[B_t12], writes=[B_t12])
                    sc.op("act", (lambda e: e.activation(out=st[R, 0:W], in_=t2[R, 0:W], func=AF.Sin, scale=ccol("sgn", 0, 64, 96))),
                          reads=[B_t12, B_cp], writes=[B_cs])
                    sc.op("act", (lambda e: e.activation(out=ct[R, 0:W], in_=t2[R, 0:W], func=AF.Sin, scale=0.5)),
                          reads=[B_t12], writes=[B_cs])
                    yield
                    sc.op("dve", (lambda e: e.tensor_tensor(out=ct[R, 0:W], in0=ct[R, 0:W], in1=ct[R, 0:W], op=ALU.mult)), reads=[B_cs], writes=[B_cs])
                    sc.op("dve", (lambda e: e.tensor_scalar(out=ct[R, 0:W], in0=ct[R, 0:W], scalar1=-2.0, scalar2=1.0, op0=ALU.mult, op1=ALU.add)),
                          reads=[B_cs], writes=[B_cs])
                    if ti < n_own:
                        sc.op("act", (lambda e: e.activation(out=cosT[R, T0:T0 + W], in_=ct[R, 0:W], func=AF.Copy)), reads=[B_cs], writes=[B_tab])
                        sc.op("act", (lambda e: e.activation(out=sinT[R, T0:T0 + W], in_=st[R, 0:W], func=AF.Copy)), reads=[B_cs], writes=[B_tab])
                    yield

                def gen_P(ti):
                    T0, W = all_tiles[ti]
                    hb, B_hb = hbuf[ti % 2], B_hs[ti % 2]
                    ct, st, B_cs = cts[ti % 2], sts[ti % 2], B_css[ti % 2]
                    hT = lambda c: hb[:, c, 0:W]
                    groups = [("kv", [3, 4], "kvg", 256.0)]
                    if ti < n_own:
                        groups.append(("q", [0, 1, 2], "qg", 384.0))
                    for (gname, chunks, gn, dim) in groups:
                        for j, m in enumerate(chunks):
                            bi = next_bank()
                            mm_group(bi, W, [(winA[:, m, k, :], hT(k)) for k in range(8)], [B_hb, B_winA])
                            copy_op(evac_eng(), lat[:, j, 0:W], bank(bi)[:, 0:W], [PSB[bi]], [B_lat])
                        nchk = len(chunks)
                        sc.op("act", (lambda e, nchk=nchk: e.activation(out=sq2[:, 0:nchk, 0:W], in_=lat[:, 0:nchk, 0:W], func=AF.Square)),
                              reads=[B_lat], writes=[B_sq2])
                        yield
                        stat_rstd([sq2[:, j, 0:W] for j in range(nchk)], [B_sq2], W, dim, rstd2[:, 0:W], B_rstd2)
                        yield
                        if gname == "kv":
                            dst = lambda c: ckvT[:, c, T0:T0 + W]
                            B_dst = B_ckv
                        else:
                            dst = lambda c: qnT[:, c, T0:T0 + W]
                            B_dst = B_qn
                        rms_apply(nchk, (lambda c: lat[:, c, 0:W]), B_lat, gn, 0, rstd2[:, 0:W], B_rstd2, dst, B_dst, W)
                        yield
                    b1 = next_bank()
                    mm_group(b1, W, [(winA[:, 5, k, :], hT(k)) for k in range(8)], [B_hb, B_winA])
                    b2 = next_bank()
                    mm_group(b2, W, [(winA[:, 6, k, :], hT(k)) for k in range(8)], [B_hb, B_winA])
                    sc.op("dve", (lambda e: e.tensor_tensor(out=k1[R, 0:W], in0=bank(b1)[R, 0:W], in1=ct[R, 0:W], op=ALU.mult)),
                          reads=[PSB[b1], B_cs], writes=[B_k12])
                    sc.op("dve", (lambda e: e.tensor_tensor(out=k2[R, 0:W], in0=bank(b2)[R, 0:W], in1=st[R, 0:W], op=ALU.mult)),
                          reads=[PSB[b2], B_cs], writes=[B_k12])
                    sc.op("dve", (lambda e: e.tensor_tensor(out=Kt[R, T0:T0 + W], in0=k1[R, 0:W], in1=k2[R, 0:W], op=ALU.add)),
                          reads=[B_k12], writes=[B_Kpe])
                    yield

                def drain(g):
                    for _ in g:
                        pass

                def chain(*gens):
                    for g in gens:
                        yield from g

                ntile = len(all_tiles)
                drain(gen_N(0))
                drain(gen_R(0))
                for ti in range(ntile):
                    gp = gen_P(ti)
                    gn = chain(gen_N(ti + 1), gen_R(ti + 1)) if ti + 1 < ntile else None
                    while gp is not None or gn is not None:
                        if gn is not None:
                            try:
                                next(gn)
                            except StopIteration:
                                gn = None
                        if gp is not None:
                            try:
                                next(gp)
                            except StopIteration:
                                gp = None
                sc.flush()
            with ExitStack() as pb:
                wuq = sb(pb, "wuq", [128, 8, 2, 3, 96], BF16)
                B_wuq = sc.buf("wuq")
                wukv = sb(pb, "wukv", [128, 2, 1024], BF16)
                B_wukv = sc.buf("wukv")
                o, n = woff["w_uq"]
                sc.op("sp", lambda e, o=o, n=n: e.dma_start(out=wuq[:, :, :, :, :].rearrange("p h v k j -> p (h v k j)"), in_=wbf[:, o:o + n]),
                      reads=[B_wbf["w_uq"]], writes=[B_wuq], dma_key="wuq", extra_dma={"cv_w_uq": sc.dsems["cv_w_uq"][1]})
                o, n = woff["w_ukv"]
                sc.op("sp", lambda e, o=o, n=n: e.dma_start(out=wukv[:, :, :].rearrange("p k j -> p (k j)"), in_=wbf[:, o:o + n]),
                      reads=[B_wbf["w_ukv"]], writes=[B_wukv], dma_key="wukv", extra_dma={"cv_w_ukv": sc.dsems["cv_w_ukv"][1]})

                Vt = sb(pb, "Vt", [128, NKT, 128], BF16)
                B_V = sc.buf("V")
                Pb = [sb(pb, "Pb%d" % i, [128, 3, 512], BF16) for i in range(3)]
                B_P = [sc.buf("P%d" % i) for i in range(3)]
                Kt2 = sb(pb, "Kt2", [128, S], BF16)
                Ks = [Kt, Kt2]
                B_Kns = [B_Kn, sc.buf("Kn2")]
                B_Kpes = [B_Kpe, sc.buf("Kpe2")]
                Qs = [sb(pb, "Qh%d" % i, [128, OWNH], BF16) for i in range(2)]
                B_Qs = [sc.buf("Q%d" % i) for i in range(2)]
                sqt = sb(pb, "sqt", [128, 512], BF16)
                B_sqt = sc.buf("sqt")
                mxs = [sb(pb, "mx%d" % i, [128, 64], F32) for i in range(2)]
                B_mxs = [sc.buf("mx%d" % i) for i in range(2)]
                negms = [sb(pb, "negm%d" % i, [128, 2], F32) for i in range(2)]
                B_negms = [sc.buf("negm%d" % i) for i in range(2)]
                rec = sb(pb, "rec", [128, 512], F32)
                B_rec = sc.buf("rec")
                qtmp = sb(pb, "qtmp", [128, 2, 512], F32)
                B_qtmp = sc.buf("qtmp")
                sc.op("pool", lambda e: e.memset(Vt[:, :, 64:128], 1.0), writes=[B_V])
                sc.op("pool", lambda e: e.tensor_copy(out=Kt2[R, :], in_=Kt[R, :]), reads=[B_Kpe], writes=[B_Kpes[1]])
                kblocks = split_tiles(S, 512)
                PB = 7
                OB = 6

                def prologue(h):
                    Kc, B_Kn_c, B_Kpe_c = Ks[h % 2], B_Kns[h % 2], B_Kpes[h % 2]
                    Qc, B_Qc = Qs[h % 2], B_Qs[h % 2]
                    mx, B_mx = mxs[h % 2], B_mxs[h % 2]
                    negm, B_negm = negms[h % 2], B_negms[h % 2]
                    for (k0, kw) in kblocks:
                        mm_group(PB, kw, [(wukv[:, k, h * 128:h * 128 + 64], ckvT[:, k, k0:k0 + kw]) for k in range(2)], [B_wukv, B_ckv], M=64)
                        copy_op("dve", Kc[0:64, k0:k0 + kw], bank(PB)[0:64, 0:kw], [PSB[PB]], [B_Kn_c])
                        yield
                    for (T0, W) in own_tiles:
                        mm_group(PB, W, [(wuq[:, h, 0, k, :], qnT[:, k, T0:T0 + W]) for k in range(3)], [B_wuq, B_qn], M=96)
                        copy_op("dve", Qc[0:64, T0:T0 + W], bank(PB)[0:64, 0:W], [PSB[PB]], [B_Qc])
                        sc.op("dve", (lambda e, T0=T0, W=W: e.tensor_tensor(out=qtmp[R, 0, 0:W], in0=bank(PB)[R, 0:W], in1=cosT[R, T0:T0 + W], op=ALU.mult)),
                              reads=[PSB[PB], B_tab], writes=[B_qtmp])
                        yield
                        mm_group(PB, W, [(wuq[:, h, 1, k, :], qnT[:, k, T0:T0 + W]) for k in range(3)], [B_wuq, B_qn], M=96)
                        sc.op("dve", (lambda e, T0=T0, W=W: e.tensor_tensor(out=qtmp[R, 1, 0:W], in0=bank(PB)[R, 0:W], in1=sinT[R, T0:T0 + W], op=ALU.mult)),
                              reads=[PSB[PB], B_tab], writes=[B_qtmp])
                        sc.op("pool", (lambda e, T0=T0, W=W: e.tensor_tensor(out=Qc[R, T0:T0 + W], in0=qtmp[R, 0, 0:W], in1=qtmp[R, 1, 0:W], op=ALU.add)),
                              reads=[B_qtmp], writes=[B_Qc])
                        yield
                    nq = len(own_tiles)
                    nk = len(kblocks)
                    col = 0
                    for (src_t, B_src, tiles) in ((Qc, [B_Qc], own_tiles), (Kc, [B_Kn_c, B_Kpe_c], kblocks)):
                        for (T0, W) in tiles:
                            sc.op("pool", (lambda e, src_t=src_t, T0=T0, W=W: e.tensor_tensor(out=sqt[0:96, 0:W], in0=src_t[0:96, T0:T0 + W], in1=src_t[0:96, T0:T0 + W], op=ALU.mult)),
                                  reads=B_src, writes=[B_sqt])
                            mm_group(PB, W, [(ones[0:96, :], sqt[0:96, 0:W])], [B_sqt, B_ones])
                            sc.op("dve", (lambda e, W=W, col=col: e.tensor_reduce(out=mx[:, col:col + 1], in_=bank(PB)[:, 0:W], axis=AX.X, op=ALU.max)),
                                  reads=[PSB[PB]], writes=[B_mx])
                            col += 1
                            yield
                    sc.op("dve", lambda e: e.tensor_reduce(out=negm[:, 0:1], in_=mx[:, 0:nq], axis=AX.X, op=ALU.max), reads=[B_mx], writes=[B_negm])
                    sc.op("dve", lambda e: e.tensor_reduce(out=negm[:, 1:2], in_=mx[:, nq:nq + nk], axis=AX.X, op=ALU.max), reads=[B_mx], writes=[B_negm])
                    sc.op("dve", lambda e: e.tensor_tensor(out=negm[:, 0:1], in0=negm[:, 0:1], in1=negm[:, 1:2], op=ALU.mult), reads=[B_negm], writes=[B_negm])
                    sc.op("act", lambda e: e.activation(out=negm[:, 0:1], in_=negm[:, 0:1], func=AF.Ln), reads=[B_negm], writes=[B_negm])
                    sc.op("act", lambda e: e.activation(out=negm[:, 0:1], in_=negm[:, 0:1], func=AF.Exp, scale=0.5), reads=[B_negm], writes=[B_negm])
                    sc.op("dve", lambda e: e.tensor_scalar(out=negm[:, 0:1], in0=negm[:, 0:1], scalar1=-scale, scalar2=None, op0=ALU.mult),
                          reads=[B_negm], writes=[B_negm])
                    yield

                def vgen(h):
                    for g in range(NKT // 8):
                        bi = PB if g % 2 == 0 else OB
                        for j in range(8):
                            kt = g * 8 + j
                            mm_group(bi, 64, [(ckvT[:, k, kt * 128:(kt + 1) * 128], wukv[:, k, h * 128 + 64:h * 128 + 128]) for k in range(2)],
                                     [B_wukv, B_ckv], c0=j * 64)
                        srcv = bank(bi)[:, 0:512].rearrange("p (j d) -> p j d", d=64)
                        copy_op("dve" if g % 2 == 0 else "act", Vt[:, g * 8:g * 8 + 8, 0:64], srcv, [PSB[bi]], [B_V])

                def attention(h):
                    Kc, B_Kn_c, B_Kpe_c = Ks[h % 2], B_Kns[h % 2], B_Kpes[h % 2]
                    Qc, B_Qc = Qs[h % 2], B_Qs[h % 2]
                    negm, B_negm = negms[h % 2], B_negms[h % 2]
                    groups = []
                    kt0 = 0
                    while kt0 < NKT:
                        n = min(3, NKT - kt0)
                        groups.append((kt0, n))
                        kt0 += n
                    NG = len(groups)
                    for qi, (T0, W) in enumerate(own_tiles):
                        def s_group(g, T0=T0, W=W):
                            sl = g % 2
                            kt0, n = groups[g]
                            for j in range(n):
                                kt = kt0 + j
                                sc.op("pe", (lambda e, sl=sl, j=j, kt=kt: e.matmul(bank(3 * sl + j)[:, 0:W], Kc[0:96, kt * 128:(kt + 1) * 128],
                                                                                      Qc[0:96, T0:T0 + W], start=True, stop=True)),
                                      reads=[B_Kn_c, B_Kpe_c, B_Qc], writes=[PSB[3 * sl + j]])
                            srcp = psum[:, 3 * sl * 512:(3 * sl + n) * 512].rearrange("p (j w) -> p j w", w=512)[:, :, 0:W]
                            pi = g % 3
                            sc.op("act", (lambda e, pi=pi, srcp=srcp, n=n: e.activation(out=Pb[pi][:, 0:n, 0:W], in_=srcp, func=AF.Exp, bias=negm[:, 0:1], scale=scale)),
                                  reads=[PSB[3 * sl + j] for j in range(n)] + [B_negm], writes=[B_P[pi]])

                        def pv_group(g, T0=T0, W=W):
                            pi = g % 3
                            kt0, n = groups[g]
                            for j in range(n):
                                kt = kt0 + j
                                sc.op("pe", (lambda e, pi=pi, j=j, kt=kt: e.matmul(bank(OB)[:, 0:W], Vt[:, kt, :], Pb[pi][:, j, 0:W],
                                                                                      start=(kt == 0), stop=(kt == NKT - 1))),
                                      reads=[B_V, B_P[pi]], writes=[PSB[OB]])
                        s_group(0)
                        yield
                        if NG > 1:
                            s_group(1)
                            yield
                        for g in range(NG):
                            if g + 2 < NG:
                                s_group(g + 2)
                            pv_group(g)
                            yield
                        sc.op("dve", (lambda e, W=W: e.reciprocal(out=rec[64:128, 0:W], in_=bank(OB)[64:128, 0:W])), reads=[PSB[OB]], writes=[B_rec])
                        p0 = (h % 2) * 64
                        sc.op("dve", (lambda e, T0=T0, W=W, p0=p0: e.tensor_tensor(out=ybT[p0:p0 + 64, h // 2, T0:T0 + W], in0=bank(OB)[0:64, 0:W],
                                                                                  in1=rec[64:128, 0:W], op=ALU.mult)),
                              reads=[PSB[OB], B_rec], writes=[B_yb])

                for _ in prologue(0):
                    pass
                for h in range(NH):
                    vgen(h)
                    ga = attention(h)
                    gp = prologue(h + 1) if h + 1 < NH else None
                    cnt = 0
                    for _ in ga:
                        cnt += 1
                        if gp is not None and cnt % 3 == 0:
                            try:
                                next(gp)
                            except StopIteration:
                                gp = None
                    if gp is not None:
                        for _ in gp:
                            pass
                sc.flush()

        with ExitStack() as pc:
            XW = 16 + TW + 1
            xres = [sb(pc, "xres%d" % i, [128, 8, XW], F32) for i in range(2)]
            B_xres = [sc.buf("xres%d" % i) for i in range(2)]
            NWB = 3
            wbuf = [sb(pc, "wbuf%d" % i, [128, 4096], BF16) for i in range(NWB)]
            B_wbuf = [sc.buf("wbuf%d" % i) for i in range(NWB)]
            yaT = sb(pc, "yaT", [128, 4, TW], BF16)
            B_ya = sc.buf("ya")
            gbt = sb(pc, "gbt", [128, 4, 1 + TW], BF16)
            B_gb = sc.buf("gb")
            zt = sb(pc, "zt", [128, 4, 2 + TW], BF16)
            B_z = sc.buf("z")
            acc = [sb(pc, "acc%d" % i, [128, TW], F32) for i in range(2)]
            B_acc = [sc.buf("acc%d" % i) for i in range(2)]
            ubuf = [sb(pc, "ubuf%d" % i, [128, 8, 31 + TW], BF16) for i in range(2)]
            B_u = [sc.buf("u%d" % i) for i in range(2)]
            csq = sb(pc, "csq", [128, 8, TW], BF16)
            B_csq = sc.buf("csq")
            mean = sb(pc, "mean", [128, TW], F32)
            B_mean = sc.buf("mean")

            class Ctx:
                pass
            ctxs = []
            for nm in ("S", "T"):
                cx = Ctx()
                cx.sqh = sb(pc, "sqh" + nm, [128, 8, TW + 1], BF16)
                cx.B_sq = sc.buf("sq" + nm)
                cx.B_h = cx.B_sq
                cx.rstd = sb(pc, "rstd" + nm, [128, TW + 1], F32)
                cx.B_rstd = sc.buf("rstd" + nm)
                cx.mT = sb(pc, "mT" + nm, [128, 8, TW + 1], F32)
                cx.B_m = sc.buf("m" + nm)
                cx.hid = sb(pc, "hid" + nm, [128, 16, TW], BF16)
                cx.B_hid = sc.buf("hid" + nm)
                ctxs.append(cx)
            CS, CT = ctxs

            class WStream:
                def __init__(self):
                    self.n = 0

                def get(self, name, blk, nblk_elems):
                    i = self.n % NWB
                    self.n += 1
                    o = woff[name][0] + blk * nblk_elems
                    sc.op("sp", (lambda e, i=i, o=o, n=nblk_elems: e.dma_start(out=wbuf[i][:, 0:n], in_=wbf[:, o:o + n])),
                          writes=[B_wbuf[i]], dma_key="wb%d" % i, extra_dma={"cv_" + name: sc.dsems["cv_" + name][1]})
                    return wbuf[i], B_wbuf[i]
            ws = WStream()

            def dense(wname, nk, nm, rhs_fn, rhs_bufs, W, consume, blk0=0, per=None):
                if per is None:
                    per = 4096 // (nk * 128)
                for b0 in range(0, nm, per):
                    nb = min(per, nm - b0)
                    wt, B_wt = ws.get(wname, blk0 + b0 // per, per * nk * 128)
                    for j in range(nb):
                        m = b0 + j
                        bi = next_bank()
                        mm_group(bi, W, [(wt[:, (j * nk + k) * 128:(j * nk + k + 1) * 128], rhs_fn(k)) for k in range(nk)], rhs_bufs + [B_wt])
                        consume(m, bi)
                    yield (nb * nk * 0.2, 1.0)

            def rmsnorm_pre(cx, xsrc, B_x, W, gidx):
                sqh, rstd = cx.sqh, cx.rstd
                for hf in range(2):
                    sc.op("act", (lambda e, hf=hf: e.activation(out=sqh[:, 4 * hf:4 * hf + 4, 0:W], in_=xsrc(slice(4 * hf, 4 * hf + 4)), func=AF.Square)),
                          reads=[B_x, cx.B_h], writes=[cx.B_sq])
                yield (0.0, 7.0)
                stat_rstd([sqh[:, c, 0:W] for c in range(8)], [cx.B_sq], W, float(D), rstd[:, 0:W], cx.B_rstd)
                for c in range(8):
                    sc.op("dve", (lambda e, c=c: e.scalar_tensor_tensor(out=sqh[:, c, 0:W], in0=xsrc(c), scalar=ccol("gains", gidx * 8 + c), in1=rstd[:, 0:W],
                                                                         op0=ALU.mult, op1=ALU.mult)),
                          reads=[B_x, cx.B_rstd, B_cp, cx.B_sq], writes=[cx.B_h])
                yield (1.6, 13.0)

            def post_norm_residual(cx, xdst, B_x, W, gidx):
                sqh, rstd, mT = cx.sqh, cx.rstd, cx.mT
                for hf in range(2):
                    sc.op("act", (lambda e, hf=hf: e.activation(out=sqh[:, 4 * hf:4 * hf + 4, 0:W], in_=mT[:, 4 * hf:4 * hf + 4, 0:W], func=AF.Square)),
                          reads=[cx.B_m, cx.B_h], writes=[cx.B_sq])
                yield (0.0, 7.0)
                stat_rstd([sqh[:, c, 0:W] for c in range(8)], [cx.B_sq], W, float(D), rstd[:, 0:W], cx.B_rstd)
                for c in range(8):
                    sc.op("dve", (lambda e, c=c: e.scalar_tensor_tensor(out=mT[:, c, 0:W], in0=mT[:, c, 0:W], scalar=ccol("gains", gidx * 8 + c), in1=rstd[:, 0:W],
                                                                         op0=ALU.mult, op1=ALU.mult)),
                          reads=[cx.B_m, cx.B_rstd, B_cp], writes=[cx.B_m])
                    aeng = "pool" if c % 3 != 2 else "dve"
                    sc.op(aeng, (lambda e, c=c: e.tensor_tensor(out=xdst(c), in0=xdst(c), in1=mT[:, c, 0:W], op=ALU.add)),
                          reads=[cx.B_m, B_x], writes=[B_x])
                yield (1.6, 17.0)

            def mlp_half(cx, W, layer, half):
                sqh, mT, hid = cx.sqh, cx.mT, cx.hid

                def cons1(m, bi):
                    sc.op("act", (lambda e: e.activation(out=hid[:, m, 0:W], in_=bank(bi)[:, 0:W], func=AF.Relu)), reads=[PSB[bi]], writes=[cx.B_hid])
                    sc.op("pool", (lambda e: e.tensor_tensor(out=hid[:, m, 0:W], in0=hid[:, m, 0:W], in1=hid[:, m, 0:W], op=ALU.mult)),
                          reads=[cx.B_hid], writes=[cx.B_hid])
                yield from dense("m%dw1" % layer, 8, 16, (lambda k: sqh[:, k, 0:W]), [cx.B_h], W, cons1, blk0=half * 4)

                def cons2(m, bi):
                    if half == 0:
                        copy_op(evac_eng(), mT[:, m, 0:W], bank(bi)[:, 0:W], [PSB[bi]], [cx.B_m])
                    else:
                        sc.op("dve", (lambda e: e.tensor_tensor(out=mT[:, m, 0:W], in0=bank(bi)[:, 0:W], in1=mT[:, m, 0:W], op=ALU.add)),
                              reads=[PSB[bi], cx.B_m], writes=[cx.B_m])
                yield from dense("m%dw2" % layer, 16, 8, (lambda k: hid[:, k, 0:W]), [cx.B_hid], W, cons2, blk0=half * 4)

            for c in range(2):
                sc.op("pool", lambda e, c=c: e.memset(ubuf[c][:, :, 0:31], 0.0), writes=[B_u[c]])
            sc.op("pool", lambda e: e.memset(zt[:, :, 0:1], 0.0), writes=[B_z])
            sc.op("pool", lambda e: e.memset(xres[0][:, :, 0:16], 0.0), writes=[B_xres[0]])

            ntl = len(own_tiles)

            def pipe_L0(ti):
                cx = CS
                T0, W = own_tiles[ti]
                xr = xres[ti % 2]
                B_xr = B_xres[ti % 2]
                ub = ubuf[ti % 2]
                B_ub = B_u[ti % 2]
                W1 = W + 1
                sqh, mT = cx.sqh, cx.mT
                x0 = lambda c: xr[:, c, 16:16 + W]
                x01 = lambda c: xr[:, c, 16:16 + W1]
                for hf in range(2):
                    sc.op("sp", (lambda e, hf=hf: e.dma_start(out=xr[:, 4 * hf:4 * hf + 4, 16:16 + W1], in_=xT[:, 4 * hf:4 * hf + 4, T0:T0 + W1])),
                          writes=[B_xr], dma_key="xr%d" % (ti % 2))
                yield (0.0, 12.0)
                yield from rmsnorm_pre(cx, x01, B_xr, W1, 0)
                gcs = mT

                def cons_g(m, bi):
                    grp, c = m // 4, m % 4
                    if grp == 0:
                        copy_op(evac_eng(), gbt[:, c, 0:W1], bank(bi)[:, 0:W1], [PSB[bi]], [B_gb])
                    elif grp == 1:
                        copy_op("act", gcs[:, c, 0:W1], bank(bi)[:, 0:W1], [PSB[bi]], [cx.B_m])
                    else:
                        sc.op("dve", (lambda e: e.tensor_tensor(out=zt[:, c, 1:1 + W1], in0=bank(bi)[:, 0:W1], in1=gcs[:, c, 0:W1], op=ALU.mult)),
                              reads=[PSB[bi], cx.B_m], writes=[B_z])
                yield from dense("w_inG", 8, 12, (lambda k: sqh[:, k, 0:W1]), [cx.B_h], W1, cons_g)
                for c in range(4):
                    a = acc[c % 2]
                    B_a = B_acc[c % 2]
                    sc.op("dve", (lambda e, c=c, a=a: e.tensor_scalar(out=a[:, 0:W], in0=zt[:, c, 0:W], scalar1=ccol("sck", c * 3 + 0), scalar2=None, op0=ALU.mult)),
                          reads=[B_z, B_cp], writes=[B_a])
                    for k in (1, 2):
                        sc.op("dve", (lambda e, c=c, a=a, k=k: e.scalar_tensor_tensor(out=a[:, 0:W], in0=zt[:, c, k:k + W], scalar=ccol("sck", c * 3 + k),
                                                                                      in1=a[:, 0:W], op0=ALU.mult, op1=ALU.add)),
                              reads=[B_z, B_cp, B_a], writes=[B_a])
                    sc.op("pool", (lambda e, c=c, a=a: e.tensor_tensor(out=yaT[:, c, 0:W], in0=a[:, 0:W], in1=gbt[:, c, 0:W], op=ALU.mult)),
                          reads=[B_a, B_gb], writes=[B_ya])
                sc.op("pool", (lambda e: e.tensor_copy(out=zt[:, :, 0:1], in_=zt[:, :, W:W + 1])), reads=[B_z], writes=[B_z])
                yield (0.0, 11.0)

                def rhs_mix(k):
                    return yaT[:, k, 0:W] if k < 4 else ybT[:, k - 4, T0:T0 + W]

                def cons_m(m, bi):
                    copy_op(evac_eng(), mT[:, m, 0:W], bank(bi)[:, 0:W], [PSB[bi]], [cx.B_m])
                yield from dense("w_out", 8, 8, rhs_mix, [B_ya, B_yb], W, cons_m)
                yield from post_norm_residual(cx, x0, B_xr, W, 1)
                yield from rmsnorm_pre(cx, x0, B_xr, W, 2)
                yield from mlp_half(cx, W, 0, 0)
                yield from mlp_half(cx, W, 0, 1)
                yield from post_norm_residual(cx, x0, B_xr, W, 3)
                yield from rmsnorm_pre(cx, x0, B_xr, W, 4)

                abank = [None]

                def cons_p(m, bi):
                    c, isg = m // 2, m % 2
                    if isg == 0:
                        abank[0] = bi
                    else:
                        ba = abank[0]
                        g = acc[c % 2]
                        B_g = B_acc[c % 2]
                        sc.op("act", (lambda e: e.activation(out=g[:, 0:W], in_=bank(bi)[:, 0:W], func=AF.Sigmoid, bias=ccol("bpw1", 8 + c), scale=1.0)),
                              reads=[PSB[bi], B_cp], writes=[B_g])
                        sc.op("dve", (lambda e: e.scalar_tensor_tensor(out=ub[:, c, 31:31 + W], in0=bank(ba)[:, 0:W], scalar=ccol("bpw1", c), in1=g[:, 0:W],
                                                                       op0=ALU.add, op1=ALU.mult)),
                              reads=[PSB[ba], B_g, B_cp], writes=[B_ub])
                yield from dense("pw1", 8, 16, (lambda k: sqh[:, k, 0:W]), [cx.B_h], W, cons_p)
                yield (0.0, 3.0)

            def pipe_L1(ti):
                cx = CT
                T0, W = own_tiles[ti]
                xr = xres[ti % 2]
                B_xr = B_xres[ti % 2]
                ub = ubuf[ti % 2]
                B_ub = B_u[ti % 2]
                sqh, mT, rstd = cx.sqh, cx.mT, cx.rstd
                x1 = lambda c: xr[:, c, 0:W]
                cacc = mT
                for c in range(8):
                    wt, B_wt = ws.get("dwd", c, 31 * 128)
                    bi = next_bank()
                    mm_group(bi, W, [(wt[:, k * 128:(k + 1) * 128], ub[:, c, k:k + W]) for k in range(31)], [B_ub, B_wt])
                    sc.op("dve", (lambda e, c=c, bi=bi: e.tensor_scalar(out=cacc[:, c, 0:W], in0=bank(bi)[:, 0:W], scalar1=ccol("bdw", c), scalar2=None, op0=ALU.add)),
                          reads=[PSB[bi], B_cp], writes=[cx.B_m])
                    sc.op("act", (lambda e, c=c: e.activation(out=sqh[:, c, 0:W], in_=cacc[:, c, 0:W], func=AF.Copy)), reads=[cx.B_m, cx.B_h], writes=[cx.B_sq])
                    sc.op("act", (lambda e, c=c: e.activation(out=csq[:, c, 0:W], in_=cacc[:, c, 0:W], func=AF.Square)), reads=[cx.B_m], writes=[B_csq])
                    yield (6.2, 1.0)
                yield (0.0, 3.0)
                b_mean = next_bank()
                mm_group(b_mean, W, [(ones[:, :], sqh[:, c, 0:W]) for c in range(8)], [cx.B_sq, B_ones])
                b_ex2 = next_bank()
                mm_group(b_ex2, W, [(ones[:, :], csq[:, c, 0:W]) for c in range(8)], [B_csq, B_ones])
                sc.op("dve", (lambda e: e.tensor_scalar(out=mean[:, 0:W], in0=bank(b_mean)[:, 0:W], scalar1=1.0 / D, scalar2=None, op0=ALU.mult)),
                      reads=[PSB[b_mean]], writes=[B_mean])
                sc.op("dve", (lambda e: e.tensor_tensor(out=rstd[:, 0:W], in0=mean[:, 0:W], in1=mean[:, 0:W], op=ALU.mult)), reads=[B_mean], writes=[cx.B_rstd])
                sc.op("dve", (lambda e: e.scalar_tensor_tensor(out=rstd[:, 0:W], in0=bank(b_ex2)[:, 0:W], scalar=1.0 / D, in1=rstd[:, 0:W],
                                                               op0=ALU.mult, op1=ALU.subtract)), reads=[PSB[b_ex2], cx.B_rstd], writes=[cx.B_rstd])
                sc.op("act", (lambda e: e.activation(out=rstd[:, 0:W], in_=rstd[:, 0:W], func=AF.Ln, bias=eps_col[:, 0:1], scale=1.0)),
                      reads=[cx.B_rstd, B_ones], writes=[cx.B_rstd])
                sc.op("act", (lambda e: e.activation(out=rstd[:, 0:W], in_=rstd[:, 0:W], func=AF.Exp, scale=-0.5)), reads=[cx.B_rstd], writes=[cx.B_rstd])
                for c in range(8):
                    sc.op("pool", (lambda e, c=c: e.tensor_tensor(out=cacc[:, c, 0:W], in0=cacc[:, c, 0:W], in1=mean[:, 0:W], op=ALU.subtract)),
                          reads=[cx.B_m, B_mean], writes=[cx.B_m])
                    sc.op("dve", (lambda e, c=c: e.tensor_tensor(out=cacc[:, c, 0:W], in0=cacc[:, c, 0:W], in1=rstd[:, 0:W], op=ALU.mult)),
                          reads=[cx.B_m, cx.B_rstd], writes=[cx.B_m])
                    sc.op("act", (lambda e, c=c: e.activation(out=sqh[:, c, 0:W], in_=cacc[:, c, 0:W], func=AF.Silu, bias=ccol("lnb", c), scale=ccol("lng", c))),
                          reads=[cx.B_m, B_cp, cx.B_sq], writes=[cx.B_h])
                yield (3.2, 18.0)

                def cons_p2(m, bi):
                    sc.op("dve", (lambda e: e.tensor_scalar(out=mT[:, m, 0:W], in0=bank(bi)[:, 0:W], scalar1=ccol("bpw2", m), scalar2=None, op0=ALU.add)),
                          reads=[PSB[bi], B_cp], writes=[cx.B_m])
                yield from dense("pw2", 8, 8, (lambda k: sqh[:, k, 0:W]), [cx.B_h], W, cons_p2)
                yield from post_norm_residual(cx, x1, B_xr, W, 5)
                yield from rmsnorm_pre(cx, x1, B_xr, W, 6)
                yield from mlp_half(cx, W, 1, 0)
                yield from mlp_half(cx, W, 1, 1)
                yield from post_norm_residual(cx, x1, B_xr, W, 7)
                c_lo = 16 if ti == 0 else 0
                tok0 = T0 - 16 + c_lo
                ncol = W - c_lo
                for hf in range(2):
                    sc.op("sp", (lambda e, hf=hf: e.dma_start(out=outT[:, 4 * hf:4 * hf + 4, tok0:tok0 + ncol], in_=xr[:, 4 * hf:4 * hf + 4, c_lo:c_lo + ncol])),
                          reads=[B_xr], dma_key="out")
                yield (0.0, 6.0)

            INF = 1e30
            gS = gT = None
            nS = nT = 0
            doneS = doneT = 0
            readyS = readyT = 0.0
            s_done_t = {}
            t_done_t = {}
            carried = 0
            t_pe = 0.0
            last = "T"
            while doneT < ntl:
                if gS is None and nS < ntl and (nS < 2 or doneT >= nS - 1):
                    gS = pipe_L0(nS)
                    if nS >= 2:
                        readyS = max(readyS, t_done_t[nS - 2])
                    nS += 1
                if gT is None and nT < ntl and doneS > nT and carried >= nT:
                    gT = pipe_L1(nT)
                    readyT = max(readyT, s_done_t[nT])
                    nT += 1
                cS = max(readyS, t_pe) if gS is not None else INF
                cT = max(readyT, t_pe) if gT is not None else INF
                assert cS < INF or cT < INF
                if cS < cT or (cS == cT and last == "T"):
                    pick = "S"
                else:
                    pick = "T"
                last = pick
                if pick == "S":
                    try:
                        pe_us, lat = next(gS)
                        t_pe = max(t_pe, readyS) + pe_us
                        readyS = t_pe + lat
                    except StopIteration:
                        gS = None
                        s_done_t[doneS] = readyS
                        doneS += 1
                else:
                    try:
                        pe_us, lat = next(gT)
                        t_pe = max(t_pe, readyT) + pe_us
                        readyT = t_pe + lat
                    except StopIteration:
                        gT = None
                        t_done_t[doneT] = readyT
                        doneT += 1
                while carried < ntl - 1 and doneS > carried and doneT >= carried:
                    it = carried
                    T0, W = own_tiles[it]
                    xr, nx = xres[it % 2], xres[(it + 1) % 2]
                    ub, nu = ubuf[it % 2], ubuf[(it + 1) % 2]
                    sc.op("pool", (lambda e, xr=xr, nx=nx, W=W: e.tensor_copy(out=nx[:, :, 0:16], in_=xr[:, :, W:W + 16])),
                          reads=[B_xres[it % 2]], writes=[B_xres[(it + 1) % 2]])
                    sc.op("pool", (lambda e, ub=ub, nu=nu, W=W: e.tensor_copy(out=nu[:, :, 0:31], in_=ub[:, :, W:W + 31])),
                          reads=[B_u[it % 2]], writes=[B_u[(it + 1) % 2]])
                    carried += 1
            sc.flush(final=True)
    return nc


def prep_core(inp, core, S):
    b, half = core // 2, core % 2
    idx = np.arange(S) if half == 0 else np.arange(S - 1, -1, -1)
    x = np.asarray(inp["x"][b], dtype=np.float32)[idx]
    xT = np.ascontiguousarray(x.T.reshape(8, 128, S).transpose(1, 0, 2))
    pos = np.ascontiguousarray(np.asarray(inp["positions"][b])[idx].astype(np.int32).reshape(1, S))
    return xT, pos, half


def prep_shared(inp, half):
    woff, WTOT = weight_layout()
    coff, CTOT = const_layout()
    f = lambda k: np.asarray(inp[k], dtype=np.float32)
    wp = np.zeros((128, WTOT), np.float32)

    def put(name, arr):
        o, n = woff[name]
        assert arr.shape == (128, n), (name, arr.shape, n)
        wp[:, o:o + n] = arr
    w_in = f("even_w_in")[0]
    kr = np.zeros((D, 128), np.float32)
    kr[:, 64:96] = w_in[:, 2176:2208]
    krp = np.zeros((D, 128), np.float32)
    krp[:, 64:80] = w_in[:, 2192:2208]
    krp[:, 80:96] = w_in[:, 2176:2192]
    winA = np.concatenate([w_in[:, 1536:2176], kr, krp], axis=1)
    put("w_inA", pack_lhsT(winA, 8, 7))
    put("w_inG", pack_lhsT(w_in[:, 0:1536], 8, 12))
    w_uq = f("even_w_uq")[0].reshape(384, 8, 96)
    wq = np.zeros((384, 8, 2, 96), np.float32)
    wq[:, :, 0, :] = w_uq
    wq[:, :, 1, 64:80] = w_uq[:, :, 80:96]
    wq[:, :, 1, 80:96] = w_uq[:, :, 64:80]
    wq = wq.reshape(3, 128, 8, 2, 96).transpose(1, 2, 3, 0, 4)
    put("w_uq", np.ascontiguousarray(wq).reshape(128, -1))
    w_ukv = f("even_w_ukv")[0]
    put("w_ukv", np.ascontiguousarray(w_ukv.reshape(2, 128, 1024).transpose(1, 0, 2)).reshape(128, -1))
    put("w_out", pack_lhsT(f("even_w_out")[0], 8, 8))
    put("m0w1", pack_lhsT(f("mlp_w1")[0], 8, 32))
    put("m0w2", pack_w2(f("mlp_w2")[0]))
    pw1 = f("odd_w_pw1")[0].reshape(D, 2, 8, 128).transpose(0, 2, 1, 3).reshape(D, 2048)
    put("pw1", pack_lhsT(pw1, 8, 16))
    put("pw2", pack_lhsT(f("odd_w_pw2")[0], 8, 8))
    put("m1w1", pack_lhsT(f("mlp_w1")[1], 8, 32))
    put("m1w2", pack_w2(f("mlp_w2")[1]))

    cpk = np.zeros((128, CTOT), np.float32)

    def putc(name, arr):
        o, n = coff[name]
        assert arr.shape == (128, n), (name, arr.shape, n)
        cpk[:, o:o + n] = arr
    putc("gains", colpack(f("sandwich_gains").reshape(-1)))
    putc("qg", colpack(f("even_q_norm")[0]))
    putc("kvg", colpack(f("even_kv_norm")[0]))
    sck = f("even_sc_kernel")[0]
    wdw = f("odd_w_dw")[0]
    if half == 1:
        sck = sck[::-1]
        wdw = wdw[::-1]
    putc("sck", np.ascontiguousarray(sck.reshape(3, 4, 128).transpose(2, 1, 0)).reshape(128, 12))
    dwd = np.zeros((128, 8, 31, 128), np.float32)
    pidx = np.arange(128)
    dwd[pidx, :, :, pidx] = wdw.reshape(31, 8, 128).transpose(2, 1, 0)
    put("dwd", dwd.reshape(128, -1))
    putc("wdw", np.ascontiguousarray(wdw.reshape(31, 8, 128).transpose(2, 1, 0)).reshape(128, 248))
    putc("bpw1", colpack(f("odd_b_pw1")[0]))
    putc("bdw", colpack(f("odd_b_dw")[0]))
    putc("lng", colpack(f("odd_ln_g")[0]))
    putc("lnb", colpack(f("odd_ln_b")[0]))
    putc("bpw2", colpack(f("odd_b_pw2")[0]))
    half_d = 16
    inv_freq = (1.0 / (np.float32(10000.0) ** (np.arange(half_d, dtype=np.float32) / np.float32(half_d)))).astype(np.float32)
    invf = np.zeros((128, 1), np.float32)
    invf[64:80, 0] = inv_freq
    invf[80:96, 0] = inv_freq
    putc("invf", invf)
    sgn = np.ones((128, 1), np.float32)
    sgn[64:80, 0] = -1.0
    putc("sgn", sgn)
    return wp, cpk


_NC_CACHE = {}


def kernel(**inputs):
    x = np.asarray(inputs["x"])
    B, S, _ = x.shape
    ncores = 2 * B
    if S not in _NC_CACHE:
        _NC_CACHE[S] = build(S)
    nc = _NC_CACHE[S]
    shared = [prep_shared(inputs, h) for h in range(2)]
    in_maps = []
    for c in range(ncores):
        xT, pos, half = prep_core(inputs, c, S)
        wp, cpk = shared[half]
        in_maps.append({"xT": xT, "pos": pos, "wpack": wp, "cpack": cpk})
    res = run_bass_kernel_spmd(nc, in_maps, core_ids=list(range(ncores)))
    OWN = S // 2
    out = np.empty((B, S, D), np.float32)
    for c in range(ncores):
        b, half = c // 2, c % 2
        oT = np.asarray(res.results[c]["outT"])
        o = oT.transpose(2, 1, 0).reshape(OWN, D)
        if half == 0:
            out[b, :OWN] = o
        else:
            out[b, OWN:] = o[::-1]
    return out
```
